# Optimizing a Trainium2 kernel written in Bass

```python
import math
import jax
import jax.numpy as jnp
from jax import lax
import numpy as np

D_MODEL = 1024
BATCH = 4
SEQ = 4096
DEPTH = 2
DEC_BATCH = 128
DEC_SEQ = 1
PAST_LEN = 8192
PAGE_SIZE = 128

GDN_HEADS = 4
GDN_DK = 128
GDN_DV = 128
GDN_CHUNK = 64
CONV_W = 4
LRU_WIDTH = 512
LRU_BLOCKS = 8
LRU_BW = LRU_WIDTH // LRU_BLOCKS
LRU_C = 8.0
SWA_HEADS = 16
SWA_KV_HEADS = 4
SWA_HEAD_DIM = 64
SWA_GROUP = SWA_HEADS // SWA_KV_HEADS
WINDOW = 128
REL_BUCKETS = 32
REL_MAX_DIST = 128
PEER_HEADS = 8
PEER_N_KEYS = 128
PEER_N_EXPERTS = PEER_N_KEYS ** 2
PEER_D_QUERY = 256
PEER_D_SUB = PEER_D_QUERY // 2
PEER_TOPK = 16
PEER_BLOCK = 128
DN_ALPHA = (2 * DEPTH) ** 0.25
DN_BETA = (8 * DEPTH) ** -0.25
LN_EPS = 1e-5
N_AB = (DEPTH + 1) // 2
N_C = DEPTH // 2
GDN_QK_W = GDN_HEADS * GDN_DK
GDN_V_W = GDN_HEADS * GDN_DV
GDN_CONV_CH = 2 * GDN_QK_W + GDN_V_W
AB_IN_COLS = GDN_CONV_CH + GDN_V_W + 2 * GDN_HEADS + 2 * LRU_WIDTH
AB_MIX_W = GDN_V_W + LRU_WIDTH
SWA_Q_W = SWA_HEADS * SWA_HEAD_DIM
SWA_KV_W = SWA_KV_HEADS * SWA_HEAD_DIM
C_IN_COLS = SWA_Q_W + 2 * SWA_KV_W

kernel_name = 'hybrid_gdn_rglru_swa_peer_step'

F32 = jnp.float32


def layer_norm(x, g, b):
    xf = x.astype(F32)
    mu = xf.mean(-1, keepdims=True)
    var = jnp.square(xf - mu).mean(-1, keepdims=True)
    return ((xf - mu) * lax.rsqrt(var + LN_EPS) * g + b).astype(x.dtype)


def causal_conv(x, w, buf, bias=None):
    xp = jnp.concatenate([buf.astype(x.dtype), x], axis=1)
    t_len = x.shape[1]
    y = sum(xp[:, i:i + t_len] * w[i] for i in range(CONV_W))
    if bias is not None:
        y = y + bias
    return y, xp[:, -(CONV_W - 1):]


def l2_normalize(x):
    return x * lax.rsqrt(jnp.sum(x * x, axis=-1, keepdims=True) + 1e-6)


def gated_delta_rule(q, k, v, g, beta, s0):
    bsz, t_len, n_h, _ = q.shape
    n_chunks = -(-t_len // GDN_CHUNK)
    pad = n_chunks * GDN_CHUNK - t_len

    def chunks(t):
        t = jnp.pad(t, [(0, 0), (0, pad)] + [(0, 0)] * (t.ndim - 2))
        t = t.reshape((bsz, n_chunks, GDN_CHUNK) + t.shape[2:])
        return jnp.moveaxis(t, 3, 1)

    q, k, v, g, beta = (chunks(t) for t in (q, k, v, g, beta))
    idx = jnp.arange(GDN_CHUNK)
    incl = idx[:, None] >= idx[None, :]
    strict = idx[:, None] > idx[None, :]
    gc = jnp.cumsum(g, axis=-1)
    decay = jnp.exp(jnp.where(incl, gc[..., :, None] - gc[..., None, :], -jnp.inf))
    kb = k * beta[..., None]
    lower = jnp.where(strict, jnp.einsum('bhncd,bhnsd->bhncs', kb, k) * decay, 0.0)
    rhs = jnp.concatenate([v * beta[..., None], kb * jnp.exp(gc)[..., None]], axis=-1)
    sol = lax.linalg.triangular_solve(lower, rhs, left_side=True, lower=True, unit_diagonal=True)
    u, w = sol[..., :GDN_DV], sol[..., GDN_DV:]
    qk = jnp.einsum('bhncd,bhnsd->bhncs', q, k) * decay

    def step(s, inp):
        q_c, k_c, u_c, w_c, gc_c, qk_c = inp
        v_new = u_c - jnp.einsum('bhcd,bhde->bhce', w_c, s)
        o = (jnp.einsum('bhcd,bhde->bhce', q_c * jnp.exp(gc_c)[..., None], s)
             + jnp.einsum('bhcs,bhse->bhce', qk_c, v_new))
        g_last = gc_c[..., -1:]
        s = (s * jnp.exp(g_last)[..., None]
             + jnp.einsum('bhcd,bhce->bhde', k_c * jnp.exp(g_last - gc_c)[..., None], v_new))
        return s, o

    xs = tuple(jnp.moveaxis(t, 2, 0) for t in (q, k, u, w, gc, qk))
    s_last, o = lax.scan(step, s0, xs)
    o = jnp.moveaxis(o, 0, 2).reshape(bsz, n_h, n_chunks * GDN_CHUNK, GDN_DV)[:, :, :t_len]
    return jnp.moveaxis(o, 1, 2), s_last


def _lru_combine(e1, e2):
    return e1[0] * e2[0], e2[0] * e1[1] + e2[1]


def rg_lru(x, h0, w_r, b_r, w_i, b_i, lam):
    bsz, t_len, _ = x.shape
    xf = x.astype(F32)
    xb = xf.reshape(bsz, t_len, LRU_BLOCKS, LRU_BW)
    r = jax.nn.sigmoid(jnp.einsum('btnc,ncd->btnd', xb, w_r).reshape(bsz, t_len, LRU_WIDTH) + b_r)
    i = jax.nn.sigmoid(jnp.einsum('btnc,ncd->btnd', xb, w_i).reshape(bsz, t_len, LRU_WIDTH) + b_i)
    log_a = -LRU_C * r * jax.nn.softplus(-lam.astype(F32))
    a = jnp.exp(log_a)
    b = jnp.sqrt(-jnp.expm1(2.0 * log_a)) * (i * xf)
    b = b.at[:, 0].add(a[:, 0] * h0.astype(F32))
    _, h = lax.associative_scan(_lru_combine, (a, b), axis=1)
    return h, h[:, -1]


def ab_mixer(x, s0, gdn_buf, h0, lru_buf, w_in, gdn_conv_w, gdn_a_log, gdn_dt_bias, gdn_norm_w,
             lru_conv_w, lru_conv_b, lru_w_r, lru_b_r, lru_w_i, lru_b_i, lru_lam, w_out):
    bsz, t_len, _ = x.shape
    proj = x @ w_in
    splits = np.cumsum([GDN_CONV_CH, GDN_V_W, GDN_HEADS, GDN_HEADS, LRU_WIDTH]).tolist()
    qkv, z, a_raw, b_raw, xr, gate = jnp.split(proj, splits, axis=-1)
    qkv, gdn_buf_new = causal_conv(qkv, gdn_conv_w, gdn_buf)
    qkv = jax.nn.silu(qkv).astype(F32)
    q, k, v = jnp.split(qkv, [GDN_QK_W, 2 * GDN_QK_W], axis=-1)
    q = l2_normalize(q.reshape(bsz, t_len, GDN_HEADS, GDN_DK)) * (GDN_DK ** -0.5)
    k = l2_normalize(k.reshape(bsz, t_len, GDN_HEADS, GDN_DK))
    v = v.reshape(bsz, t_len, GDN_HEADS, GDN_DV)
    beta = jax.nn.sigmoid(b_raw.astype(F32))
    g = -jnp.exp(gdn_a_log.astype(F32)) * jax.nn.softplus(a_raw.astype(F32) + gdn_dt_bias)
    o, s_new = gated_delta_rule(q, k, v, g, beta, s0.astype(F32))
    o = o * lax.rsqrt(jnp.mean(o * o, axis=-1, keepdims=True) + 1e-6) * gdn_norm_w
    o = o * jax.nn.silu(z.astype(F32).reshape(bsz, t_len, GDN_HEADS, GDN_DV))
    o_a = o.reshape(bsz, t_len, GDN_V_W).astype(x.dtype)
    xr, lru_buf_new = causal_conv(xr, lru_conv_w, lru_buf, lru_conv_b)
    h, h_last = rg_lru(xr, h0, lru_w_r, lru_b_r, lru_w_i, lru_b_i, lru_lam)
    o_b = (jax.nn.gelu(gate.astype(F32)) * h).astype(x.dtype)
    y = jnp.concatenate([o_a, o_b], axis=-1) @ w_out
    return y, s_new, gdn_buf_new, h_last, lru_buf_new


def t5_bucket(rel):
    n = jnp.maximum(rel, 0)
    exact = REL_BUCKETS // 2
    nf = jnp.maximum(n, 1).astype(F32)
    large = exact + (jnp.log(nf / exact) / math.log(REL_MAX_DIST / exact)
                     * (REL_BUCKETS - exact)).astype(jnp.int32)
    return jnp.where(n < exact, n, jnp.minimum(large, REL_BUCKETS - 1))


def swa_project(x, w_in, b_in):
    bsz, t_len, _ = x.shape
    proj = x @ w_in + b_in
    q, k, v = jnp.split(proj, [SWA_Q_W, SWA_Q_W + SWA_KV_W], axis=-1)
    return (q.reshape(bsz, t_len, SWA_KV_HEADS, SWA_GROUP, SWA_HEAD_DIM),
            k.reshape(bsz, t_len, SWA_KV_HEADS, SWA_HEAD_DIM),
            v.reshape(bsz, t_len, SWA_KV_HEADS, SWA_HEAD_DIM))


def banded_attention(q, k, v, qpos, kpos, rel_bias, sinks):
    n_blk, n_q = qpos.shape
    n_k = kpos.shape[1]
    rel = qpos[:, :, None] - kpos[:, None, :]
    mask = (rel >= 0) & (rel < WINDOW) & (kpos[:, None, :] >= 0)
    bias = rel_bias.astype(F32)[t5_bucket(rel)]
    bias = bias.reshape(n_blk, n_q, n_k, SWA_KV_HEADS, SWA_GROUP).transpose(0, 3, 4, 1, 2)
    logits = (jnp.einsum('bnqhgd,bnkhd->bnhgqk', q.astype(F32), k.astype(F32)) * (SWA_HEAD_DIM ** -0.5)
              + bias[None])
    logits = jnp.where(mask[None, :, None, None], logits, -jnp.inf)
    sink = sinks.astype(F32).reshape(SWA_KV_HEADS, SWA_GROUP)[None, None, :, :, None, None]
    m = jnp.maximum(logits.max(-1, keepdims=True), sink)
    p = jnp.exp(logits - m)
    probs = p / (p.sum(-1, keepdims=True) + jnp.exp(sink - m))
    return jnp.einsum('bnhgqk,bnkhd->bnqhgd', probs, v.astype(F32))


def swa_prompt(x, w_in, b_in, sinks, w_out, b_out, rel_bias):
    bsz, t_len, _ = x.shape
    q, k, v = swa_project(x, w_in, b_in)
    n_blk = t_len // WINDOW

    def key_blocks(t):
        tp = jnp.concatenate([jnp.zeros_like(t[:, :WINDOW]), t], axis=1)
        prev = tp[:, :t_len].reshape((bsz, n_blk, WINDOW) + t.shape[2:])
        cur = tp[:, WINDOW:].reshape((bsz, n_blk, WINDOW) + t.shape[2:])
        return jnp.concatenate([prev, cur], axis=2)

    base = jnp.arange(n_blk)[:, None] * WINDOW
    qpos = base + jnp.arange(WINDOW)[None, :]
    kpos = base - WINDOW + jnp.arange(2 * WINDOW)[None, :]
    qb = q.reshape(bsz, n_blk, WINDOW, SWA_KV_HEADS, SWA_GROUP, SWA_HEAD_DIM)
    o = banded_attention(qb, key_blocks(k), key_blocks(v), qpos, kpos, rel_bias, sinks)
    y = o.reshape(bsz, t_len, SWA_Q_W).astype(x.dtype) @ w_out + b_out
    return y, k[:, -WINDOW:], v[:, -WINDOW:]


def swa_sample(x, k_buf, v_buf, w_in, b_in, sinks, w_out, b_out, rel_bias):
    bsz, t_len, _ = x.shape
    q, k, v = swa_project(x, w_in, b_in)
    wb = k_buf.shape[1]
    kc = jnp.concatenate([k_buf.astype(k.dtype), k], axis=1)
    vc = jnp.concatenate([v_buf.astype(v.dtype), v], axis=1)
    qpos = (PAST_LEN + jnp.arange(t_len))[None, :]
    kpos = (PAST_LEN - wb + jnp.arange(wb + t_len))[None, :]
    o = banded_attention(q[:, None], kc[:, None], vc[:, None], qpos, kpos, rel_bias, sinks)[:, 0]
    y = o.reshape(bsz, t_len, SWA_Q_W).astype(x.dtype) @ w_out + b_out
    return y, kc[:, -wb:], vc[:, -wb:]


def peer(x, w_q, sub_keys, u_tab, v_tab):
    shp = x.shape
    t = x.reshape(-1, D_MODEL)
    n_tok = t.shape[0]
    n_blocks = -(-n_tok // PEER_BLOCK)
    t = jnp.pad(t, ((0, n_blocks * PEER_BLOCK - n_tok), (0, 0)))

    def block(tb):
        q = (tb @ w_q).reshape(PEER_BLOCK, PEER_HEADS, 2, PEER_D_SUB)
        s = jnp.einsum('thcd,hcnd->thcn', q, sub_keys).astype(F32)
        s_top, i_top = lax.top_k(s, PEER_TOPK)
        cand = (s_top[:, :, 0, :, None] + s_top[:, :, 1, None, :]).reshape(PEER_BLOCK, PEER_HEADS, -1)
        cand_idx = (i_top[:, :, 0, :, None] * PEER_N_KEYS + i_top[:, :, 1, None, :]).reshape(
            PEER_BLOCK, PEER_HEADS, -1)
        best, pos = lax.top_k(cand, PEER_TOPK)
        expert = jnp.take_along_axis(cand_idx, pos, axis=-1)
        gate = jax.nn.softmax(best, axis=-1)
        act = jax.nn.gelu(jnp.einsum('td,thkd->thk', tb, u_tab[expert]).astype(F32))
        return jnp.einsum('thk,thkd->td', (gate * act).astype(tb.dtype), v_tab[expert])

    y = lax.map(block, t.reshape(n_blocks, PEER_BLOCK, D_MODEL))
    return y.reshape(-1, D_MODEL)[:n_tok].reshape(shp)


def setup_inputs(seed: int = 0) -> dict:
    key = jax.random.key(seed)
    keys = iter(jax.random.split(key, 48))

    def nrm(shape, scale):
        return scale * jax.random.normal(next(keys), shape, jnp.float32)

    win_buf = min(WINDOW, PAST_LEN)
    a_init = jax.random.uniform(next(keys), (N_AB, LRU_WIDTH), jnp.float32, 0.9, 0.999)
    sig_l = a_init ** (1.0 / LRU_C)
    lru_lam = jnp.log(sig_l) - jnp.log1p(-sig_l)
    gdn_a_log = jnp.log(jax.random.uniform(next(keys), (N_AB, GDN_HEADS), jnp.float32, 1.0, 16.0))
    dt = jnp.exp(jax.random.uniform(next(keys), (N_AB, GDN_HEADS), jnp.float32,
                                    math.log(1e-3), math.log(1e-1)))
    gdn_dt_bias = dt + jnp.log(-jnp.expm1(-dt))
    return {
        'x_prompt': nrm((BATCH, SEQ, D_MODEL), 1.0),
        'x_sample': nrm((DEC_BATCH, DEC_SEQ, D_MODEL), 1.0),
        'state_gdn': nrm((N_AB, DEC_BATCH, GDN_HEADS, GDN_DK, GDN_DV), 0.05),
        'state_gdn_conv': nrm((N_AB, DEC_BATCH, CONV_W - 1, GDN_CONV_CH), 1.0),
        'state_lru': nrm((N_AB, DEC_BATCH, LRU_WIDTH), 0.5),
        'state_lru_conv': nrm((N_AB, DEC_BATCH, CONV_W - 1, LRU_WIDTH), 1.0),
        'cache_swa_k': nrm((N_C, DEC_BATCH, win_buf, SWA_KV_HEADS, SWA_HEAD_DIM), 1.0),
        'cache_swa_v': nrm((N_C, DEC_BATCH, win_buf, SWA_KV_HEADS, SWA_HEAD_DIM), 1.0),
        'w_in_ab': nrm((N_AB, D_MODEL, AB_IN_COLS), D_MODEL ** -0.5),
        'gdn_conv_w': nrm((N_AB, CONV_W, GDN_CONV_CH), CONV_W ** -0.5),
        'gdn_a_log': gdn_a_log,
        'gdn_dt_bias': gdn_dt_bias,
        'gdn_norm_w': 1.0 + nrm((N_AB, GDN_DV), 0.01),
        'lru_conv_w': nrm((N_AB, CONV_W, LRU_WIDTH), CONV_W ** -0.5),
        'lru_conv_b': nrm((N_AB, LRU_WIDTH), 0.01),
        'lru_w_r': nrm((N_AB, LRU_BLOCKS, LRU_BW, LRU_BW), LRU_BW ** -0.5),
        'lru_b_r': nrm((N_AB, LRU_WIDTH), 0.01),
        'lru_w_i': nrm((N_AB, LRU_BLOCKS, LRU_BW, LRU_BW), LRU_BW ** -0.5),
        'lru_b_i': nrm((N_AB, LRU_WIDTH), 0.01),
        'lru_lam': lru_lam,
        'w_out_ab': nrm((N_AB, AB_MIX_W, D_MODEL), DN_BETA * AB_MIX_W ** -0.5),
        'w_in_c': nrm((N_C, D_MODEL, C_IN_COLS), D_MODEL ** -0.5),
        'b_in_c': nrm((N_C, C_IN_COLS), 0.01),
        'swa_sinks': nrm((N_C, SWA_HEADS), 0.5),
        'w_out_c': nrm((N_C, SWA_Q_W, D_MODEL), DN_BETA * SWA_Q_W ** -0.5),
        'b_out_c': nrm((N_C, D_MODEL), 0.01),
        'rel_bias': nrm((REL_BUCKETS, SWA_HEADS), 0.3),
        'ln_mix_g': 1.0 + nrm((DEPTH, D_MODEL), 0.01),
        'ln_mix_b': nrm((DEPTH, D_MODEL), 0.01),
        'ln_ffn_g': 1.0 + nrm((DEPTH, D_MODEL), 0.01),
        'ln_ffn_b': nrm((DEPTH, D_MODEL), 0.01),
        'peer_w_q': nrm((DEPTH, D_MODEL, PEER_HEADS * PEER_D_QUERY), D_MODEL ** -0.5),
        'peer_keys': nrm((DEPTH, PEER_HEADS, 2, PEER_N_KEYS, PEER_D_SUB), PEER_D_SUB ** -0.5),
        'peer_u': nrm((DEPTH, PEER_N_EXPERTS, D_MODEL), D_MODEL ** -0.5),
        'peer_v': nrm((DEPTH, PEER_N_EXPERTS, D_MODEL), DN_BETA * PEER_HEADS ** -0.5),
    }


def reference(x_prompt, x_sample, state_gdn, state_gdn_conv, state_lru, state_lru_conv,
              cache_swa_k, cache_swa_v, w_in_ab, gdn_conv_w, gdn_a_log, gdn_dt_bias, gdn_norm_w,
              lru_conv_w, lru_conv_b, lru_w_r, lru_b_r, lru_w_i, lru_b_i, lru_lam, w_out_ab,
              w_in_c, b_in_c, swa_sinks, w_out_c, b_out_c, rel_bias,
              ln_mix_g, ln_mix_b, ln_ffn_g, ln_ffn_b, peer_w_q, peer_keys, peer_u, peer_v):
    xp, xs = x_prompt, x_sample
    bsz = xp.shape[0]
    p_gdn, p_gdn_conv, p_lru, p_lru_conv, p_k, p_v = [], [], [], [], [], []
    s_gdn, s_gdn_conv, s_lru, s_lru_conv, s_k, s_v = [], [], [], [], [], []
    for layer in range(DEPTH):
        j = layer // 2
        if layer % 2 == 0:
            ab_w = (w_in_ab[j], gdn_conv_w[j], gdn_a_log[j], gdn_dt_bias[j], gdn_norm_w[j],
                    lru_conv_w[j], lru_conv_b[j], lru_w_r[j], lru_b_r[j], lru_w_i[j], lru_b_i[j],
                    lru_lam[j], w_out_ab[j])
            mp, st, cb, hl, lb = ab_mixer(
                xp, jnp.zeros((bsz, GDN_HEADS, GDN_DK, GDN_DV), F32),
                jnp.zeros((bsz, CONV_W - 1, GDN_CONV_CH), xp.dtype),
                jnp.zeros((bsz, LRU_WIDTH), F32),
                jnp.zeros((bsz, CONV_W - 1, LRU_WIDTH), xp.dtype), *ab_w)
            p_gdn.append(st); p_gdn_conv.append(cb); p_lru.append(hl); p_lru_conv.append(lb)
            ms, st, cb, hl, lb = ab_mixer(xs, state_gdn[j], state_gdn_conv[j], state_lru[j],
                                          state_lru_conv[j], *ab_w)
            s_gdn.append(st); s_gdn_conv.append(cb); s_lru.append(hl); s_lru_conv.append(lb)
        else:
            c_w = (w_in_c[j], b_in_c[j], swa_sinks[j], w_out_c[j], b_out_c[j], rel_bias)
            mp, kk, vv = swa_prompt(xp, *c_w)
            p_k.append(kk); p_v.append(vv)
            ms, kk, vv = swa_sample(xs, cache_swa_k[j], cache_swa_v[j], *c_w)
            s_k.append(kk); s_v.append(vv)
        xp = layer_norm(DN_ALPHA * xp + mp, ln_mix_g[layer], ln_mix_b[layer])
        xs = layer_norm(DN_ALPHA * xs + ms, ln_mix_g[layer], ln_mix_b[layer])
        peer_w = (peer_w_q[layer], peer_keys[layer], peer_u[layer], peer_v[layer])
        xp = layer_norm(DN_ALPHA * xp + peer(xp, *peer_w), ln_ffn_g[layer], ln_ffn_b[layer])
        xs = layer_norm(DN_ALPHA * xs + peer(xs, *peer_w), ln_ffn_g[layer], ln_ffn_b[layer])
    return (xp, xs,
            jnp.stack(p_gdn), jnp.stack(p_gdn_conv), jnp.stack(p_lru), jnp.stack(p_lru_conv),
            jnp.stack(p_k), jnp.stack(p_v),
            jnp.stack(s_gdn), jnp.stack(s_gdn_conv), jnp.stack(s_lru), jnp.stack(s_lru_conv),
            jnp.stack(s_k), jnp.stack(s_v))
```

```python
import numpy as np
from contextlib import ExitStack
import concourse.bass as bass
import concourse.mybir as mybir
from concourse.bass_utils import run_bass_kernel_spmd

F32 = mybir.dt.float32
U32 = mybir.dt.uint32
I32 = mybir.dt.int32
AF = mybir.ActivationFunctionType
ALU = mybir.AluOpType
AX = mybir.AxisListType

EPOCH = 30000


def _rng(ap):
    sp = str(ap.space)
    a = ap.ap
    off = ap.offset
    if sp in ("SB", "PSUM"):
        row = a[0][0]
        lo = off % row if row > 0 else off
        ext = 0
        for s, c in a[1:]:
            ext += abs(s) * (c - 1)
        hi = lo + ext + 1
        if sp == "PSUM":
            lo = (lo // 512) * 512
            hi = -(-hi // 512) * 512
        return (sp, lo, hi)
    ext = 0
    for s, c in a:
        ext += abs(s) * (c - 1)
    return ("D:" + ap.name, off, off + ext + 1)


class Sched:
    ENG = ("sp", "act", "dve", "pool", "pe")

    def __init__(self, nc, ring=32, n_epochs=4):
        self.nc = nc
        self.eng = {"sp": nc.sync, "act": nc.scalar, "dve": nc.vector,
                    "pool": nc.gpsimd, "pe": nc.tensor}
        self.count = {e: 0 for e in self.ENG}
        self.esems = {e: [] for e in self.ENG}
        self.n_epochs = n_epochs
        self.known = {e: {} for e in self.ENG}
        self.segs = {}
        self.ring = ring
        self.ring_sems = []
        self.ring_cnt = [0] * ring
        self.ring_next = 0
        self.n_wait = 0
        self.n_ins = 0
        self.defer = None
        self.snap = {e: {} for e in self.ENG}

    def alloc_sems(self, stack):
        nc = self.nc
        for e in self.ENG:
            if e == "sp":
                continue
            for i in range(self.n_epochs):
                self.esems[e].append(stack.enter_context(nc.semaphore(f"s_{e}{i}")))
        for i in range(self.ring):
            self.ring_sems.append(stack.enter_context(nc.semaphore(f"s_dma{i}")))

    def _need(self, engine, ev, waits):
        if ev is None:
            return
        if ev[0] == "eng":
            _, f, n = ev
            kk = ("eng", f)
            if self.known[engine].get(kk, 0) >= n:
                return
            self.known[engine][kk] = n
            ep = (n - 1) // EPOCH
            waits.append((self.esems[f][ep], n - ep * EPOCH))
            sn = self.snap[f].get(n)
            if sn:
                kn = self.known[engine]
                for k2, v2 in sn.items():
                    if kn.get(k2, 0) < v2:
                        kn[k2] = v2
        else:
            _, slot, k = ev
            kk = ("dma", slot)
            if self.known[engine].get(kk, 0) >= k:
                return
            self.known[engine][kk] = k
            waits.append((self.ring_sems[slot], 16 * k))

    def _collect(self, engine, ins, outs):
        waits = []
        for ap in ins:
            sp, lo, hi = _rng(ap)
            for seg in self.segs.get(sp, ()):
                if seg[0] < hi and lo < seg[1]:
                    self._need(engine, seg[2], waits)
        for ap in outs:
            sp, lo, hi = _rng(ap)
            for seg in self.segs.get(sp, ()):
                if seg[0] < hi and lo < seg[1]:
                    ev = seg[2]
                    has_readers = bool(seg[3]) or bool(seg[4])
                    if not has_readers and not (engine == "pe" and ev is not None and ev[0] == "eng" and ev[1] == "pe"):
                        self._need(engine, ev, waits)
                    for f, n in seg[3].items():
                        self._need(engine, ("eng", f, n), waits)
                    for sl_, k_ in seg[4].items():
                        self._need(engine, ("dma", sl_, k_), waits)
        return waits

    def _update(self, ev, ins, outs):
        for ap in ins:
            sp, lo, hi = _rng(ap)
            found = False
            for seg in self.segs.setdefault(sp, []):
                if seg[0] < hi and lo < seg[1]:
                    found = True
                    if ev[0] == "eng":
                        seg[3][ev[1]] = ev[2]
                    else:
                        seg[4][ev[1]] = ev[2]
            if not found:
                seg = [lo, hi, None, {}, {}]
                if ev[0] == "eng":
                    seg[3][ev[1]] = ev[2]
                else:
                    seg[4][ev[1]] = ev[2]
                self.segs[sp].append(seg)
        for ap in outs:
            sp, lo, hi = _rng(ap)
            lst = self.segs.setdefault(sp, [])
            keep = []
            for seg in lst:
                if seg[1] <= lo or hi <= seg[0]:
                    keep.append(seg)
                    continue
                if seg[0] < lo:
                    keep.append([seg[0], lo, seg[2], dict(seg[3]), dict(seg[4])])
                if hi < seg[1]:
                    keep.append([hi, seg[1], seg[2], dict(seg[3]), dict(seg[4])])
            keep.append([lo, hi, ev, {}, {}])
            self.segs[sp] = keep

    @staticmethod
    def _split(ins, outs):
        pin = [a for a in ins if str(a.space) == "PSUM"]
        if pin:
            ins = [a for a in ins if str(a.space) != "PSUM"]
            outs = list(outs) + pin
        return ins, outs

    def op(self, engine, fn, ins=(), outs=()):
        if self.defer is not None:
            self.defer.append(lambda: self.op(engine, fn, ins, outs))
            return None
        ins, outs = self._split(ins, outs)
        e = self.eng[engine]
        for s, v in self._collect(engine, ins, outs):
            e.wait_ge(s, v)
            self.n_wait += 1
        i = fn(e)
        self.count[engine] += 1
        n = self.count[engine]
        self.snap[engine][n] = dict(self.known[engine])
        ep = (n - 1) // EPOCH
        assert ep < self.n_epochs, f"too many instructions on {engine}"
        i.then_inc(self.esems[engine][ep], 1)
        self.n_ins += 1
        self._update(("eng", engine, n), ins, outs)
        return i

    def dma(self, engine, fn, ins=(), outs=()):
        if self.defer is not None:
            self.defer.append(lambda: self.dma(engine, fn, ins, outs))
            return None
        e = self.eng[engine]
        slot = self.ring_next
        self.ring_next = (self.ring_next + 1) % self.ring
        k = self.ring_cnt[slot] + 1
        assert 16 * k < 60000
        self.ring_cnt[slot] = k
        waits = self._collect(engine, ins, outs)
        if k > 1:
            self._need(engine, ("dma", slot, k - 1), waits)
        for s, v in waits:
            e.wait_ge(s, v)
            self.n_wait += 1
        i = fn(e)
        i.then_inc(self.ring_sems[slot], 16)
        self.n_ins += 1
        self._update(("dma", slot, k), ins, outs)
        return i

    def finish(self):
        waits = []
        for slot in range(self.ring):
            if self.ring_cnt[slot]:
                self._need("sp", ("dma", slot, self.ring_cnt[slot]), waits)
        for f in self.ENG:
            if f != "sp" and self.count[f]:
                self._need("sp", ("eng", f, self.count[f]), waits)
        for s, v in waits:
            self.nc.sync.wait_ge(s, v)


class Bump:
    def __init__(self, big, total):
        self.big = big
        self.total = total
        self.off = 0
        self.marks = []
        self.peak = 0

    def alloc(self, n, dtype=None, parts=128):
        assert self.off + n <= self.total, f"SBUF overflow {self.off}+{n}>{self.total}"
        ap = self.big[0:parts, self.off:self.off + n]
        self.off += n
        self.peak = max(self.peak, self.off)
        if dtype is not None:
            ap = ap.bitcast(dtype)
        return ap

    def mark(self):
        self.marks.append(self.off)

    def release(self):
        self.off = self.marks.pop()


D = 1024
NPRE = 16
NMAIN = 16
NT = NPRE + NMAIN
NS = 16
ABC = 3080
ALPHA = 4.0 ** 0.25
LN_EPS = 1e-5
SBCOLS = 53000


class StopTile(Exception):
    pass


def ckpt(n):
    import os
    if int(os.environ.get('KSTOP', 10**9)) <= n:
        raise StopTile()


class K:
    def __init__(self, debug=None):
        self.debug = debug or set()
        self.nc = bass.Bass("TRN2", target_bir_lowering=False)
        self.din = {}
        self.dout = {}

    def inp(self, name, shape, dt=F32):
        t = self.nc.dram_tensor(name, list(shape), dt, kind="ExternalInput").ap()
        self.din[name] = t
        return t

    def out(self, name, shape, dt=F32):
        t = self.nc.dram_tensor(name, list(shape), dt, kind="ExternalOutput").ap()
        self.dout[name] = t
        return t

    def scratch(self, name, shape, dt=F32):
        return self.nc.dram_tensor(name, list(shape), dt).ap()

    def mm(self, out, lhsT, rhs, start=True, stop=True):
        return self.S.op("pe", lambda e: e.matmul(out, lhsT=lhsT, rhs=rhs, start=start, stop=stop),
                         ins=[lhsT, rhs], outs=[out])

    def tr(self, out, in_):
        idt = self.ident[0:in_.shape[0], 0:in_.shape[0]]
        return self.S.op("pe", lambda e: e.transpose(out, in_, idt), ins=[in_, idt], outs=[out])

    def act(self, out, in_, func, bias=None, scale=1.0, accum=None, eng="act"):
        ins = [in_]
        kw = {}
        if bias is not None:
            kw["bias"] = bias
            if not isinstance(bias, (int, float)):
                ins.append(bias)
        if not isinstance(scale, (int, float)):
            ins.append(scale)
        outs = [out]
        if accum is not None:
            kw["accum_out"] = accum
            outs.append(accum)
        return self.S.op("act", lambda e: e.activation(out=out, in_=in_, func=func, scale=scale, **kw),
                         ins=ins, outs=outs)

    def cpa(self, out, in_):
        return self.S.op("act", lambda e: e.copy(out=out, in_=in_), ins=[in_], outs=[out])

    def cpv(self, out, in_, eng="dve"):
        return self.S.op(eng, lambda e: e.tensor_copy(out=out, in_=in_), ins=[in_], outs=[out])

    def tt(self, out, in0, in1, op, eng="dve"):
        return self.S.op(eng, lambda e: e.tensor_tensor(out=out, in0=in0, in1=in1, op=op),
                         ins=[in0, in1], outs=[out])

    def ts(self, out, in0, s1, op0, s2=None, op1=None, eng="dve", accum=None):
        ins = [in0]
        for s in (s1, s2):
            if s is not None and not isinstance(s, (int, float)):
                ins.append(s)
        kw = {}
        if op1 is not None:
            kw["op1"] = op1
        outs = [out]
        if accum is not None:
            kw["accum_out"] = accum
            outs.append(accum)
        return self.S.op(eng, lambda e: e.tensor_scalar(out=out, in0=in0, scalar1=s1, scalar2=s2, op0=op0, **kw),
                         ins=ins, outs=outs)

    def stt(self, out, in0, scalar, in1, op0, op1, accum=None):
        ins = [in0, in1]
        if not isinstance(scalar, (int, float)):
            ins.append(scalar)
        outs = [out]
        kw = {}
        if accum is not None:
            kw["accum_out"] = accum
            outs.append(accum)
        return self.S.op("dve", lambda e: e.scalar_tensor_tensor(out=out, in0=in0, scalar=scalar, in1=in1,
                                                                  op0=op0, op1=op1, **kw), ins=ins, outs=outs)

    def red(self, out, in_, op=ALU.add, axis=AX.X):
        return self.S.op("dve", lambda e: e.tensor_reduce(out=out, in_=in_, axis=axis, op=op), ins=[in_], outs=[out])

    def recip(self, out, in_):
        return self.S.op("dve", lambda e: e.reciprocal(out=out, in_=in_), ins=[in_], outs=[out])

    def memset(self, ap, v, eng="dve"):
        return self.S.op(eng, lambda e: e.memset(ap, v), outs=[ap])

    def ld(self, out, in_, eng="sp"):
        return self.S.dma(eng, lambda e: e.dma_start(out=out, in_=in_), ins=[in_], outs=[out])

    def psb(self, b, n=512, parts=128):
        return self.ps[0:parts, b * 512:b * 512 + n]

    def dbg(self, name, ap_sb, shape):
        if name in self.debug:
            o = self.out("dbg_" + name, shape, ap_sb.dtype)
            self.ld(o, ap_sb)

    def resid_ln(self, out, x_tok, y, g_bc, b_bc, P=128):
        bp = self.bp
        bp.mark()
        xa = bp.alloc(1024)[0:P]
        st = bp.alloc(12)[0:P]
        mv = bp.alloc(2)[0:P]
        rs = bp.alloc(1)[0:P]
        self.stt(xa, x_tok, ALPHA, y, ALU.mult, ALU.add)
        self.S.op("dve", lambda e: e.bn_stats(out=st[:, 0:6], in_=xa[:, 0:512]), ins=[xa[:, 0:512]], outs=[st[:, 0:6]])
        self.S.op("dve", lambda e: e.bn_stats(out=st[:, 6:12], in_=xa[:, 512:1024]), ins=[xa[:, 512:1024]], outs=[st[:, 6:12]])
        self.S.op("dve", lambda e: e.bn_aggr(out=mv, in_=st), ins=[st], outs=[mv])
        self.act(rs, mv[:, 1:2], AF.Sqrt, bias=self.eps_ln[0:P], scale=1.0)
        self.recip(rs, rs)
        self.ts(xa, xa, mv[:, 0:1], ALU.subtract, rs, ALU.mult)
        self.tt(xa, xa, g_bc[0:P], ALU.mult)
        self.tt(out, xa, b_bc[0:P], ALU.add)
        bp.release()

    def gelu_tanh(self, out, x, n, P=128):
        bp = self.bp
        bp.mark()
        t = bp.alloc(n)[0:P]
        if len(x.shape) == 3:
            t = t.rearrange("p (a b) -> p a b", a=x.shape[1])
        self.tt(t, x, x, ALU.mult)
        self.ts(t, t, 0.044715, ALU.mult, 1.0, ALU.add)
        self.tt(t, t, x, ALU.mult)
        self.act(t, t, AF.Sigmoid, scale=1.5957691216057308)
        self.tt(out, t, x, ALU.mult)
        bp.release()


def phase_a(k):
    S, bp, nc = k.S, k.bp, k.nc
    din = k.din
    bp.mark()
    w_in = bp.alloc(8 * ABC).rearrange("p (kc c) -> p kc c", kc=8)
    w_in_d = din["w_in_ab"].rearrange("(kc p) c -> p kc c", p=128)
    for kc in range(8):
        k.ld(w_in[:, kc, :], w_in_d[:, kc, :], eng="sp" if kc % 2 == 0 else "act")
    w_oh = bp.alloc(8 * 512).rearrange("p (kc c) -> p kc c", kc=8)
    w_out_d = din["w_out_ab"].rearrange("(kc p) c -> p kc c", p=128)
    cw = bp.alloc(16 * 4).rearrange("p (c i) -> p c i", c=16)
    k.ld(cw, din["conv_w"])
    cb = bp.alloc(4)
    k.ld(cb, din["lru_cb"])
    lb = bp.alloc(8)
    k.ld(lb, din["lru_b"])
    lam = bp.alloc(4)
    k.ld(lam, din["lru_lam"])
    wbd = bp.alloc(8 * 128).rearrange("p (g c) -> p g c", g=8)
    k.ld(wbd, din["lru_wbd"])
    dtb = bp.alloc(4)
    k.ld(dtb, din["gdn_dt_bias"].partition_broadcast(128))
    nA = bp.alloc(4)
    k.ld(nA, din["gdn_a_log"].partition_broadcast(128))
    k.act(nA, nA, AF.Exp)
    k.ts(nA, nA, -1.0, ALU.mult)
    nw = bp.alloc(128)
    k.ld(nw, din["gdn_norm_w"].partition_broadcast(128))
    lng = bp.alloc(1024)
    k.ld(lng, din["ln_mix_g"][0].partition_broadcast(128))
    lnb = bp.alloc(1024)
    k.ld(lnb, din["ln_mix_b"][0].partition_broadcast(128))
    pflag = bp.alloc(1)
    k.ld(pflag, din["pflag"])
    m8sp = bp.alloc(4)
    k.act(m8sp, lam, AF.Exp, scale=-1.0)
    k.act(m8sp, m8sp, AF.Ln, bias=k.one_c, scale=1.0)
    k.ts(m8sp, m8sp, -8.0, ALU.mult)

    bp.mark()
    Sst = bp.alloc(512).rearrange("p (h d) -> p h d", h=4)
    k.memset(Sst, 0.0)
    pc = bp.alloc(16 * 131).rearrange("p (c t) -> p c t", c=16)
    k.memset(pc, 0.0)
    hst = bp.alloc(4)
    k.memset(hst, 0.0)
    xT = [bp.alloc(1024).rearrange("p (kc t) -> p kc t", kc=8) for _ in range(2)]
    xtok = [bp.alloc(1024) for _ in range(1)]

    import os
    for t in range(int(os.environ.get('KT0', 0)), int(os.environ.get('KT1', NT))):
        _off0, _nm0 = bp.off, len(bp.marks)
        try:
            full = t >= NPRE - 1
            bp.mark()
            xTt = xT[t % 2]
            k.ld(xTt, din["xT"][t], eng="sp")
            if full:
                k.ld(xtok[0], din["xtok"][t - (NPRE - 1)], eng="act")
            if t == NPRE:
                k.ts(hst, hst, pflag, ALU.mult)
            hseq = bp.alloc(512).rearrange("p (c t) -> p c t", c=4)
            o = bp.alloc(512).rearrange("p (h d) -> p h d", h=4)
            bp.mark()
            colbase = [c * 128 for c in range(12)] + [2056 + c * 128 for c in range(4)]
            for g4 in range(4):
                for ci in range(4):
                    c = g4 * 4 + ci
                    for kc in range(8):
                        k.mm(k.psb(g4 % 2)[:, ci * 128:(ci + 1) * 128], w_in[:, kc, colbase[c]:colbase[c] + 128],
                             xTt[:, kc, :], start=(kc == 0), stop=(kc == 7))
                k.cpa(pc[:, g4 * 4:(g4 + 1) * 4, 3:131], k.psb(g4 % 2).rearrange("p (c t) -> p c t", c=4))
            ckpt(1)
            cv = bp.alloc(16 * 128).rearrange("p (c t) -> p c t", c=16)
            for c in range(16):
                if c < 12:
                    k.ts(cv[:, c, :], pc[:, c, 0:128], cw[:, c, 0:1], ALU.mult)
                else:
                    k.ts(cv[:, c, :], pc[:, c, 0:128], cw[:, c, 0:1], ALU.mult, cb[:, c - 12:c - 11], ALU.add)
                for i in range(1, 4):
                    k.stt(cv[:, c, :], pc[:, c, i:i + 128], cw[:, c, i:i + 1], cv[:, c, :], ALU.mult, ALU.add)
            ckpt(2)
            k.cpv(pc[:, :, 0:3], pc[:, :, 128:131], eng="pool")
            if t == NT - 1:
                for c in range(16):
                    k.tr(k.ps[0:3, 4 * 512 + c * 128:4 * 512 + (c + 1) * 128], pc[:, c, 128:131])
                bp.mark()
                cst = bp.alloc(2048)
                k.cpv(cst[0:3, :], k.ps[0:3, 4 * 512:8 * 512])
                k.ld(k.dout["p_gdn_conv"], cst[0:3, 0:1536])
                k.ld(k.dout["p_lru_conv"], cst[0:3, 1536:2048])
                bp.release()
            ckpt(3)
            qkv = cv[:, 0:12, :]
            k.act(qkv, qkv, AF.Silu)
            ckpt(4)
            qkn = bp.alloc(1024).rearrange("p (c t) -> p c t", c=8)
            bp.mark()
            sq = bp.alloc(1024).rearrange("p (c t) -> p c t", c=8)
            k.act(sq, cv[:, 0:8, :], AF.Square)
            for hh in range(2):
                k.mm(k.psb(2 + hh), k.ones, sq[:, hh * 4:(hh + 1) * 4, :].rearrange("p c t -> p (c t)"))
            rs = bp.alloc(1024).rearrange("p (c t) -> p c t", c=8)
            k.act(rs[:, 0:4, :].rearrange("p c t -> p (c t)"), k.psb(2), AF.Sqrt, bias=k.c128e6, scale=128.0)
            k.act(rs[:, 4:8, :].rearrange("p c t -> p (c t)"), k.psb(3), AF.Sqrt, bias=k.eps6, scale=1.0)
            k.recip(rs, rs)
            k.tt(qkn, cv[:, 0:8, :], rs, ALU.mult)
            bp.release()
            ckpt(5)
            for kc in range(8):
                k.mm(k.ps[:, 4 * 512:4 * 512 + 8], xTt[:, kc, :], w_in[:, kc, 2048:2056], start=(kc == 0), stop=(kc == 7))
            ab = bp.alloc(8)
            k.cpv(ab, k.ps[:, 4 * 512:4 * 512 + 8])
            for h in range(4):
                k.tr(k.psb(5)[:, h * 128:(h + 1) * 128], cv[:, 8 + h, :])
            vtok = bp.alloc(512).rearrange("p (h d) -> p h d", h=4)
            k.cpa(vtok.rearrange("p h d -> p (h d)"), k.psb(5))
            for h in range(4):
                k.tr(k.psb(6)[:, h * 128:(h + 1) * 128], qkn[:, 4 + h, :])
            ktok = bp.alloc(512).rearrange("p (h d) -> p h d", h=4)
            k.cpa(ktok.rearrange("p h d -> p (h d)"), k.psb(6))
            ckpt(6)
            beta = bp.alloc(4)
            k.act(beta, ab[:, 4:8], AF.Sigmoid)
            tg = bp.alloc(4); t2 = bp.alloc(4); g = bp.alloc(4)
            k.tt(tg, ab[:, 0:4], dtb, ALU.add)
            k.ts(t2, tg, -1.0, ALU.mult)
            k.tt(t2, t2, tg, ALU.max)
            k.act(t2, t2, AF.Exp, scale=-1.0)
            k.act(t2, t2, AF.Ln, bias=k.one_c, scale=1.0)
            k.ts(tg, tg, 0.0, ALU.max)
            k.tt(tg, tg, t2, ALU.add)
            k.tt(g, tg, nA, ALU.mult)
            ckpt(7)
            k.mm(k.ps[:, 4 * 512 + 16:4 * 512 + 20], k.triLE, g)
            k.mm(k.ps[:, 4 * 512 + 32:4 * 512 + 36], k.ones, g)
            gc = bp.alloc(4); gl = bp.alloc(4); eg = bp.alloc(4); ekl = bp.alloc(4); egl = bp.alloc(4)
            k.cpv(gc, k.ps[:, 4 * 512 + 16:4 * 512 + 20])
            k.cpv(gl, k.ps[:, 4 * 512 + 32:4 * 512 + 36])
            k.act(eg, gc, AF.Exp)
            k.tt(ekl, gl, gc, ALU.subtract)
            k.act(ekl, ekl, AF.Exp)
            k.act(egl, gl, AF.Exp)
            ckpt(8)
            ED = bp.alloc(1024).rearrange("p (a h d) -> p a h d", a=2, h=4)
            bp.mark()
            G1 = bp.alloc(512).rearrange("p (h d) -> p h d", h=4)
            k.tt(G1, k.triLE.unsqueeze(1).to_broadcast([128, 4, 128]), g.unsqueeze(2).to_broadcast([128, 4, 128]), ALU.mult)
            k.mm(k.psb(7), k.ones, G1.rearrange("p h d -> p (h d)"))
            X = bp.alloc(512).rearrange("p (h d) -> p h d", h=4)
            k.tt(X, k.psb(7).rearrange("p (h d) -> p h d", h=4), gc.unsqueeze(2).to_broadcast([128, 4, 128]), ALU.subtract)
            k.ts(ED[:, 0], X, 0.0, ALU.min)
            k.ts(ED[:, 1], X, -1.0, ALU.mult, 0.0, ALU.min)
            k.act(ED.rearrange("p a h d -> p (a h d)"), ED.rearrange("p a h d -> p (a h d)"), AF.Exp)
            bp.release()
            ckpt(9)
            for h in range(4):
                k.mm(k.psb(5)[:, h * 128:(h + 1) * 128], qkn[:, 4 + h, :], qkn[:, 4 + h, :])
            for h in range(4):
                k.mm(k.psb(6)[:, h * 128:(h + 1) * 128], qkn[:, 4 + h, :], qkn[:, h, :])
            Y = bp.alloc(512).rearrange("p (h d) -> p h d", h=4)
            QKT = bp.alloc(512).rearrange("p (h d) -> p h d", h=4)
            bp.mark()
            A = [bp.alloc(512).rearrange("p (h d) -> p h d", h=4) for _ in range(2)]
            B = [bp.alloc(512).rearrange("p (h d) -> p h d", h=4) for _ in range(2)]
            L = A[0]
            k.tt(L, k.psb(5).rearrange("p (h d) -> p h d", h=4), ED[:, 1], ALU.mult)
            k.tt(L, L, beta.unsqueeze(2).to_broadcast([128, 4, 128]), ALU.mult)
            k.tt(L, L, k.maskSL.unsqueeze(1).to_broadcast([128, 4, 128]), ALU.mult)
            k.tt(QKT, k.psb(6).rearrange("p (h d) -> p h d", h=4), ED[:, 0], ALU.mult)
            k.tt(QKT, QKT, k.triLE.unsqueeze(1).to_broadcast([128, 4, 128]), ALU.mult)
            for h in range(4):
                k.tr(k.psb(7)[:, h * 128:(h + 1) * 128], L[:, h, :])
            k.cpa(B[0].rearrange("p h d -> p (h d)"), k.psb(7))
            k.tt(Y, k.ident.unsqueeze(1).to_broadcast([128, 4, 128]), B[0], ALU.subtract)
            for j in range(1, 7):
                Ap, Bp = A[(j - 1) % 2], B[(j - 1) % 2]
                An, Bn = A[j % 2], B[j % 2]
                for h in range(4):
                    k.mm(k.psb(5)[:, h * 128:(h + 1) * 128], Bp[:, h, :], Ap[:, h, :])
                if j < 6:
                    for h in range(4):
                        k.mm(k.psb(6)[:, h * 128:(h + 1) * 128], Ap[:, h, :], Bp[:, h, :])
                k.cpa(An.rearrange("p h d -> p (h d)"), k.psb(5))
                if j < 6:
                    k.cpv(Bn.rearrange("p h d -> p (h d)"), k.psb(6))
                for h in range(4):
                    k.mm(k.psb(7)[:, h * 128:(h + 1) * 128], An[:, h, :], Y[:, h, :])
                k.tt(Y, Y, k.psb(7).rearrange("p (h d) -> p h d", h=4), ALU.add)
            bp.release()
            ckpt(10)
            Rv = bp.alloc(512).rearrange("p (h d) -> p h d", h=4)
            Rw = bp.alloc(512).rearrange("p (h d) -> p h d", h=4)
            be = bp.alloc(4)
            k.tt(Rv, vtok, beta.unsqueeze(2).to_broadcast([128, 4, 128]), ALU.mult)
            k.tt(be, beta, eg, ALU.mult)
            k.tt(Rw, ktok, be.unsqueeze(2).to_broadcast([128, 4, 128]), ALU.mult)
            for h in range(4):
                k.mm(k.psb(5)[:, h * 128:(h + 1) * 128], Rw[:, h, :], Y[:, h, :])
            nwT = bp.alloc(512).rearrange("p (h d) -> p h d", h=4)
            k.act(nwT.rearrange("p h d -> p (h d)"), k.psb(5), AF.Copy, scale=-1.0)
            for h in range(4):
                k.mm(k.psb(6)[:, h * 128:(h + 1) * 128], Y[:, h, :], Rv[:, h, :], start=True, stop=False)
                k.mm(k.psb(6)[:, h * 128:(h + 1) * 128], nwT[:, h, :], Sst[:, h, :], start=False, stop=True)
            vnew = bp.alloc(512).rearrange("p (h d) -> p h d", h=4)
            k.cpa(vnew.rearrange("p h d -> p (h d)"), k.psb(6))
            if full:
                for h in range(4):
                    k.mm(k.psb(5)[:, h * 128:(h + 1) * 128], qkn[:, h, :], Sst[:, h, :])
                for h in range(4):
                    k.mm(k.psb(7)[:, h * 128:(h + 1) * 128], QKT[:, h, :], vnew[:, h, :])
                k.tt(o, k.psb(5).rearrange("p (h d) -> p h d", h=4), eg.unsqueeze(2).to_broadcast([128, 4, 128]), ALU.mult)
                k.tt(o, o, k.psb(7).rearrange("p (h d) -> p h d", h=4), ALU.add)
            ckpt(11)
            kp = bp.alloc(512).rearrange("p (h d) -> p h d", h=4)
            k.tt(kp, ktok, ekl.unsqueeze(2).to_broadcast([128, 4, 128]), ALU.mult)
            for h in range(4):
                k.mm(k.psb(6)[:, h * 128:(h + 1) * 128], kp[:, h, :], vnew[:, h, :])
            k.tt(Sst, Sst, egl.unsqueeze(2).to_broadcast([128, 4, 128]), ALU.mult)
            k.tt(Sst, Sst, k.psb(6).rearrange("p (h d) -> p h d", h=4), ALU.add)
            ckpt(12)
            xc = cv[:, 12:16, :]
            for c in range(4):
                k.mm(k.psb(2)[:, c * 128:(c + 1) * 128], wbd[:, c, :], xc[:, c, :])
            for c in range(4):
                k.mm(k.psb(3)[:, c * 128:(c + 1) * 128], wbd[:, 4 + c, :], xc[:, c, :])
            bp.mark()
            ckpt(13)
            r = bp.alloc(512).rearrange("p (c t) -> p c t", c=4)
            ig = bp.alloc(512).rearrange("p (c t) -> p c t", c=4)
            a_ = bp.alloc(512).rearrange("p (c t) -> p c t", c=4)
            th = bp.alloc(512).rearrange("p (c t) -> p c t", c=4)
            for c in range(4):
                k.act(r[:, c, :], k.psb(2)[:, c * 128:(c + 1) * 128], AF.Sigmoid, bias=lb[:, c:c + 1])
                k.act(ig[:, c, :], k.psb(3)[:, c * 128:(c + 1) * 128], AF.Sigmoid, bias=lb[:, 4 + c:5 + c])
            ckpt(14)
            for c in range(4):
                k.act(a_[:, c, :], r[:, c, :], AF.Exp, scale=m8sp[:, c:c + 1])
                k.act(th[:, c, :], r[:, c, :], AF.Tanh, scale=m8sp[:, c:c + 1])
            ckpt(15)
            bb = bp.alloc(512).rearrange("p (c t) -> p c t", c=4)
            k.tt(bb, a_, a_, ALU.mult)
            k.stt(bb, bb, 1.0, th, ALU.add, ALU.mult)
            k.act(bb, bb, AF.Sqrt, scale=-1.0)
            k.tt(bb, bb, ig, ALU.mult)
            k.tt(bb, bb, xc, ALU.mult)
            ckpt(16)
            for c in range(4):
                S.op("dve", lambda e, c=c: e.tensor_tensor_scan(out=hseq[:, c, :], data0=a_[:, c, :], data1=bb[:, c, :],
                                                                initial=hst[:, c:c + 1], op0=ALU.mult, op1=ALU.add),
                     ins=[a_[:, c, :], bb[:, c, :], hst[:, c:c + 1]], outs=[hseq[:, c, :]])
            ckpt(17)
            k.cpv(hst, hseq[:, :, 127])
            bp.release()
            bp.release()
            if full:
                mixT = bp.alloc(1024).rearrange("p (c t) -> p c t", c=8)
                for ci in range(4):
                    for kc in range(8):
                        k.mm(k.psb(2)[:, ci * 128:(ci + 1) * 128], w_in[:, kc, 2568 + ci * 128:2568 + (ci + 1) * 128],
                             xTt[:, kc, :], start=(kc == 0), stop=(kc == 7))
                gt = bp.alloc(512).rearrange("p (c t) -> p c t", c=4)
                k.cpa(gt.rearrange("p c t -> p (c t)"), k.psb(2))
                k.gelu_tanh(gt, gt, 512)
                k.tt(mixT[:, 4:8, :], gt, hseq, ALU.mult)
                for kc in range(8):
                    k.mm(k.psb(3), xTt[:, kc, :], w_in[:, kc, 1536:2048], start=(kc == 0), stop=(kc == 7))
                sz = bp.alloc(512).rearrange("p (h d) -> p h d", h=4)
                k.act(sz.rearrange("p h d -> p (h d)"), k.psb(3), AF.Silu)
                osq = bp.alloc(512).rearrange("p (h d) -> p h d", h=4)
                ss = bp.alloc(4)
                k.tt(osq, o, o, ALU.mult)
                k.red(ss, osq)
                k.act(ss, ss, AF.Sqrt, bias=k.eps6, scale=1.0 / 128.0)
                k.recip(ss, ss)
                k.tt(o, o, ss.unsqueeze(2).to_broadcast([128, 4, 128]), ALU.mult)
                k.tt(o, o, nw.unsqueeze(1).to_broadcast([128, 4, 128]), ALU.mult)
                k.tt(o, o, sz, ALU.mult)
                for h in range(4):
                    k.tr(k.psb(5)[:, h * 128:(h + 1) * 128], o[:, h, :])
                k.cpa(mixT[:, 0:4, :].rearrange("p c t -> p (c t)"), k.psb(5))
                for hf in range(2):
                    for kc in range(8):
                        k.ld(w_oh[:, kc, :], w_out_d[:, kc, hf * 512:(hf + 1) * 512], eng="sp" if kc % 2 == 0 else "act")
                    for fc in range(8):
                        k.mm(k.psb(2 + hf), mixT[:, fc, :], w_oh[:, fc, :], start=(fc == 0), stop=(fc == 7))
                x1 = bp.alloc(1024)
                k.resid_ln(x1, xtok[0], k.ps[:, 2 * 512:4 * 512], lng, lnb)
                k.ld(k.X1[t - (NPRE - 1)], x1)
            bp.release()
        except StopTile:
            bp.off = _off0; del bp.marks[_nm0:]
    k.ld(k.dout["p_gdn"].rearrange("h k v -> k h v"), Sst)
    k.tr(k.ps[0:4, 0:128], hst)
    hl = bp.alloc(128)
    k.cpv(hl[0:4, :], k.ps[0:4, 0:128])
    k.ld(k.dout["p_lru"], hl[0:4, :])
    bp.release()
    phase_a_sample(k, w_in, w_oh, w_out_d, cw, cb, lb, m8sp, wbd, dtb, nA, lng, lnb)
    bp.release()


def phase_a_sample(k, w_in, w_oh, w_out_d, cw, cb, lb, m8sp, wbd, dtb, nA, lng, lnb):
    S, bp, nc = k.S, k.bp, k.nc
    din, dout = k.din, k.dout
    NSS = NS
    bp.mark()
    xT = bp.alloc(8 * NSS).rearrange("p (kc s) -> p kc s", kc=8)
    k.ld(xT, din["xsT"])
    xtok = bp.alloc(1024)
    k.ld(xtok, din["xstok"])
    hist = bp.alloc(16 * 3 * NSS).rearrange("p (c i s) -> p c i s", c=16, i=3)
    k.ld(hist, din["convhT"])
    h0 = bp.alloc(4 * NSS).rearrange("p (c s) -> p c s", c=4)
    k.ld(h0, din["lru_h0T"])
    nwc = bp.alloc(1)
    k.ld(nwc, din["gdn_norm_w"].rearrange("(p o) -> p o", o=1))
    Sall = bp.alloc(NSS * 512).rearrange("p (s h d) -> p s h d", s=NSS, h=4)
    for s in range(NSS):
        k.ld(Sall[:, s], din["s_gdn_in"][s].rearrange("h k v -> k h v"), eng="sp" if s % 2 == 0 else "act")
    k.ld(dout["s_gdn_conv"][:, 0:2, :], din["gconv_nat"][:, 1:3, :])
    k.ld(dout["s_lru_conv"][:, 0:2, :], din["lconv_nat"][:, 1:3, :])
    colbase = [c * 128 for c in range(12)] + [2056 + c * 128 for c in range(4)] + \
              [1536 + c * 128 for c in range(4)] + [2568 + c * 128 for c in range(4)]
    for c in range(24):
        for kc in range(8):
            k.mm(k.psb(0)[:, c * NSS:(c + 1) * NSS], w_in[:, kc, colbase[c]:colbase[c] + 128], xT[:, kc, :],
                 start=(kc == 0), stop=(kc == 7))
    pf = bp.alloc(24 * NSS).rearrange("p (c s) -> p c s", c=24)
    k.cpa(pf.rearrange("p c s -> p (c s)"), k.psb(0)[:, 0:24 * NSS])
    for bi, c0 in enumerate((0, 512, 1024, 2056)):
        for kc in range(8):
            k.mm(k.psb(1 + bi)[0:NSS, :], xT[:, kc, :], w_in[:, kc, c0:c0 + 512], start=(kc == 0), stop=(kc == 7))
    bp.mark()
    pre = bp.alloc(2048)
    k.cpv(pre[0:NSS, 0:1024], k.ps[0:NSS, 512:1536])
    k.cpa(pre[0:NSS, 1024:2048], k.ps[0:NSS, 1536:2560])
    k.ld(dout["s_gdn_conv"][:, 2, :], pre[0:NSS, 0:1536])
    k.ld(dout["s_lru_conv"][:, 2, :], pre[0:NSS, 1536:2048])
    bp.release()
    for kc in range(8):
        k.mm(k.psb(5)[0:NSS, 0:8], xT[:, kc, :], w_in[:, kc, 2048:2056], start=(kc == 0), stop=(kc == 7))
    ab = bp.alloc(8)
    k.cpv(ab[0:NSS], k.psb(5)[0:NSS, 0:8])
    cv = bp.alloc(16 * NSS).rearrange("p (c s) -> p c s", c=16)
    tmp = bp.alloc(16 * 3 * NSS).rearrange("p (c i s) -> p c i s", c=16, i=3)
    k.tt(tmp, hist, cw[:, :, 0:3].unsqueeze(3).to_broadcast([128, 16, 3, NSS]), ALU.mult)
    k.red(cv, tmp.rearrange("p c i s -> p c s i"))
    t2 = bp.alloc(16 * NSS).rearrange("p (c s) -> p c s", c=16)
    k.tt(t2, pf[:, 0:16, :], cw[:, :, 3:4].to_broadcast([128, 16, NSS]), ALU.mult)
    k.tt(cv, cv, t2, ALU.add)
    k.tt(cv[:, 12:16, :], cv[:, 12:16, :], cb.unsqueeze(2).to_broadcast([128, 4, NSS]), ALU.add)
    k.act(cv[:, 0:12, :], cv[:, 0:12, :], AF.Silu)
    sq = bp.alloc(8 * NSS)
    k.act(sq.rearrange("p (c s) -> p c s", c=8), cv[:, 0:8, :], AF.Square)
    k.mm(k.psb(5)[:, 128:128 + 8 * NSS], k.ones, sq)
    rs = bp.alloc(8 * NSS)
    k.act(rs[:, 0:4 * NSS], k.psb(5)[:, 128:128 + 4 * NSS], AF.Sqrt, bias=k.c128e6, scale=128.0)
    k.act(rs[:, 4 * NSS:8 * NSS], k.psb(5)[:, 128 + 4 * NSS:128 + 8 * NSS], AF.Sqrt, bias=k.eps6, scale=1.0)
    k.recip(rs, rs)
    qkn = bp.alloc(8 * NSS).rearrange("p (c s) -> p c s", c=8)
    k.tt(qkn, cv[:, 0:8, :], rs.rearrange("p (c s) -> p c s", c=8), ALU.mult)
    P = NSS
    beta = bp.alloc(12)
    tg = bp.alloc(4); tt2 = bp.alloc(4)
    k.act(beta[0:P, 0:4], ab[0:P, 4:8], AF.Sigmoid)
    k.tt(tg[0:P], ab[0:P, 0:4], dtb[0:P], ALU.add)
    k.ts(tt2[0:P], tg[0:P], -1.0, ALU.mult)
    k.tt(tt2[0:P], tt2[0:P], tg[0:P], ALU.max)
    k.act(tt2[0:P], tt2[0:P], AF.Exp, scale=-1.0)
    k.act(tt2[0:P], tt2[0:P], AF.Ln, bias=k.one_c[0:P], scale=1.0)
    k.ts(tg[0:P], tg[0:P], 0.0, ALU.max)
    k.tt(tg[0:P], tg[0:P], tt2[0:P], ALU.add)
    k.tt(tg[0:P], tg[0:P], nA[0:P], ALU.mult)
    k.act(beta[0:P, 4:8], tg[0:P], AF.Exp)
    bd = bp.alloc(2 * 4 * NSS).rearrange("p (a h s) -> p a h s", a=2, h=4)
    k.tt(bd[0:P], beta[0:P, 0:8].rearrange("p (a h) -> p a h", a=2).unsqueeze(3).to_broadcast([P, 2, 4, NSS]),
         k.ident[0:P, 0:NSS].unsqueeze(1).unsqueeze(1).to_broadcast([P, 2, 4, NSS]), ALU.mult)
    k.mm(k.psb(5)[:, 256:256 + 8 * NSS], k.ones[0:P, :], bd[0:P].rearrange("p a h s -> p (a h s)"))
    BC = bp.alloc(3 * 4 * NSS).rearrange("p (a h s) -> p a h s", a=3, h=4)
    k.cpv(BC[:, 0:2].rearrange("p a h s -> p (a h s)"), k.psb(5)[:, 256:256 + 8 * NSS])
    prod = bp.alloc(4 * NSS)
    k.tt(prod.rearrange("p (h s) -> p h s", h=4), qkn[:, 0:4, :], qkn[:, 4:8, :], ALU.mult)
    k.mm(k.psb(5)[:, 384:384 + 4 * NSS], k.ones, prod)
    k.cpv(BC[:, 2].rearrange("p h s -> p (h s)"), k.psb(5)[:, 384:384 + 4 * NSS])
    kq = bp.alloc(4 * NSS * 2).rearrange("p (h s a) -> p h s a", h=4, s=NSS)
    k.cpv(kq[:, :, :, 0], qkn[:, 4:8, :])
    k.cpv(kq[:, :, :, 1], qkn[:, 0:4, :])
    for h in range(4):
        for s in range(NSS):
            c0 = (h * NSS + s) * 2
            k.mm(k.psb(6)[:, c0:c0 + 2], Sall[:, s, h, :], kq[:, h, s, :])
    ksqs = bp.alloc(4 * NSS * 2).rearrange("p (h s a) -> p h s a", h=4, s=NSS)
    k.cpv(ksqs.rearrange("p h s a -> p (h s a)"), k.psb(6)[:, 0:4 * NSS * 2])
    vnT = bp.alloc(4 * NSS).rearrange("p (h s) -> p h s", h=4)
    oT = bp.alloc(4 * NSS).rearrange("p (h s) -> p h s", h=4)
    k.tt(vnT, ksqs[:, :, :, 0], BC[:, 1], ALU.mult)
    k.tt(vnT, cv[:, 8:12, :], vnT, ALU.subtract)
    k.tt(vnT, vnT, BC[:, 0], ALU.mult)
    k.tt(oT, ksqs[:, :, :, 1], BC[:, 1], ALU.mult)
    t3 = bp.alloc(4 * NSS).rearrange("p (h s) -> p h s", h=4)
    k.tt(t3, vnT, BC[:, 2], ALU.mult)
    k.tt(oT, oT, t3, ALU.add)
    for h in range(4):
        k.tr(k.psb(7)[0:NSS, h * 128:(h + 1) * 128], vnT[:, h, :])
    vn_tok = bp.alloc(512)
    k.cpv(vn_tok[0:P], k.psb(7)[0:NSS, :])
    for h in range(4):
        k.tr(k.psb(7)[0:NSS, h * 128:(h + 1) * 128], qkn[:, 4 + h, :])
    k_tok = bp.alloc(512)
    k.cpv(k_tok[0:P], k.psb(7)[0:NSS, :])
    Am = [bp.alloc(512) for _ in range(2)]
    for s in range(NSS):
        am = Am[s % 2]
        k.ts(am[0:P], k_tok[0:P], k.ident[0:P, s:s + 1], ALU.mult)
        bank = 1 + s % 2
        for h in range(4):
            k.mm(k.psb(bank)[:, h * 128:(h + 1) * 128], am[0:P, h * 128:(h + 1) * 128], vn_tok[0:P, h * 128:(h + 1) * 128])
        for h in range(4):
            k.stt(Sall[:, s, h, :], Sall[:, s, h, :], BC[:, 1, h, s:s + 1], k.psb(bank)[:, h * 128:(h + 1) * 128], ALU.mult, ALU.add)
        k.ld(dout["s_gdn"][s].rearrange("h k v -> k h v"), Sall[:, s], eng="sp" if s % 2 == 0 else "act")
    mixT = bp.alloc(8 * NSS).rearrange("p (c s) -> p c s", c=8)
    osq = bp.alloc(4 * NSS)
    k.tt(osq.rearrange("p (h s) -> p h s", h=4), oT, oT, ALU.mult)
    k.mm(k.psb(5)[:, 0:4 * NSS], k.ones, osq)
    rso = bp.alloc(4 * NSS)
    k.act(rso, k.psb(5)[:, 0:4 * NSS], AF.Sqrt, bias=k.eps6, scale=1.0 / 128.0)
    k.recip(rso, rso)
    k.tt(oT, oT, rso.rearrange("p (h s) -> p h s", h=4), ALU.mult)
    k.ts(oT.rearrange("p h s -> p (h s)"), oT.rearrange("p h s -> p (h s)"), nwc, ALU.mult)
    sz = bp.alloc(4 * NSS).rearrange("p (h s) -> p h s", h=4)
    k.act(sz, pf[:, 16:20, :], AF.Silu)
    k.tt(mixT[:, 0:4, :], oT, sz, ALU.mult)
    xc = cv[:, 12:16, :]
    for c in range(4):
        k.mm(k.psb(5)[:, 128 + c * NSS:128 + (c + 1) * NSS], wbd[:, c, :], xc[:, c, :])
        k.mm(k.psb(5)[:, 256 + c * NSS:256 + (c + 1) * NSS], wbd[:, 4 + c, :], xc[:, c, :])
    r = bp.alloc(4 * NSS).rearrange("p (c s) -> p c s", c=4)
    ig = bp.alloc(4 * NSS).rearrange("p (c s) -> p c s", c=4)
    a_ = bp.alloc(4 * NSS).rearrange("p (c s) -> p c s", c=4)
    th = bp.alloc(4 * NSS).rearrange("p (c s) -> p c s", c=4)
    for c in range(4):
        k.act(r[:, c, :], k.psb(5)[:, 128 + c * NSS:128 + (c + 1) * NSS], AF.Sigmoid, bias=lb[:, c:c + 1])
        k.act(ig[:, c, :], k.psb(5)[:, 256 + c * NSS:256 + (c + 1) * NSS], AF.Sigmoid, bias=lb[:, 4 + c:5 + c])
    for c in range(4):
        k.act(a_[:, c, :], r[:, c, :], AF.Exp, scale=m8sp[:, c:c + 1])
        k.act(th[:, c, :], r[:, c, :], AF.Tanh, scale=m8sp[:, c:c + 1])
    bb = bp.alloc(4 * NSS).rearrange("p (c s) -> p c s", c=4)
    k.tt(bb, a_, a_, ALU.mult)
    k.stt(bb, bb, 1.0, th, ALU.add, ALU.mult)
    k.act(bb, bb, AF.Sqrt, scale=-1.0)
    k.tt(bb, bb, ig, ALU.mult)
    k.tt(bb, bb, xc, ALU.mult)
    hn = bp.alloc(4 * NSS).rearrange("p (c s) -> p c s", c=4)
    k.tt(hn, a_, h0, ALU.mult)
    k.tt(hn, hn, bb, ALU.add)
    for c in range(4):
        k.tr(k.psb(7)[0:NSS, c * 128:(c + 1) * 128], hn[:, c, :])
    hl = bp.alloc(512)
    k.cpv(hl[0:P], k.psb(7)[0:NSS, :])
    k.ld(dout["s_lru"], hl[0:P])
    gt = bp.alloc(4 * NSS).rearrange("p (c s) -> p c s", c=4)
    k.gelu_tanh(gt, pf[:, 20:24, :], 4 * NSS)
    k.tt(mixT[:, 4:8, :], gt, hn, ALU.mult)
    for hf in range(2):
        for kc in range(8):
            k.ld(w_oh[:, kc, :], w_out_d[:, kc, hf * 512:(hf + 1) * 512], eng="sp" if kc % 2 == 0 else "act")
        for fc in range(8):
            k.mm(k.psb(2 + hf)[0:NSS, :], mixT[:, fc, :], w_oh[:, fc, :], start=(fc == 0), stop=(fc == 7))
    x1 = bp.alloc(1024)
    k.memset(x1, 0.0)
    k.resid_ln(x1[0:P], xtok[0:P], k.ps[0:NSS, 2 * 512:4 * 512], lng, lnb, P=P)
    k.ld(k.X1[NMAIN + 1], x1)
    bp.release()


def phase_peer(k, layer, Xin, Xout_fn, ntiles):
    S, bp, nc = k.S, k.bp, k.nc
    din = k.din
    NB = 16
    bp.mark()
    wq = bp.alloc(8 * 2048).rearrange("p (kc c) -> p kc c", kc=8)
    wq_d = din["peer_w_q"][layer].rearrange("(kc p) c -> p kc c", p=128)
    for kc in range(8):
        k.ld(wq[:, kc, :], wq_d[:, kc, :], eng="sp" if kc % 2 == 0 else "act")
    keysT = bp.alloc(16 * 128).rearrange("p (g n) -> p g n", g=16)
    k.ld(keysT, din["peer_keysT"][layer])
    lng = bp.alloc(1024); k.ld(lng, din["ln_ffn_g"][layer].partition_broadcast(128))
    lnb = bp.alloc(1024); k.ld(lnb, din["ln_ffn_b"][layer].partition_broadcast(128))
    iota16 = bp.alloc(16); k.ld(iota16, din["iota16"].partition_broadcast(128))
    u_tab = din["peer_u%d" % layer]
    v_tab = din["peer_v%d" % layer]
    gbuf = [bp.alloc(1024) for _ in range(NB)]
    st = [dict(x1=bp.alloc(1024), eidx=bp.alloc(128, I32), gate=bp.alloc(128).rearrange("p (h k) -> p h k", h=8))
          for _ in range(2)]
    xT = bp.alloc(1024).rearrange("p (kc t) -> p kc t", kc=8)
    qT = bp.alloc(2048).rearrange("p (g t) -> p g t", g=16)
    sc = bp.alloc(2048).rearrange("p (g n) -> p g n", g=16)
    m16 = bp.alloc(256).rearrange("p (g k) -> p g k", g=16)
    i16 = bp.alloc(256, U32).rearrange("p (g k) -> p g k", g=16)
    tmp = bp.alloc(256)
    cand = qT.rearrange("p g t -> p (g t)").rearrange("p (h i j) -> p h i j", h=8, i=16)
    b16 = bp.alloc(128).rearrange("p (h k) -> p h k", h=8)
    pos = bp.alloc(128, U32).rearrange("p (h k) -> p h k", h=8)
    hif = bp.alloc(128).rearrange("p (h k) -> p h k", h=8)
    lof = bp.alloc(128).rearrange("p (h k) -> p h k", h=8)
    i16f = bp.alloc(256).rearrange("p (g k) -> p g k", g=16)
    pu = bp.alloc(256, U32)
    sel = bp.alloc(256).rearrange("p (a h k) -> p a h k", a=2, h=8)
    ef = bp.alloc(128)
    gs = bp.alloc(8)
    actp = bp.alloc(128)
    junk = [bp.alloc(1024) for _ in range(2)]
    wgt = bp.alloc(128)
    eidx2 = bp.alloc(16, I32)
    gate2 = bp.alloc(16)
    x1rep = sc.rearrange("p g n -> p (g n)")[:, 0:1024]
    ssel = bp.alloc(128); k.ld(ssel, din["ssel"])
    scrE = k.scratch("peer_scr_e%d" % layer, [16, 128], I32)
    scrG = k.scratch("peer_scr_g%d" % layer, [16, 128])
    full_run = ntiles >= NMAIN + 1
    dgb = [bp.alloc(128) for _ in range(3)]

    def top16(vals, mo, io, tmpb):
        S.op("dve", lambda e: e.max(out=mo[:, 0:8], in_=vals), ins=[vals], outs=[mo[:, 0:8]])
        S.op("dve", lambda e: e.max_index(out=io[:, 0:8], in_max=mo[:, 0:8], in_values=vals), ins=[vals, mo[:, 0:8]], outs=[io[:, 0:8]])
        S.op("dve", lambda e: e.match_replace(out=tmpb, in_to_replace=mo[:, 0:8], in_values=vals, imm_value=-1e30),
             ins=[vals, mo[:, 0:8]], outs=[tmpb])
        S.op("dve", lambda e: e.max(out=mo[:, 8:16], in_=tmpb), ins=[tmpb], outs=[mo[:, 8:16]])
        S.op("dve", lambda e: e.max_index(out=io[:, 8:16], in_max=mo[:, 8:16], in_values=tmpb), ins=[tmpb, mo[:, 8:16]], outs=[io[:, 8:16]])

    def front(t):
        x1, eidx, gate = st[t % 2]["x1"], st[t % 2]["eidx"], st[t % 2]["gate"]
        k.ld(x1, Xin[t])
        for hf in range(2):
            for c in range(4):
                k.tr(k.psb(hf)[:, c * 128:(c + 1) * 128], x1[:, (hf * 4 + c) * 128:(hf * 4 + c + 1) * 128])
            k.cpa(xT[:, hf * 4:(hf + 1) * 4, :].rearrange("p c t -> p (c t)"), k.psb(hf))
        for g4 in range(4):
            for ci in range(4):
                g = g4 * 4 + ci
                for kc in range(8):
                    k.mm(k.psb(2 + g4 % 2)[:, ci * 128:(ci + 1) * 128], wq[:, kc, g * 128:(g + 1) * 128], xT[:, kc, :],
                         start=(kc == 0), stop=(kc == 7))
            k.cpa(qT[:, g4 * 4:(g4 + 1) * 4, :].rearrange("p c t -> p (c t)"), k.psb(2 + g4 % 2))
        for g4 in range(4):
            for ci in range(4):
                g = g4 * 4 + ci
                k.mm(k.psb(4 + g4 % 2)[:, ci * 128:(ci + 1) * 128], qT[:, g, :], keysT[:, g, :])
            k.cpa(sc[:, g4 * 4:(g4 + 1) * 4, :].rearrange("p c t -> p (c t)"), k.psb(4 + g4 % 2))
        for g in range(16):
            top16(sc[:, g, :], m16[:, g, :], i16[:, g, :], tmp[:, 0:128])
        for h in range(8):
            k.tt(cand[:, h], m16[:, 2 * h, :].unsqueeze(2).to_broadcast([128, 16, 16]),
                 m16[:, 2 * h + 1, :].unsqueeze(1).to_broadcast([128, 16, 16]), ALU.add)
        for h in range(8):
            top16(cand[:, h].rearrange("p i j -> p (i j)"), b16[:, h, :], pos[:, h, :], tmp)
        k.cpv(i16f.rearrange("p g k -> p (g k)"), i16.rearrange("p g k -> p (g k)"))
        posu = pos.rearrange("p h k -> p (h k)")
        S.op("dve", lambda e: e.tensor_single_scalar(out=pu[:, 0:128], in_=posu, scalar=4, op=ALU.logical_shift_right), ins=[posu], outs=[pu[:, 0:128]])
        S.op("dve", lambda e: e.tensor_single_scalar(out=pu[:, 128:256], in_=posu, scalar=15, op=ALU.bitwise_and), ins=[posu], outs=[pu[:, 128:256]])
        k.cpv(hif.rearrange("p h k -> p (h k)"), pu[:, 0:128])
        k.cpv(lof.rearrange("p h k -> p (h k)"), pu[:, 128:256])
        eq = cand
        i16v = i16f.rearrange("p (h c) k -> p h c k", c=2)
        for a, idxf in enumerate((hif, lof)):
            k.tt(eq, iota16.unsqueeze(1).unsqueeze(1).to_broadcast([128, 8, 16, 16]),
                 idxf.unsqueeze(3).to_broadcast([128, 8, 16, 16]), ALU.is_equal)
            k.tt(eq, eq, i16v[:, :, a, :].unsqueeze(2).to_broadcast([128, 8, 16, 16]), ALU.mult)
            k.red(sel[:, a], eq)
        k.stt(ef, sel[:, 0].rearrange("p h k -> p (h k)"), 128.0, sel[:, 1].rearrange("p h k -> p (h k)"), ALU.mult, ALU.add)
        k.cpv(eidx, ef)
        k.tt(gate, b16, b16[:, :, 0:1].to_broadcast([128, 8, 16]), ALU.subtract)
        k.act(gate.rearrange("p h k -> p (h k)"), gate.rearrange("p h k -> p (h k)"), AF.Exp)
        k.red(gs, gate)
        k.recip(gs, gs)
        k.tt(gate, gate, gs.unsqueeze(2).to_broadcast([128, 8, 16]), ALU.mult)

    def record_front(t):
        S.defer = []
        front(t)
        lst = S.defer
        S.defer = None
        return lst

    front(0)
    gi = 0
    for t in range(ntiles):
        x1, eidx, gate = st[t % 2]["x1"], st[t % 2]["eidx"], st[t % 2]["gate"]
        pend = record_front(t + 1) if t + 1 < ntiles else []
        pi = 0
        per_step = -(-len(pend) // 120) if pend else 0
        if full_run and t == ntiles - 1:
            k.ld(scrE, eidx[0:16, :])
            k.ld(scrG, gate.rearrange("p h k -> p (h k)")[0:16, :])
            k.ld(eidx2, scrE.rearrange("t (h k) -> (t h) k", h=8))
            k.ld(gate2, scrG.rearrange("t (h k) -> (t h) k", h=8))
            for tok in range(16):
                k.ld(x1rep[tok * 8:(tok + 1) * 8, :], Xin[t][tok].partition_broadcast(8), eng="sp" if tok % 2 == 0 else "act")
            for kk in range(16):
                ub = gbuf[gi % NB]; gi += 1
                S.dma("pool", lambda e, ub=ub, kk=kk: e.indirect_dma_start(out=ub, out_offset=None, in_=u_tab,
                      in_offset=bass.IndirectOffsetOnAxis(ap=eidx2[:, kk:kk + 1], axis=0)), ins=[eidx2[:, kk:kk + 1], u_tab], outs=[ub])
                k.stt(junk[kk % 2], ub, 1.0, x1rep, ALU.mult, ALU.mult, accum=actp[:, kk:kk + 1])
            k.gelu_tanh(wgt[:, 0:16], actp[:, 0:16], 16)
            k.tt(wgt[:, 0:16], wgt[:, 0:16], gate2, ALU.mult)
            yacc = junk[0]
            k.memset(yacc, 0.0)
            for kk in range(16):
                vb = gbuf[gi % NB]; gi += 1
                S.dma("pool", lambda e, vb=vb, kk=kk: e.indirect_dma_start(out=vb, out_offset=None, in_=v_tab,
                      in_offset=bass.IndirectOffsetOnAxis(ap=eidx2[:, kk:kk + 1], axis=0)), ins=[eidx2[:, kk:kk + 1], v_tab], outs=[vb])
                k.stt(yacc, vb, wgt[:, kk:kk + 1], yacc, ALU.mult, ALU.add)
            k.mm(k.psb(6), ssel, yacc[:, 0:512])
            k.mm(k.psb(7), ssel, yacc[:, 512:1024])
            k.cpv(junk[1], k.ps[:, 6 * 512:8 * 512])
            k.resid_ln(x1, x1, junk[1], lng, lnb)
            k.ld(Xout_fn(t), x1)
            continue
        for s in range(128):
            ub = gbuf[gi % NB]; gi += 1
            S.dma("pool", lambda e, ub=ub, s=s, eidx=eidx: e.indirect_dma_start(out=ub, out_offset=None, in_=u_tab,
                  in_offset=bass.IndirectOffsetOnAxis(ap=eidx[:, s:s + 1], axis=0)), ins=[eidx[:, s:s + 1], u_tab], outs=[ub])
            k.stt(junk[s % 2], ub, 1.0, x1, ALU.mult, ALU.mult, accum=actp[:, s:s + 1])
        k.gelu_tanh(wgt, actp, 128)
        k.tt(wgt, wgt, gate.rearrange("p h k -> p (h k)"), ALU.mult)
        yacc = junk[0]
        k.memset(junk[0], 0.0)
        k.memset(junk[1], 0.0, eng="pool")
        pe_slots = [s for s in range(128) if s % 8 in (1, 4, 6)]
        n_dve = 0
        for s in range(128):
            vb = gbuf[gi % NB]; gi += 1
            S.dma("pool", lambda e, vb=vb, s=s, eidx=eidx: e.indirect_dma_start(out=vb, out_offset=None, in_=v_tab,
                  in_offset=bass.IndirectOffsetOnAxis(ap=eidx[:, s:s + 1], axis=0)), ins=[eidx[:, s:s + 1], v_tab], outs=[vb])
            if s % 8 in (1, 4, 6):
                dg = dgb[s % 3]
                k.act(dg, k.ident, AF.Identity, scale=wgt[:, s:s + 1])
                first, last = (s == pe_slots[0]), (s == pe_slots[-1])
                k.mm(k.psb(6), dg, vb[:, 0:512], start=first, stop=last)
                k.mm(k.psb(7), dg, vb[:, 512:1024], start=first, stop=last)
            else:
                ya = junk[n_dve % 2]; n_dve += 1
                k.stt(ya, vb, wgt[:, s:s + 1], ya, ALU.mult, ALU.add)
            if s >= 4:
                for _ in range(per_step):
                    if pi < len(pend):
                        pend[pi](); pi += 1
        while pi < len(pend):
            pend[pi](); pi += 1
        k.tt(yacc, yacc, junk[1], ALU.add)
        k.tt(yacc, yacc, k.ps[:, 6 * 512:8 * 512], ALU.add)
        k.resid_ln(x1, x1, yacc, lng, lnb)
        k.ld(Xout_fn(t), x1)
    bp.release()


def phase_c(k, Xin, Xout):
    S, bp, nc = k.S, k.bp, k.nc
    din = k.din
    bp.mark()
    w_in = bp.alloc(8 * 1536).rearrange("p (kc c) -> p kc c", kc=8)
    w_in_d = din["w_in_c"].rearrange("(kc p) c -> p kc c", p=128)
    for kc in range(8):
        k.ld(w_in[:, kc, :], w_in_d[:, kc, :], eng="sp" if kc % 2 == 0 else "act")
    w_out = bp.alloc(8 * 1024).rearrange("p (kc c) -> p kc c", kc=8)
    w_out_d = din["w_out_c"].rearrange("(kc p) c -> p kc c", p=128)
    for kc in range(8):
        k.ld(w_out[:, kc, :], w_out_d[:, kc, :], eng="sp" if kc % 2 == 0 else "act")
    bqk = bp.alloc(10); k.ld(bqk, din["b_in_qk"])
    bkv = bp.alloc(512); k.ld(bkv, din["b_in_c"][1024:1536].partition_broadcast(128))
    bout = bp.alloc(1024); k.ld(bout, din["b_out_c"].partition_broadcast(128))
    lng = bp.alloc(1024); k.ld(lng, din["ln_mix_g"][1].partition_broadcast(128))
    lnb = bp.alloc(1024); k.ld(lnb, din["ln_mix_b"][1].partition_broadcast(128))
    sink = bp.alloc(16); k.ld(sink, din["swa_sinks"].partition_broadcast(128))
    hmask = bp.alloc(128); k.ld(hmask, din["hmask"])
    rb = bp.alloc(16)
    k.memset(rb[0:33, :], -30000.0)
    k.ld(rb[0:32, :], din["rel_bias"])
    bp.mark()
    Tb = bp.alloc(16 * 256).rearrange("p (h c) -> p h c", h=16)
    bp.mark()
    ohs_ = [bp.alloc(4096) for _ in range(2)]
    stages_ = [bp.alloc(4096) for _ in range(2)]
    Tscr = k.scratch("Tscr", [16, 128 * 256])
    for blk in range(8):
        oh, stage = ohs_[blk % 2], stages_[blk % 2]
        k.ld(oh[0:33, :], din["t5_onehot"][:, blk * 4096:(blk + 1) * 4096], eng="sp")
        for j in range(8):
            k.mm(k.ps[0:16, j * 512:(j + 1) * 512], rb[0:33, :], oh[0:33, j * 512:(j + 1) * 512])
        k.cpv(stage[0:16, 0:2048], k.ps[0:16, 0:2048])
        k.cpa(stage[0:16, 2048:4096], k.ps[0:16, 2048:4096])
        k.ld(Tscr[:, blk * 4096:(blk + 1) * 4096], stage[0:16, :], eng="act")
    bp.release()
    k.ld(Tb, Tscr.rearrange("h (q c) -> q h c", q=128))
    KT = [bp.alloc(256).rearrange("p (c t) -> p c t", c=2) for _ in range(2)]
    V = [bp.alloc(256) for _ in range(2)]
    ntile = NMAIN + 1
    for t in range(ntile):
        bp.mark()
        cur, prv = t % 2, (t + 1) % 2
        x2 = bp.alloc(1024)
        k.ld(x2, Xin[t])
        xT = bp.alloc(1024).rearrange("p (kc t) -> p kc t", kc=8)
        for hf in range(2):
            for c in range(4):
                k.tr(k.psb(hf)[:, c * 128:(c + 1) * 128], x2[:, (hf * 4 + c) * 128:(hf * 4 + c + 1) * 128])
            k.cpa(xT[:, hf * 4:(hf + 1) * 4, :].rearrange("p c t -> p (c t)"), k.psb(hf))
        for c in range(2):
            for kc in range(8):
                k.mm(k.psb(2)[:, c * 128:(c + 1) * 128], w_in[:, kc, 1024 + c * 128:1024 + (c + 1) * 128], xT[:, kc, :],
                     start=(kc == 0), stop=(kc == 7))
        for c in range(2):
            k.act(KT[cur][:, c, :], k.psb(2)[:, c * 128:(c + 1) * 128], AF.Identity, bias=bqk[:, 8 + c:9 + c])
        for kc in range(8):
            k.mm(k.psb(3), xT[:, kc, :], w_in[:, kc, 1024:1536], start=(kc == 0), stop=(kc == 7))
        kvtok = bp.alloc(512)
        k.tt(kvtok, k.psb(3), bkv, ALU.add)
        k.cpv(V[cur], kvtok[:, 256:512], eng="pool")
        if t == ntile - 1:
            k.ld(k.dout["p_swa_k"], kvtok[:, 0:256])
            k.ld(k.dout["p_swa_v"], kvtok[:, 256:512])
        if t == 0:
            bp.release()
            continue
        QT = bp.alloc(1024).rearrange("p (c t) -> p c t", c=8)
        for g4 in range(2):
            for ci in range(4):
                c = g4 * 4 + ci
                for kc in range(8):
                    k.mm(k.psb(4 + g4)[:, ci * 128:(ci + 1) * 128], w_in[:, kc, c * 128:(c + 1) * 128], xT[:, kc, :],
                         start=(kc == 0), stop=(kc == 7))
            for ci in range(4):
                c = g4 * 4 + ci
                k.act(QT[:, c, :], k.psb(4 + g4)[:, ci * 128:(ci + 1) * 128], AF.Identity, bias=bqk[:, c:c + 1])
        lg = bp.alloc(16 * 256).rearrange("p (h c) -> p h c", h=16)
        mx = bp.alloc(16); nmx = bp.alloc(16); ssum = bp.alloc(16); es = bp.alloc(16)
        for hp in range(8):
            bank = 6 + hp % 2
            for hh in range(2):
                h = hp * 2 + hh
                j = h // 4
                half = j % 2
                m = (h // 8) * 4 + h % 4
                qh = QT[half * 64:(half + 1) * 64, m, :]
                k.mm(k.psb(bank)[:, hh * 256:hh * 256 + 128], qh, KT[prv][half * 64:(half + 1) * 64, j // 2, :])
                k.mm(k.psb(bank)[:, hh * 256 + 128:hh * 256 + 256], qh, KT[cur][half * 64:(half + 1) * 64, j // 2, :])
            k.stt(lg[:, hp * 2:hp * 2 + 2, :], k.psb(bank).rearrange("p (h c) -> p h c", h=2), 0.125,
                  Tb[:, hp * 2:hp * 2 + 2, :], ALU.mult, ALU.add)
        if t == 1:
            k.tt(lg[:, :, 0:128], lg[:, :, 0:128], hmask.unsqueeze(1).to_broadcast([128, 16, 128]), ALU.add)
        k.red(mx, lg, op=ALU.max)
        k.tt(mx, mx, sink, ALU.max)
        k.ts(nmx, mx, -1.0, ALU.mult)
        for h in range(16):
            k.act(lg[:, h, :], lg[:, h, :], AF.Exp, bias=nmx[:, h:h + 1], accum=ssum[:, h:h + 1])
        k.tt(es, sink, mx, ALU.subtract)
        k.act(es, es, AF.Exp)
        k.tt(ssum, ssum, es, ALU.add)
        k.recip(ssum, ssum)
        o = bp.alloc(1024).rearrange("p (h d) -> p h d", h=16)
        pT = [bp.alloc(512).rearrange("p (a b t) -> p a b t", a=2, b=2) for _ in range(2)]
        for hp in range(8):
            bank = 4 + hp % 2
            for hh in range(2):
                for b in range(2):
                    k.tr(k.psb(bank)[:, (hh * 2 + b) * 128:(hh * 2 + b + 1) * 128], lg[:, hp * 2 + hh, b * 128:(b + 1) * 128])
            pt = pT[hp % 2]
            if hp % 2 == 0:
                k.cpa(pt.rearrange("p a b t -> p (a b t)"), k.psb(bank))
            else:
                k.cpv(pt.rearrange("p a b t -> p (a b t)"), k.psb(bank))
            for hh in range(2):
                h = hp * 2 + hh
                j = h // 4
                ob = k.psb(6 + (h // 8))[:, (h % 8) * 64:(h % 8 + 1) * 64]
                k.mm(ob, pt[:, hh, 0, :], V[prv][:, j * 64:(j + 1) * 64], start=True, stop=False)
                k.mm(ob, pt[:, hh, 1, :], V[cur][:, j * 64:(j + 1) * 64], start=False, stop=True)
        for hf in range(2):
            k.tt(o[:, hf * 8:(hf + 1) * 8, :], k.psb(6 + hf).rearrange("p (h d) -> p h d", h=8),
                 ssum[:, hf * 8:(hf + 1) * 8].unsqueeze(2).to_broadcast([128, 8, 64]), ALU.mult)
        oT = bp.alloc(1024).rearrange("p (kc t) -> p kc t", kc=8)
        of = o.rearrange("p h d -> p (h d)")
        for hf in range(2):
            for c in range(4):
                k.tr(k.psb(hf)[:, c * 128:(c + 1) * 128], of[:, (hf * 4 + c) * 128:(hf * 4 + c + 1) * 128])
            k.cpa(oT[:, hf * 4:(hf + 1) * 4, :].rearrange("p c t -> p (c t)"), k.psb(hf))
        for hf in range(2):
            for fc in range(8):
                k.mm(k.psb(2 + hf), oT[:, fc, :], w_out[:, fc, hf * 512:(hf + 1) * 512], start=(fc == 0), stop=(fc == 7))
        y = bp.alloc(1024)
        k.tt(y, k.ps[:, 2 * 512:4 * 512], bout, ALU.add)
        x3 = bp.alloc(1024)
        k.resid_ln(x3, x2, y, lng, lnb)
        k.ld(Xout[t], x3)
        bp.release()
    bp.release()
    phase_c_sample(k, Xin[NMAIN + 1], Xout[NMAIN + 1], w_in, w_out, bout, lng, lnb, rb)
    bp.release()


def phase_c_sample(k, Xin_tile, Xout_tile, w_in, w_out, bout, lng, lnb, rb):
    S, bp, nc = k.S, k.bp, k.nc
    din, dout = k.din, k.dout
    P = NS
    bp.mark()
    x2 = bp.alloc(1024)
    k.ld(x2, Xin_tile)
    O = bp.alloc(NS * 64).rearrange("p (s d) -> p s d", s=NS)
    bp.mark()
    xT = bp.alloc(1024).rearrange("p (kc t) -> p kc t", kc=8)
    for hf in range(2):
        for c in range(4):
            k.tr(k.psb(hf)[:, c * 128:(c + 1) * 128], x2[:, (hf * 4 + c) * 128:(hf * 4 + c + 1) * 128])
        k.cpa(xT[:, hf * 4:(hf + 1) * 4, :].rearrange("p c t -> p (c t)"), k.psb(hf))
    ball = bp.alloc(1536); k.ld(ball[0:P], din["b_in_c"].partition_broadcast(P))
    sinkc = bp.alloc(1); k.ld(sinkc[0:16], din["swa_sinks"].rearrange("(p o) -> p o", o=1))
    mdiag = bp.alloc(4); k.ld(mdiag[0:16], din["mdiag"])
    ohs = bp.alloc(128); k.ld(ohs[0:32], din["t5_onehot_s"])
    for b3 in range(3):
        for kc in range(8):
            k.mm(k.psb(2 + b3)[0:P, :], xT[:, kc, 0:P], w_in[:, kc, b3 * 512:(b3 + 1) * 512], start=(kc == 0), stop=(kc == 7))
    qkv = bp.alloc(1536)
    k.tt(qkv[0:P], k.ps[0:P, 2 * 512:5 * 512], ball[0:P], ALU.add)
    qn = bp.alloc(1024)
    for j2 in range(2):
        src = qkv[0:P, j2 * 512:(j2 + 1) * 512].rearrange("p (g jh d) -> p jh g d", g=4, jh=2)
        dst = qn[0:P, j2 * 512:(j2 + 1) * 512].rearrange("p (jh g d) -> p jh g d", jh=2, g=4)
        k.cpv(dst, src)
    KV = bp.alloc(NS * 512).rearrange("p (s c) -> p s c", s=NS)
    for s in range(NS):
        e = "sp" if s % 2 == 0 else "act"
        k.ld(KV[0:127, s, 0:256], din["cache_k"][s, 1:128, :], eng=e)
        k.ld(KV[0:127, s, 256:512], din["cache_v"][s, 1:128, :], eng=e)
        k.ld(KV[127:128, s, :], qkv[s:s + 1, 1024:1536], eng=e)
        k.ld(dout["s_swa_k"][s], KV[:, s, 0:256], eng=e)
        k.ld(dout["s_swa_v"][s], KV[:, s, 256:512], eng=e)
    Esel = bp.alloc(NS * 128).rearrange("p (s m) -> p s m", s=NS)
    k.tt(Esel[0:P], k.ident[0:P, 0:NS].unsqueeze(2).to_broadcast([P, NS, 128]),
         k.ones[0:P, :].unsqueeze(1).to_broadcast([P, NS, 128]), ALU.mult)
    LGT = bp.alloc(NS * 16).rearrange("p (s h) -> p s h", s=NS)
    prod = [bp.alloc(1024) for _ in range(2)]
    for s in range(NS):
        for hf in range(2):
            k.mm(k.psb(6 + hf), Esel[0:P, s, :], qn[0:P, hf * 512:(hf + 1) * 512])
        pr = prod[s % 2]
        k.tt(pr.rearrange("p (j g d) -> p j g d", j=4, g=4), k.ps[:, 6 * 512:8 * 512].rearrange("p (j g d) -> p j g d", j=4, g=4),
             KV[:, s, 0:256].rearrange("p (j d) -> p j d", j=4).unsqueeze(2).to_broadcast([128, 4, 4, 64]), ALU.mult)
        k.red(LGT[:, s, :], pr.rearrange("p (h d) -> p h d", h=16))
    LG = bp.alloc(NS * 128).rearrange("p (s r) -> p s r", s=NS)
    for g4 in range(4):
        for si in range(4):
            s = g4 * 4 + si
            k.tr(k.psb(2 + g4 % 2)[0:16, si * 128:(si + 1) * 128], LGT[:, s, :])
        k.cpv(LG[0:16, g4 * 4:(g4 + 1) * 4, :].rearrange("p s r -> p (s r)"), k.psb(2 + g4 % 2)[0:16, :])
    k.mm(k.psb(4)[0:16, 0:128], rb[0:32, :], ohs[0:32, :])
    Bs = bp.alloc(128)
    k.cpv(Bs[0:16], k.psb(4)[0:16, 0:128])
    k.stt(LG[0:16], LG[0:16], 0.125, Bs[0:16].unsqueeze(1).to_broadcast([16, NS, 128]), ALU.mult, ALU.add)
    mx = bp.alloc(NS); nmx = bp.alloc(NS); ssum = bp.alloc(NS); es = bp.alloc(NS)
    k.red(mx[0:16], LG[0:16], op=ALU.max)
    k.ts(mx[0:16], mx[0:16], sinkc[0:16], ALU.max)
    k.tt(LG[0:16], LG[0:16], mx[0:16].unsqueeze(2).to_broadcast([16, NS, 128]), ALU.subtract)
    k.act(LG[0:16].rearrange("p s r -> p (s r)"), LG[0:16].rearrange("p s r -> p (s r)"), AF.Exp)
    k.red(ssum[0:16], LG[0:16])
    k.ts(es[0:16], mx[0:16], -1.0, ALU.mult, sinkc[0:16], ALU.add)
    k.act(es[0:16], es[0:16], AF.Exp)
    k.tt(ssum[0:16], ssum[0:16], es[0:16], ALU.add)
    k.recip(ssum[0:16], ssum[0:16])
    k.tt(LG[0:16], LG[0:16], ssum[0:16].unsqueeze(2).to_broadcast([16, NS, 128]), ALU.mult)
    PT = bp.alloc(NS * 16).rearrange("p (s h) -> p s h", s=NS)
    for s in range(NS):
        k.tr(k.psb(5)[:, s * 16:(s + 1) * 16], LG[0:16, s, :])
    k.cpv(PT.rearrange("p s h -> p (s h)"), k.psb(5)[:, 0:NS * 16])
    pm = bp.alloc(8 * 256)
    for half in range(2):
        for si in range(8):
            s = half * 8 + si
            k.mm(k.ps[0:16, si * 256:(si + 1) * 256], PT[:, s, :], KV[:, s, 256:512])
        k.tt(pm[0:16].rearrange("p (s j d) -> p s j d", s=8, j=4), k.ps[0:16, 0:2048].rearrange("p (s j d) -> p s j d", s=8, j=4),
             mdiag[0:16].unsqueeze(1).unsqueeze(3).to_broadcast([16, 8, 4, 64]), ALU.mult)
        k.red(O[0:16, half * 8:(half + 1) * 8, :], pm[0:16].rearrange("p (s j d) -> p s d j", s=8, j=4))
    bp.release()
    Oscr = k.scratch("Oscr", [16, NS, 64])
    k.ld(Oscr, O[0:16])
    otok = bp.alloc(1024)
    k.memset(otok, 0.0)
    k.ld(otok[0:P].rearrange("p (h d) -> p h d", h=16), Oscr.rearrange("h s d -> s h d"))
    oT = bp.alloc(1024).rearrange("p (kc t) -> p kc t", kc=8)
    for hf in range(2):
        for c in range(4):
            k.tr(k.psb(hf)[:, c * 128:(c + 1) * 128], otok[:, (hf * 4 + c) * 128:(hf * 4 + c + 1) * 128])
        k.cpa(oT[:, hf * 4:(hf + 1) * 4, :].rearrange("p c t -> p (c t)"), k.psb(hf))
    for hf in range(2):
        for fc in range(8):
            k.mm(k.psb(2 + hf)[0:P, :], oT[:, fc, 0:P], w_out[:, fc, hf * 512:(hf + 1) * 512], start=(fc == 0), stop=(fc == 7))
    y = bp.alloc(1024)
    k.tt(y[0:P], k.ps[0:P, 2 * 512:4 * 512], bout[0:P], ALU.add)
    x3 = bp.alloc(1024)
    k.memset(x3, 0.0)
    k.resid_ln(x3[0:P], x2[0:P], y[0:P], lng, lnb, P=P)
    k.ld(Xout_tile, x3)
    bp.release()


def build(phases=("a", "b", "c", "d"), debug=()):
    k = K(debug=set(debug))
    import os
    k.npeer = int(os.environ.get("KNP", NMAIN + 2))
    nc = k.nc
    inp, out = k.inp, k.out
    inp("xT", [NT, 128, 8, 128]); inp("xtok", [NMAIN + 1, 128, 1024])
    inp("pflag", [128, 1])
    inp("ident", [128, 128]); inp("ones", [128, 128]); inp("triLE", [128, 128]); inp("maskSL", [128, 128])
    inp("w_in_ab", [1024, ABC]); inp("w_out_ab", [1024, 1024])
    inp("conv_w", [128, 16, 4]); inp("lru_cb", [128, 4]); inp("lru_b", [128, 8]); inp("lru_lam", [128, 4])
    inp("lru_wbd", [128, 8, 128]); inp("gdn_dt_bias", [4]); inp("gdn_a_log", [4]); inp("gdn_norm_w", [128])
    inp("ln_mix_g", [2, 1024]); inp("ln_mix_b", [2, 1024]); inp("ln_ffn_g", [2, 1024]); inp("ln_ffn_b", [2, 1024])
    inp("xsT", [128, 8, NS]); inp("xstok", [128, 1024]); inp("convhT", [128, 16, 3, NS]); inp("lru_h0T", [128, 4, NS])
    inp("s_gdn_in", [NS, 4, 128, 128]); inp("gconv_nat", [NS, 3, 1536]); inp("lconv_nat", [NS, 3, 512])
    out("s_gdn", [NS, 4, 128, 128]); out("s_gdn_conv", [NS, 3, 1536]); out("s_lru", [NS, 512]); out("s_lru_conv", [NS, 3, 512])
    out("p_gdn", [4, 128, 128]); out("p_gdn_conv", [3, 1536]); out("p_lru", [4, 128]); out("p_lru_conv", [3, 512])
    if "b" in phases or "d" in phases:
        inp("peer_w_q", [2, 1024, 2048]); inp("peer_keysT", [2, 128, 16, 128]); inp("iota16", [16]); inp("ssel", [128, 128])
        for l_ in range(2):
            inp("peer_u%d" % l_, [16384, 1024]); inp("peer_v%d" % l_, [16384, 1024])
    inp("w_in_c", [1024, 1536]); inp("w_out_c", [1024, 1024]); inp("b_in_qk", [128, 10]); inp("b_in_c", [1536])
    inp("b_out_c", [1024]); inp("swa_sinks", [16]); inp("hmask", [128, 128]); inp("rel_bias", [32, 16])
    inp("t5_onehot", [33, 128 * 256])
    out("p_swa_k", [128, 256]); out("p_swa_v", [128, 256])
    inp("cache_k", [NS, 128, 256]); inp("cache_v", [NS, 128, 256]); inp("mdiag", [16, 4]); inp("t5_onehot_s", [32, 128])
    out("s_swa_k", [NS, 128, 256]); out("s_swa_v", [NS, 128, 256])
    NX = NMAIN + 2
    mk = lambda n: (out(n, [NX, 128, 1024]) if n.lower() in k.debug else k.scratch(n, [NX, 128, 1024]))
    k.X1 = mk("X1"); k.X2 = mk("X2"); k.X3 = mk("X3"); k.X4 = mk("X4")
    with ExitStack() as st:
        big = st.enter_context(nc.sbuf_tensor("big", [128, SBCOLS], F32))
        k.ps = st.enter_context(nc.psum_tensor("ps", [128, 4096], F32))
        k.S = Sched(nc)
        k.S.alloc_sems(st)
        k.bp = Bump(big, SBCOLS)
        bp = k.bp
        k.ident = bp.alloc(128); k.ld(k.ident, k.din["ident"])
        k.ones = bp.alloc(128); k.ld(k.ones, k.din["ones"])
        k.triLE = bp.alloc(128); k.ld(k.triLE, k.din["triLE"])
        k.maskSL = bp.alloc(128); k.ld(k.maskSL, k.din["maskSL"])
        k.one_c = bp.alloc(1); k.memset(k.one_c, 1.0)
        k.eps6 = bp.alloc(1); k.memset(k.eps6, 1e-6)
        k.c128e6 = bp.alloc(1); k.memset(k.c128e6, 128e-6)
        k.eps_ln = bp.alloc(1); k.memset(k.eps_ln, LN_EPS)
        if "a" in phases:
            phase_a(k)
        if "b" in phases:
            phase_peer(k, 0, k.X1, lambda t: k.X2[t], k.npeer)
        if "c" in phases:
            phase_c(k, k.X1 if "ctest" in k.debug else k.X2, k.X3)
        if "d" in phases:
            y_p = out("y_p", [NMAIN, 128, 1024]); y_s = out("y_s", [128, 1024])
            xin = [k.X3[t + 1] for t in range(NMAIN + 1)]
            phase_peer(k, 1, xin, lambda t: (y_p[t] if t < NMAIN else y_s), min(k.npeer, NMAIN + 1))
        k.S.finish()
        print("instructions", k.S.n_ins, "waits", k.S.n_wait, "sbuf peak", bp.peak, k.S.count)
    return k


def host_prep(inputs):
    f = lambda a: np.ascontiguousarray(a, dtype=np.float32)
    xp = inputs["x_prompt"]
    common = {}
    p = np.arange(128)
    common["ident"] = np.eye(128, dtype=np.float32)
    common["ones"] = np.ones((128, 128), np.float32)
    common["triLE"] = (p[:, None] <= p[None, :]).astype(np.float32)
    common["maskSL"] = (p[None, :] < p[:, None]).astype(np.float32)
    common["w_in_ab"] = f(inputs["w_in_ab"][0]); common["w_out_ab"] = f(inputs["w_out_ab"][0])
    cwg = inputs["gdn_conv_w"][0].T.reshape(12, 128, 4)
    cwl = inputs["lru_conv_w"][0].T.reshape(4, 128, 4)
    common["conv_w"] = f(np.concatenate([cwg, cwl], 0).transpose(1, 0, 2))
    common["lru_cb"] = f(inputs["lru_conv_b"][0].reshape(4, 128).T)
    common["lru_b"] = f(np.concatenate([inputs["lru_b_r"][0].reshape(4, 128), inputs["lru_b_i"][0].reshape(4, 128)], 0).T)
    common["lru_lam"] = f(inputs["lru_lam"][0].reshape(4, 128).T)
    wbd = np.zeros((128, 8, 128), np.float32)
    for gi, w in enumerate([inputs["lru_w_r"][0], inputs["lru_w_i"][0]]):
        for n in range(8):
            c, nl = n // 2, n % 2
            wbd[nl * 64:(nl + 1) * 64, gi * 4 + c, nl * 64:(nl + 1) * 64] = w[n]
    common["lru_wbd"] = wbd
    common["gdn_dt_bias"] = f(inputs["gdn_dt_bias"][0]); common["gdn_a_log"] = f(inputs["gdn_a_log"][0])
    common["gdn_norm_w"] = f(inputs["gdn_norm_w"][0])
    for n in ("ln_mix_g", "ln_mix_b", "ln_ffn_g", "ln_ffn_b", "peer_w_q"):
        common[n] = f(inputs[n])
    for l_ in range(2):
        common["peer_u%d" % l_] = f(inputs["peer_u"][l_]); common["peer_v%d" % l_] = f(inputs["peer_v"][l_])
    common["peer_keysT"] = f(inputs["peer_keys"].transpose(0, 4, 1, 2, 3).reshape(2, 128, 16, 128))
    common["iota16"] = np.arange(16, dtype=np.float32)
    common["ssel"] = f((np.arange(128)[:, None] // 8) == np.arange(128)[None, :])
    wc = inputs["w_in_c"][0]; bc = inputs["b_in_c"][0]
    order = []
    for m in range(8):
        for half in range(2):
            h = (m // 4) * 8 + half * 4 + m % 4
            order.extend(range(h * 64, (h + 1) * 64))
    order = np.array(order)
    wcp = np.concatenate([wc[:, order], wc[:, 1024:]], 1)
    bcp = np.concatenate([bc[order], bc[1024:]])
    common["w_in_c"] = f(wcp); common["b_in_c"] = f(bcp)
    common["b_in_qk"] = f(bcp[:1280].reshape(10, 128).T)
    common["w_out_c"] = f(inputs["w_out_c"][0]); common["b_out_c"] = f(inputs["b_out_c"][0])
    common["swa_sinks"] = f(inputs["swa_sinks"][0]); common["rel_bias"] = f(inputs["rel_bias"])
    q = np.arange(128)[:, None]; kk = np.arange(256)[None, :]
    rel = q + 128 - kk
    valid = (rel >= 0) & (rel < 128)
    n = np.maximum(rel, 0); nf = np.maximum(n, 1).astype(np.float32)
    large = 16 + (np.log(nf / 16) / np.float32(np.log(128 / 16)) * 16).astype(np.int32)
    bucket = np.where(n < 16, n, np.minimum(large, 31))
    bucket = np.where(valid, bucket, 32)
    rels = 127 - np.arange(128)
    ns_ = np.maximum(rels, 0); nfs = np.maximum(ns_, 1).astype(np.float32)
    larges = 16 + (np.log(nfs / 16) / np.float32(np.log(128 / 16)) * 16).astype(np.int32)
    bks = np.where(ns_ < 16, ns_, np.minimum(larges, 31))
    common["t5_onehot_s"] = f(bks[None, :] == np.arange(32)[:, None])
    common["mdiag"] = f(np.arange(16)[:, None] // 4 == np.arange(4)[None, :])
    common["t5_onehot"] = f((bucket[None] == np.arange(33)[:, None, None]).reshape(33, -1))
    maps = []
    for c in range(8):
        b, half = c // 2, c % 2
        m = dict(common)
        xs = np.zeros((NT * 128, 1024), np.float32)
        if half == 1:
            xs[:] = xp[b]
        else:
            xs[NPRE * 128:] = xp[b, :NMAIN * 128]
        m["xT"] = f(xs.reshape(NT, 128, 8, 128).transpose(0, 3, 2, 1))
        m["xtok"] = f(xs[(NPRE - 1) * 128:].reshape(NMAIN + 1, 128, 1024))
        m["pflag"] = np.full((128, 1), float(half), np.float32)
        m["cache_k"] = f(inputs["cache_swa_k"][0, c * NS:(c + 1) * NS].reshape(NS, 128, 256))
        m["cache_v"] = f(inputs["cache_swa_v"][0, c * NS:(c + 1) * NS].reshape(NS, 128, 256))
        sl = slice(c * NS, (c + 1) * NS)
        xs_ = inputs["x_sample"][sl, 0, :]
        m["xsT"] = f(xs_.reshape(NS, 8, 128).transpose(2, 1, 0))
        xst = np.zeros((128, 1024), np.float32); xst[:NS] = xs_
        m["xstok"] = xst
        gcv = inputs["state_gdn_conv"][0, sl]; lcv = inputs["state_lru_conv"][0, sl]
        m["gconv_nat"] = f(gcv); m["lconv_nat"] = f(lcv)
        hT = np.concatenate([gcv.reshape(NS, 3, 12, 128), lcv.reshape(NS, 3, 4, 128)], 2)
        m["convhT"] = f(hT.transpose(3, 2, 1, 0))
        m["lru_h0T"] = f(inputs["state_lru"][0, sl].reshape(NS, 4, 128).transpose(2, 1, 0))
        m["s_gdn_in"] = f(inputs["state_gdn"][0, sl])
        m["hmask"] = np.full((128, 128), 0.0 if half == 1 else -30000.0, np.float32)
        maps.append(m)
    return maps


_CACHE = {}


def run(inputs, phases=("a", "b", "c", "d"), debug=()):
    key = (tuple(phases), tuple(debug))
    if key not in _CACHE:
        _CACHE[key] = build(phases, debug)
    k = _CACHE[key]
    maps = host_prep(inputs)
    maps = [{n: m[n] for n in k.din} for m in maps]
    res = run_bass_kernel_spmd(k.nc, maps, core_ids=list(range(8)))
    return res.results


def kernel(**inputs):
    inputs = {n: np.asarray(v) for n, v in inputs.items()}
    r = run(inputs)
    f32 = np.float32
    y_prompt = np.stack([np.concatenate([r[2 * b]["y_p"].reshape(2048, 1024), r[2 * b + 1]["y_p"].reshape(2048, 1024)], 0)
                         for b in range(4)], 0).astype(f32)
    y_sample = np.concatenate([r[c]["y_s"][:NS] for c in range(8)], 0).reshape(128, 1, 1024).astype(f32)
    odd = [r[2 * b + 1] for b in range(4)]
    p_gdn = np.stack([o["p_gdn"] for o in odd], 0)[None].astype(f32)
    p_gdn_conv = np.stack([o["p_gdn_conv"] for o in odd], 0)[None].astype(f32)
    p_lru = np.stack([o["p_lru"].reshape(512) for o in odd], 0)[None].astype(f32)
    p_lru_conv = np.stack([o["p_lru_conv"] for o in odd], 0)[None].astype(f32)
    p_k = np.stack([o["p_swa_k"].reshape(128, 4, 64) for o in odd], 0)[None].astype(f32)
    p_v = np.stack([o["p_swa_v"].reshape(128, 4, 64) for o in odd], 0)[None].astype(f32)
    cat = lambda n, shp: np.concatenate([r[c][n] for c in range(8)], 0).reshape(shp)[None].astype(f32)
    s_gdn = cat("s_gdn", (128, 4, 128, 128))
    s_gdn_conv = cat("s_gdn_conv", (128, 3, 1536))
    s_lru = cat("s_lru", (128, 512))
    s_lru_conv = cat("s_lru_conv", (128, 3, 512))
    s_k = cat("s_swa_k", (128, 128, 4, 64))
    s_v = cat("s_swa_v", (128, 128, 4, 64))
    return (y_prompt, y_sample, p_gdn, p_gdn_conv, p_lru, p_lru_conv, p_k, p_v,
            s_gdn, s_gdn_conv, s_lru, s_lru_conv, s_k, s_v)
```

```python
import numpy as np
from contextlib import ExitStack
import concourse.bass as bass
import concourse.mybir as mybir
from concourse.bass_utils import run_bass_kernel_spmd

F32 = mybir.dt.float32
U32 = mybir.dt.uint32
I32 = mybir.dt.int32
AF = mybir.ActivationFunctionType
ALU = mybir.AluOpType
AX = mybir.AxisListType

EPOCH = 30000


def _rng(ap):
    sp = str(ap.space)
    a = ap.ap
    off = ap.offset
    if sp in ("SB", "PSUM"):
        row = a[0][0]
        lo = off % row if row > 0 else off
        ext = 0
        for s, c in a[1:]:
            ext += abs(s) * (c - 1)
        hi = lo + ext + 1
        if sp == "PSUM":
            lo = (lo // 512) * 512
            hi = -(-hi // 512) * 512
        return (sp, lo, hi)
    ext = 0
    for s, c in a:
        ext += abs(s) * (c - 1)
    return ("D:" + ap.name, off, off + ext + 1)


class Sched:
    ENG = ("sp", "act", "dve", "pool", "pe")

    def __init__(self, nc, ring=32, n_epochs=4):
        self.nc = nc
        self.eng = {"sp": nc.sync, "act": nc.scalar, "dve": nc.vector,
                    "pool": nc.gpsimd, "pe": nc.tensor}
        self.count = {e: 0 for e in self.ENG}
        self.esems = {e: [] for e in self.ENG}
        self.n_epochs = n_epochs
        self.known = {e: {} for e in self.ENG}
        self.segs = {}
        self.ring = ring
        self.ring_sems = []
        self.ring_cnt = [0] * ring
        self.ring_next = 0
        self.n_wait = 0
        self.n_ins = 0
        self.defer = None

    def alloc_sems(self, stack):
        nc = self.nc
        for e in self.ENG:
            if e == "sp":
                continue
            for i in range(self.n_epochs):
                self.esems[e].append(stack.enter_context(nc.semaphore(f"s_{e}{i}")))
        for i in range(self.ring):
            self.ring_sems.append(stack.enter_context(nc.semaphore(f"s_dma{i}")))

    def _need(self, engine, ev, waits):
        if ev is None:
            return
        if ev[0] == "eng":
            _, f, n = ev
            kk = ("eng", f)
            if self.known[engine].get(kk, 0) >= n:
                return
            self.known[engine][kk] = n
            ep = (n - 1) // EPOCH
            waits.append((self.esems[f][ep], n - ep * EPOCH))
        else:
            _, slot, k = ev
            kk = ("dma", slot)
            if self.known[engine].get(kk, 0) >= k:
                return
            self.known[engine][kk] = k
            waits.append((self.ring_sems[slot], 16 * k))

    def _collect(self, engine, ins, outs):
        waits = []
        for ap in ins:
            sp, lo, hi = _rng(ap)
            for seg in self.segs.get(sp, ()):
                if seg[0] < hi and lo < seg[1]:
                    self._need(engine, seg[2], waits)
        for ap in outs:
            sp, lo, hi = _rng(ap)
            for seg in self.segs.get(sp, ()):
                if seg[0] < hi and lo < seg[1]:
                    ev = seg[2]
                    if not (engine == "pe" and ev is not None and ev[0] == "eng" and ev[1] == "pe"):
                        self._need(engine, ev, waits)
                    for f, n in seg[3].items():
                        self._need(engine, ("eng", f, n), waits)
                    for sl_, k_ in seg[4].items():
                        self._need(engine, ("dma", sl_, k_), waits)
        return waits

    def _update(self, ev, ins, outs):
        for ap in ins:
            sp, lo, hi = _rng(ap)
            found = False
            for seg in self.segs.setdefault(sp, []):
                if seg[0] < hi and lo < seg[1]:
                    found = True
                    if ev[0] == "eng":
                        seg[3][ev[1]] = ev[2]
                    else:
                        seg[4][ev[1]] = ev[2]
            if not found:
                seg = [lo, hi, None, {}, {}]
                if ev[0] == "eng":
                    seg[3][ev[1]] = ev[2]
                else:
                    seg[4][ev[1]] = ev[2]
                self.segs[sp].append(seg)
        for ap in outs:
            sp, lo, hi = _rng(ap)
            lst = self.segs.setdefault(sp, [])
            keep = []
            for seg in lst:
                if seg[1] <= lo or hi <= seg[0]:
                    keep.append(seg)
                    continue
                if seg[0] < lo:
                    keep.append([seg[0], lo, seg[2], dict(seg[3]), dict(seg[4])])
                if hi < seg[1]:
                    keep.append([hi, seg[1], seg[2], dict(seg[3]), dict(seg[4])])
            keep.append([lo, hi, ev, {}, {}])
            self.segs[sp] = keep

    @staticmethod
    def _split(ins, outs):
        pin = [a for a in ins if str(a.space) == "PSUM"]
        if pin:
            ins = [a for a in ins if str(a.space) != "PSUM"]
            outs = list(outs) + pin
        return ins, outs

    def op(self, engine, fn, ins=(), outs=()):
        if self.defer is not None:
            self.defer.append(lambda: self.op(engine, fn, ins, outs))
            return None
        ins, outs = self._split(ins, outs)
        e = self.eng[engine]
        for s, v in self._collect(engine, ins, outs):
            e.wait_ge(s, v)
            self.n_wait += 1
        i = fn(e)
        self.count[engine] += 1
        n = self.count[engine]
        ep = (n - 1) // EPOCH
        assert ep < self.n_epochs, f"too many instructions on {engine}"
        i.then_inc(self.esems[engine][ep], 1)
        self.n_ins += 1
        self._update(("eng", engine, n), ins, outs)
        return i

    def dma(self, engine, fn, ins=(), outs=()):
        if self.defer is not None:
            self.defer.append(lambda: self.dma(engine, fn, ins, outs))
            return None
        e = self.eng[engine]
        slot = self.ring_next
        self.ring_next = (self.ring_next + 1) % self.ring
        k = self.ring_cnt[slot] + 1
        assert 16 * k < 60000
        self.ring_cnt[slot] = k
        waits = self._collect(engine, ins, outs)
        if k > 1:
            self._need(engine, ("dma", slot, k - 1), waits)
        for s, v in waits:
            e.wait_ge(s, v)
            self.n_wait += 1
        i = fn(e)
        i.then_inc(self.ring_sems[slot], 16)
        self.n_ins += 1
        self._update(("dma", slot, k), ins, outs)
        return i

    def finish(self):
        waits = []
        for slot in range(self.ring):
            if self.ring_cnt[slot]:
                self._need("sp", ("dma", slot, self.ring_cnt[slot]), waits)
        for f in self.ENG:
            if f != "sp" and self.count[f]:
                self._need("sp", ("eng", f, self.count[f]), waits)
        for s, v in waits:
            self.nc.sync.wait_ge(s, v)


class Bump:
    def __init__(self, big, total):
        self.big = big
        self.total = total
        self.off = 0
        self.marks = []
        self.peak = 0

    def alloc(self, n, dtype=None, parts=128):
        assert self.off + n <= self.total, f"SBUF overflow {self.off}+{n}>{self.total}"
        ap = self.big[0:parts, self.off:self.off + n]
        self.off += n
        self.peak = max(self.peak, self.off)
        if dtype is not None:
            ap = ap.bitcast(dtype)
        return ap

    def mark(self):
        self.marks.append(self.off)

    def release(self):
        self.off = self.marks.pop()


D = 1024
NPRE = 16
NMAIN = 16
NT = NPRE + NMAIN
NS = 16
ABC = 3080
ALPHA = 4.0 ** 0.25
LN_EPS = 1e-5
SBCOLS = 53000


class StopTile(Exception):
    pass


def ckpt(n):
    import os
    if int(os.environ.get('KSTOP', 10**9)) <= n:
        raise StopTile()


class K:
    def __init__(self, debug=None):
        self.debug = debug or set()
        self.nc = bass.Bass("TRN2", target_bir_lowering=False)
        self.din = {}
        self.dout = {}

    def inp(self, name, shape, dt=F32):
        t = self.nc.dram_tensor(name, list(shape), dt, kind="ExternalInput").ap()
        self.din[name] = t
        return t

    def out(self, name, shape, dt=F32):
        t = self.nc.dram_tensor(name, list(shape), dt, kind="ExternalOutput").ap()
        self.dout[name] = t
        return t

    def scratch(self, name, shape, dt=F32):
        return self.nc.dram_tensor(name, list(shape), dt).ap()

    def mm(self, out, lhsT, rhs, start=True, stop=True):
        return self.S.op("pe", lambda e: e.matmul(out, lhsT=lhsT, rhs=rhs, start=start, stop=stop),
                         ins=[lhsT, rhs], outs=[out])

    def tr(self, out, in_):
        idt = self.ident[0:in_.shape[0], 0:in_.shape[0]]
        return self.S.op("pe", lambda e: e.transpose(out, in_, idt), ins=[in_, idt], outs=[out])

    def act(self, out, in_, func, bias=None, scale=1.0, accum=None, eng="act"):
        ins = [in_]
        kw = {}
        if bias is not None:
            kw["bias"] = bias
            if not isinstance(bias, (int, float)):
                ins.append(bias)
        if not isinstance(scale, (int, float)):
            ins.append(scale)
        outs = [out]
        if accum is not None:
            kw["accum_out"] = accum
            outs.append(accum)
        return self.S.op("act", lambda e: e.activation(out=out, in_=in_, func=func, scale=scale, **kw),
                         ins=ins, outs=outs)

    def cpa(self, out, in_):
        return self.S.op("act", lambda e: e.copy(out=out, in_=in_), ins=[in_], outs=[out])

    def cpv(self, out, in_, eng="dve"):
        return self.S.op(eng, lambda e: e.tensor_copy(out=out, in_=in_), ins=[in_], outs=[out])

    def tt(self, out, in0, in1, op, eng="dve"):
        return self.S.op(eng, lambda e: e.tensor_tensor(out=out, in0=in0, in1=in1, op=op),
                         ins=[in0, in1], outs=[out])

    def ts(self, out, in0, s1, op0, s2=None, op1=None, eng="dve", accum=None):
        ins = [in0]
        for s in (s1, s2):
            if s is not None and not isinstance(s, (int, float)):
                ins.append(s)
        kw = {}
        if op1 is not None:
            kw["op1"] = op1
        outs = [out]
        if accum is not None:
            kw["accum_out"] = accum
            outs.append(accum)
        return self.S.op(eng, lambda e: e.tensor_scalar(out=out, in0=in0, scalar1=s1, scalar2=s2, op0=op0, **kw),
                         ins=ins, outs=outs)

    def stt(self, out, in0, scalar, in1, op0, op1, accum=None):
        ins = [in0, in1]
        if not isinstance(scalar, (int, float)):
            ins.append(scalar)
        outs = [out]
        kw = {}
        if accum is not None:
            kw["accum_out"] = accum
            outs.append(accum)
        return self.S.op("dve", lambda e: e.scalar_tensor_tensor(out=out, in0=in0, scalar=scalar, in1=in1,
                                                                  op0=op0, op1=op1, **kw), ins=ins, outs=outs)

    def red(self, out, in_, op=ALU.add, axis=AX.X):
        return self.S.op("dve", lambda e: e.tensor_reduce(out=out, in_=in_, axis=axis, op=op), ins=[in_], outs=[out])

    def recip(self, out, in_):
        return self.S.op("dve", lambda e: e.reciprocal(out=out, in_=in_), ins=[in_], outs=[out])

    def memset(self, ap, v, eng="dve"):
        return self.S.op(eng, lambda e: e.memset(ap, v), outs=[ap])

    def ld(self, out, in_, eng="sp"):
        return self.S.dma(eng, lambda e: e.dma_start(out=out, in_=in_), ins=[in_], outs=[out])

    def psb(self, b, n=512, parts=128):
        return self.ps[0:parts, b * 512:b * 512 + n]

    def dbg(self, name, ap_sb, shape):
        if name in self.debug:
            o = self.out("dbg_" + name, shape, ap_sb.dtype)
            self.ld(o, ap_sb)

    def resid_ln(self, out, x_tok, y, g_bc, b_bc, P=128):
        bp = self.bp
        bp.mark()
        xa = bp.alloc(1024)[0:P]
        st = bp.alloc(12)[0:P]
        mv = bp.alloc(2)[0:P]
        rs = bp.alloc(1)[0:P]
        self.stt(xa, x_tok, ALPHA, y, ALU.mult, ALU.add)
        self.S.op("dve", lambda e: e.bn_stats(out=st[:, 0:6], in_=xa[:, 0:512]), ins=[xa[:, 0:512]], outs=[st[:, 0:6]])
        self.S.op("dve", lambda e: e.bn_stats(out=st[:, 6:12], in_=xa[:, 512:1024]), ins=[xa[:, 512:1024]], outs=[st[:, 6:12]])
        self.S.op("dve", lambda e: e.bn_aggr(out=mv, in_=st), ins=[st], outs=[mv])
        self.act(rs, mv[:, 1:2], AF.Sqrt, bias=self.eps_ln[0:P], scale=1.0)
        self.recip(rs, rs)
        self.ts(xa, xa, mv[:, 0:1], ALU.subtract, rs, ALU.mult)
        self.tt(xa, xa, g_bc[0:P], ALU.mult)
        self.tt(out, xa, b_bc[0:P], ALU.add)
        bp.release()

    def gelu_tanh(self, out, x, n, P=128):
        bp = self.bp
        bp.mark()
        t = bp.alloc(n)[0:P]
        if len(x.shape) == 3:
            t = t.rearrange("p (a b) -> p a b", a=x.shape[1])
        self.tt(t, x, x, ALU.mult)
        self.ts(t, t, 0.044715, ALU.mult, 1.0, ALU.add)
        self.tt(t, t, x, ALU.mult)
        self.act(t, t, AF.Sigmoid, scale=1.5957691216057308)
        self.tt(out, t, x, ALU.mult)
        bp.release()


def phase_a(k):
    S, bp, nc = k.S, k.bp, k.nc
    din = k.din
    bp.mark()
    w_in = bp.alloc(8 * ABC).rearrange("p (kc c) -> p kc c", kc=8)
    w_in_d = din["w_in_ab"].rearrange("(kc p) c -> p kc c", p=128)
    for kc in range(8):
        k.ld(w_in[:, kc, :], w_in_d[:, kc, :], eng="sp" if kc % 2 == 0 else "act")
    w_oh = bp.alloc(8 * 512).rearrange("p (kc c) -> p kc c", kc=8)
    w_out_d = din["w_out_ab"].rearrange("(kc p) c -> p kc c", p=128)
    cw = bp.alloc(16 * 4).rearrange("p (c i) -> p c i", c=16)
    k.ld(cw, din["conv_w"])
    cb = bp.alloc(4)
    k.ld(cb, din["lru_cb"])
    lb = bp.alloc(8)
    k.ld(lb, din["lru_b"])
    lam = bp.alloc(4)
    k.ld(lam, din["lru_lam"])
    wbd = bp.alloc(8 * 128).rearrange("p (g c) -> p g c", g=8)
    k.ld(wbd, din["lru_wbd"])
    dtb = bp.alloc(4)
    k.ld(dtb, din["gdn_dt_bias"].partition_broadcast(128))
    nA = bp.alloc(4)
    k.ld(nA, din["gdn_a_log"].partition_broadcast(128))
    k.act(nA, nA, AF.Exp)
    k.ts(nA, nA, -1.0, ALU.mult)
    nw = bp.alloc(128)
    k.ld(nw, din["gdn_norm_w"].partition_broadcast(128))
    lng = bp.alloc(1024)
    k.ld(lng, din["ln_mix_g"][0].partition_broadcast(128))
    lnb = bp.alloc(1024)
    k.ld(lnb, din["ln_mix_b"][0].partition_broadcast(128))
    pflag = bp.alloc(1)
    k.ld(pflag, din["pflag"])
    m8sp = bp.alloc(4)
    k.act(m8sp, lam, AF.Exp, scale=-1.0)
    k.act(m8sp, m8sp, AF.Ln, bias=k.one_c, scale=1.0)
    k.ts(m8sp, m8sp, -8.0, ALU.mult)

    bp.mark()
    Sst = bp.alloc(512).rearrange("p (h d) -> p h d", h=4)
    k.memset(Sst, 0.0)
    pc = bp.alloc(16 * 131).rearrange("p (c t) -> p c t", c=16)
    k.memset(pc, 0.0)
    hst = bp.alloc(4)
    k.memset(hst, 0.0)
    xT = [bp.alloc(1024).rearrange("p (kc t) -> p kc t", kc=8) for _ in range(2)]
    xtok = [bp.alloc(1024) for _ in range(1)]

    import os
    for t in range(int(os.environ.get('KT0', 0)), int(os.environ.get('KT1', NT))):
        _off0, _nm0 = bp.off, len(bp.marks)
        try:
            full = t >= NPRE - 1
            need_q = t >= NPRE - 2
            bp.mark()
            xTt = xT[t % 2]
            k.ld(xTt, din["xT"][t], eng="sp")
            if full:
                k.ld(xtok[0], din["xtok"][t - (NPRE - 1)], eng="act")
            if t == NPRE:
                k.ts(hst, hst, pflag, ALU.mult)
            hseq = bp.alloc(512).rearrange("p (c t) -> p c t", c=4)
            o = bp.alloc(512).rearrange("p (h d) -> p h d", h=4)
            bp.mark()
            colbase = [c * 128 for c in range(12)] + [2056 + c * 128 for c in range(4)]
            for g4 in range(4):
                if g4 == 0 and not need_q:
                    continue
                for ci in range(4):
                    c = g4 * 4 + ci
                    for kc in range(8):
                        k.mm(k.psb(g4 % 2)[:, ci * 128:(ci + 1) * 128], w_in[:, kc, colbase[c]:colbase[c] + 128],
                             xTt[:, kc, :], start=(kc == 0), stop=(kc == 7))
                k.cpa(pc[:, g4 * 4:(g4 + 1) * 4, 3:131], k.psb(g4 % 2).rearrange("p (c t) -> p c t", c=4))
            ckpt(1)
            cv = bp.alloc(16 * 128).rearrange("p (c t) -> p c t", c=16)
            for c in range(16):
                if c < 4 and not need_q:
                    continue
                if c < 12:
                    k.ts(cv[:, c, :], pc[:, c, 0:128], cw[:, c, 0:1], ALU.mult)
                else:
                    k.ts(cv[:, c, :], pc[:, c, 0:128], cw[:, c, 0:1], ALU.mult, cb[:, c - 12:c - 11], ALU.add)
                for i in range(1, 4):
                    k.stt(cv[:, c, :], pc[:, c, i:i + 128], cw[:, c, i:i + 1], cv[:, c, :], ALU.mult, ALU.add)
            ckpt(2)
            k.cpv(pc[:, :, 0:3], pc[:, :, 128:131], eng="pool")
            if t == NT - 1:
                for c in range(16):
                    k.tr(k.ps[0:3, 4 * 512 + c * 128:4 * 512 + (c + 1) * 128], pc[:, c, 128:131])
                bp.mark()
                cst = bp.alloc(2048)
                k.cpv(cst[0:3, :], k.ps[0:3, 4 * 512:8 * 512])
                k.ld(k.dout["p_gdn_conv"], cst[0:3, 0:1536])
                k.ld(k.dout["p_lru_conv"], cst[0:3, 1536:2048])
                bp.release()
            ckpt(3)
            qlo = 0 if need_q else 4
            qkv = cv[:, qlo:12, :]
            k.act(qkv, qkv, AF.Silu)
            ckpt(4)
            qkn = bp.alloc(1024).rearrange("p (c t) -> p c t", c=8)
            bp.mark()
            sq = bp.alloc(1024).rearrange("p (c t) -> p c t", c=8)
            k.act(sq[:, qlo:8, :], cv[:, qlo:8, :], AF.Square)
            for hh in range(qlo // 4, 2):
                k.mm(k.psb(2 + hh), k.ones, sq[:, hh * 4:(hh + 1) * 4, :].rearrange("p c t -> p (c t)"))
            rs = bp.alloc(1024).rearrange("p (c t) -> p c t", c=8)
            if need_q:
                k.act(rs[:, 0:4, :].rearrange("p c t -> p (c t)"), k.psb(2), AF.Sqrt, bias=k.c128e6, scale=128.0)
            k.act(rs[:, 4:8, :].rearrange("p c t -> p (c t)"), k.psb(3), AF.Sqrt, bias=k.eps6, scale=1.0)
            k.recip(rs[:, qlo:8, :], rs[:, qlo:8, :])
            k.tt(qkn[:, qlo:8, :], cv[:, qlo:8, :], rs[:, qlo:8, :], ALU.mult)
            bp.release()
            ckpt(5)
            for kc in range(8):
                k.mm(k.ps[:, 4 * 512:4 * 512 + 8], xTt[:, kc, :], w_in[:, kc, 2048:2056], start=(kc == 0), stop=(kc == 7))
            ab = bp.alloc(8)
            k.cpv(ab, k.ps[:, 4 * 512:4 * 512 + 8])
            for h in range(4):
                k.tr(k.psb(5)[:, h * 128:(h + 1) * 128], cv[:, 8 + h, :])
            vtok = bp.alloc(512).rearrange("p (h d) -> p h d", h=4)
            k.cpa(vtok.rearrange("p h d -> p (h d)"), k.psb(5))
            for h in range(4):
                k.tr(k.psb(6)[:, h * 128:(h + 1) * 128], qkn[:, 4 + h, :])
            ktok = bp.alloc(512).rearrange("p (h d) -> p h d", h=4)
            k.cpa(ktok.rearrange("p h d -> p (h d)"), k.psb(6))
            ckpt(6)
            beta = bp.alloc(4)
            k.act(beta, ab[:, 4:8], AF.Sigmoid)
            tg = bp.alloc(4); t2 = bp.alloc(4); g = bp.alloc(4)
            k.tt(tg, ab[:, 0:4], dtb, ALU.add)
            k.ts(t2, tg, -1.0, ALU.mult)
            k.tt(t2, t2, tg, ALU.max)
            k.act(t2, t2, AF.Exp, scale=-1.0)
            k.act(t2, t2, AF.Ln, bias=k.one_c, scale=1.0)
            k.ts(tg, tg, 0.0, ALU.max)
            k.tt(tg, tg, t2, ALU.add)
            k.tt(g, tg, nA, ALU.mult)
            ckpt(7)
            k.mm(k.ps[:, 4 * 512 + 16:4 * 512 + 20], k.triLE, g)
            k.mm(k.ps[:, 4 * 512 + 32:4 * 512 + 36], k.ones, g)
            gc = bp.alloc(4); gl = bp.alloc(4); eg = bp.alloc(4); ekl = bp.alloc(4); egl = bp.alloc(4)
            k.cpv(gc, k.ps[:, 4 * 512 + 16:4 * 512 + 20])
            k.cpv(gl, k.ps[:, 4 * 512 + 32:4 * 512 + 36])
            k.act(eg, gc, AF.Exp)
            k.tt(ekl, gl, gc, ALU.subtract)
            k.act(ekl, ekl, AF.Exp)
            k.act(egl, gl, AF.Exp)
            ckpt(8)
            ED = bp.alloc(1024).rearrange("p (a h d) -> p a h d", a=2, h=4)
            bp.mark()
            G1 = bp.alloc(512).rearrange("p (h d) -> p h d", h=4)
            k.tt(G1, k.triLE.unsqueeze(1).to_broadcast([128, 4, 128]), g.unsqueeze(2).to_broadcast([128, 4, 128]), ALU.mult)
            k.mm(k.psb(7), k.ones, G1.rearrange("p h d -> p (h d)"))
            X = bp.alloc(512).rearrange("p (h d) -> p h d", h=4)
            k.tt(X, k.psb(7).rearrange("p (h d) -> p h d", h=4), gc.unsqueeze(2).to_broadcast([128, 4, 128]), ALU.subtract)
            k.ts(ED[:, 0], X, 0.0, ALU.min)
            k.ts(ED[:, 1], X, -1.0, ALU.mult, 0.0, ALU.min)
            k.act(ED.rearrange("p a h d -> p (a h d)"), ED.rearrange("p a h d -> p (a h d)"), AF.Exp)
            bp.release()
            ckpt(9)
            for h in range(4):
                k.mm(k.psb(5)[:, h * 128:(h + 1) * 128], qkn[:, 4 + h, :], qkn[:, 4 + h, :])
            if full:
                for h in range(4):
                    k.mm(k.psb(6)[:, h * 128:(h + 1) * 128], qkn[:, 4 + h, :], qkn[:, h, :])
            Y = bp.alloc(512).rearrange("p (h d) -> p h d", h=4)
            QKT = bp.alloc(512).rearrange("p (h d) -> p h d", h=4)
            bp.mark()
            A = [bp.alloc(512).rearrange("p (h d) -> p h d", h=4) for _ in range(2)]
            B = [bp.alloc(512).rearrange("p (h d) -> p h d", h=4) for _ in range(2)]
            L = A[0]
            k.tt(L, k.psb(5).rearrange("p (h d) -> p h d", h=4), ED[:, 1], ALU.mult)
            k.tt(L, L, beta.unsqueeze(2).to_broadcast([128, 4, 128]), ALU.mult)
            k.tt(L, L, k.maskSL.unsqueeze(1).to_broadcast([128, 4, 128]), ALU.mult)
            if full:
                k.tt(QKT, k.psb(6).rearrange("p (h d) -> p h d", h=4), ED[:, 0], ALU.mult)
                k.tt(QKT, QKT, k.triLE.unsqueeze(1).to_broadcast([128, 4, 128]), ALU.mult)
            for h in range(4):
                k.tr(k.psb(7)[:, h * 128:(h + 1) * 128], L[:, h, :])
            k.cpa(B[0].rearrange("p h d -> p (h d)"), k.psb(7))
            k.tt(Y, k.ident.unsqueeze(1).to_broadcast([128, 4, 128]), B[0], ALU.subtract)
            for j in range(1, 7):
                Ap, Bp = A[(j - 1) % 2], B[(j - 1) % 2]
                An, Bn = A[j % 2], B[j % 2]
                for h in range(4):
                    k.mm(k.psb(5)[:, h * 128:(h + 1) * 128], Bp[:, h, :], Ap[:, h, :])
                if j < 6:
                    for h in range(4):
                        k.mm(k.psb(6)[:, h * 128:(h + 1) * 128], Ap[:, h, :], Bp[:, h, :])
                k.cpa(An.rearrange("p h d -> p (h d)"), k.psb(5))
                if j < 6:
                    k.cpv(Bn.rearrange("p h d -> p (h d)"), k.psb(6))
                for h in range(4):
                    k.mm(k.psb(7)[:, h * 128:(h + 1) * 128], An[:, h, :], Y[:, h, :])
                k.tt(Y, Y, k.psb(7).rearrange("p (h d) -> p h d", h=4), ALU.add)
            bp.release()
            ckpt(10)
            Rv = bp.alloc(512).rearrange("p (h d) -> p h d", h=4)
            Rw = bp.alloc(512).rearrange("p (h d) -> p h d", h=4)
            be = bp.alloc(4)
            k.tt(Rv, vtok, beta.unsqueeze(2).to_broadcast([128, 4, 128]), ALU.mult)
            k.tt(be, beta, eg, ALU.mult)
            k.tt(Rw, ktok, be.unsqueeze(2).to_broadcast([128, 4, 128]), ALU.mult)
            for h in range(4):
                k.mm(k.psb(5)[:, h * 128:(h + 1) * 128], Rw[:, h, :], Y[:, h, :])
            nwT = bp.alloc(512).rearrange("p (h d) -> p h d", h=4)
            k.act(nwT.rearrange("p h d -> p (h d)"), k.psb(5), AF.Copy, scale=-1.0)
            for h in range(4):
                k.mm(k.psb(6)[:, h * 128:(h + 1) * 128], Y[:, h, :], Rv[:, h, :], start=True, stop=False)
                k.mm(k.psb(6)[:, h * 128:(h + 1) * 128], nwT[:, h, :], Sst[:, h, :], start=False, stop=True)
            vnew = bp.alloc(512).rearrange("p (h d) -> p h d", h=4)
            k.cpa(vnew.rearrange("p h d -> p (h d)"), k.psb(6))
            if full:
                for h in range(4):
                    k.mm(k.psb(5)[:, h * 128:(h + 1) * 128], qkn[:, h, :], Sst[:, h, :])
                for h in range(4):
                    k.mm(k.psb(7)[:, h * 128:(h + 1) * 128], QKT[:, h, :], vnew[:, h, :])
                k.tt(o, k.psb(5).rearrange("p (h d) -> p h d", h=4), eg.unsqueeze(2).to_broadcast([128, 4, 128]), ALU.mult)
                k.tt(o, o, k.psb(7).rearrange("p (h d) -> p h d", h=4), ALU.add)
            ckpt(11)
            kp = bp.alloc(512).rearrange("p (h d) -> p h d", h=4)
            k.tt(kp, ktok, ekl.unsqueeze(2).to_broadcast([128, 4, 128]), ALU.mult)
            for h in range(4):
                k.mm(k.psb(6)[:, h * 128:(h + 1) * 128], kp[:, h, :], vnew[:, h, :])
            k.tt(Sst, Sst, egl.unsqueeze(2).to_broadcast([128, 4, 128]), ALU.mult)
            k.tt(Sst, Sst, k.psb(6).rearrange("p (h d) -> p h d", h=4), ALU.add)
            ckpt(12)
            xc = cv[:, 12:16, :]
            for c in range(4):
                k.mm(k.psb(2)[:, c * 128:(c + 1) * 128], wbd[:, c, :], xc[:, c, :])
            for c in range(4):
                k.mm(k.psb(3)[:, c * 128:(c + 1) * 128], wbd[:, 4 + c, :], xc[:, c, :])
            bp.mark()
            ckpt(13)
            r = bp.alloc(512).rearrange("p (c t) -> p c t", c=4)
            ig = bp.alloc(512).rearrange("p (c t) -> p c t", c=4)
            a_ = bp.alloc(512).rearrange("p (c t) -> p c t", c=4)
            th = bp.alloc(512).rearrange("p (c t) -> p c t", c=4)
            for c in range(4):
                k.act(r[:, c, :], k.psb(2)[:, c * 128:(c + 1) * 128], AF.Sigmoid, bias=lb[:, c:c + 1])
                k.act(ig[:, c, :], k.psb(3)[:, c * 128:(c + 1) * 128], AF.Sigmoid, bias=lb[:, 4 + c:5 + c])
            ckpt(14)
            for c in range(4):
                k.act(a_[:, c, :], r[:, c, :], AF.Exp, scale=m8sp[:, c:c + 1])
                k.act(th[:, c, :], r[:, c, :], AF.Tanh, scale=m8sp[:, c:c + 1])
            ckpt(15)
            bb = bp.alloc(512).rearrange("p (c t) -> p c t", c=4)
            k.tt(bb, a_, a_, ALU.mult)
            k.stt(bb, bb, 1.0, th, ALU.add, ALU.mult)
            k.act(bb, bb, AF.Sqrt, scale=-1.0)
            k.tt(bb, bb, ig, ALU.mult)
            k.tt(bb, bb, xc, ALU.mult)
            ckpt(16)
            for c in range(4):
                S.op("dve", lambda e, c=c: e.tensor_tensor_scan(out=hseq[:, c, :], data0=a_[:, c, :], data1=bb[:, c, :],
                                                                initial=hst[:, c:c + 1], op0=ALU.mult, op1=ALU.add),
                     ins=[a_[:, c, :], bb[:, c, :], hst[:, c:c + 1]], outs=[hseq[:, c, :]])
            ckpt(17)
            k.cpv(hst, hseq[:, :, 127])
            bp.release()
            bp.release()
            if full:
                mixT = bp.alloc(1024).rearrange("p (c t) -> p c t", c=8)
                for ci in range(4):
                    for kc in range(8):
                        k.mm(k.psb(2)[:, ci * 128:(ci + 1) * 128], w_in[:, kc, 2568 + ci * 128:2568 + (ci + 1) * 128],
                             xTt[:, kc, :], start=(kc == 0), stop=(kc == 7))
                gt = bp.alloc(512).rearrange("p (c t) -> p c t", c=4)
                k.cpa(gt.rearrange("p c t -> p (c t)"), k.psb(2))
                k.gelu_tanh(gt, gt, 512)
                k.tt(mixT[:, 4:8, :], gt, hseq, ALU.mult)
                for kc in range(8):
                    k.mm(k.psb(3), xTt[:, kc, :], w_in[:, kc, 1536:2048], start=(kc == 0), stop=(kc == 7))
                sz = bp.alloc(512).rearrange("p (h d) -> p h d", h=4)
                k.act(sz.rearrange("p h d -> p (h d)"), k.psb(3), AF.Silu)
                osq = bp.alloc(512).rearrange("p (h d) -> p h d", h=4)
                ss = bp.alloc(4)
                k.tt(osq, o, o, ALU.mult)
                k.red(ss, osq)
                k.act(ss, ss, AF.Sqrt, bias=k.eps6, scale=1.0 / 128.0)
                k.recip(ss, ss)
                k.tt(o, o, ss.unsqueeze(2).to_broadcast([128, 4, 128]), ALU.mult)
                k.tt(o, o, nw.unsqueeze(1).to_broadcast([128, 4, 128]), ALU.mult)
                k.tt(o, o, sz, ALU.mult)
                for h in range(4):
                    k.tr(k.psb(5)[:, h * 128:(h + 1) * 128], o[:, h, :])
                k.cpa(mixT[:, 0:4, :].rearrange("p c t -> p (c t)"), k.psb(5))
                for hf in range(2):
                    for kc in range(8):
                        k.ld(w_oh[:, kc, :], w_out_d[:, kc, hf * 512:(hf + 1) * 512], eng="sp" if kc % 2 == 0 else "act")
                    for fc in range(8):
                        k.mm(k.psb(2 + hf), mixT[:, fc, :], w_oh[:, fc, :], start=(fc == 0), stop=(fc == 7))
                x1 = bp.alloc(1024)
                k.resid_ln(x1, xtok[0], k.ps[:, 2 * 512:4 * 512], lng, lnb)
                k.ld(k.X1[t - (NPRE - 1)], x1)
            bp.release()
        except StopTile:
            bp.off = _off0; del bp.marks[_nm0:]
    k.ld(k.dout["p_gdn"].rearrange("h k v -> k h v"), Sst)
    k.tr(k.ps[0:4, 0:128], hst)
    hl = bp.alloc(128)
    k.cpv(hl[0:4, :], k.ps[0:4, 0:128])
    k.ld(k.dout["p_lru"], hl[0:4, :])
    bp.release()
    phase_a_sample(k, w_in, w_oh, w_out_d, cw, cb, lb, m8sp, wbd, dtb, nA, lng, lnb)
    bp.release()


def phase_a_sample(k, w_in, w_oh, w_out_d, cw, cb, lb, m8sp, wbd, dtb, nA, lng, lnb):
    S, bp, nc = k.S, k.bp, k.nc
    din, dout = k.din, k.dout
    NSS = NS
    bp.mark()
    xT = bp.alloc(8 * NSS).rearrange("p (kc s) -> p kc s", kc=8)
    k.ld(xT, din["xsT"])
    xtok = bp.alloc(1024)
    k.ld(xtok, din["xstok"])
    hist = bp.alloc(16 * 3 * NSS).rearrange("p (c i s) -> p c i s", c=16, i=3)
    k.ld(hist, din["convhT"])
    h0 = bp.alloc(4 * NSS).rearrange("p (c s) -> p c s", c=4)
    k.ld(h0, din["lru_h0T"])
    nwc = bp.alloc(1)
    k.ld(nwc, din["gdn_norm_w"].rearrange("(p o) -> p o", o=1))
    Sall = bp.alloc(NSS * 512).rearrange("p (s h d) -> p s h d", s=NSS, h=4)
    for s in range(NSS):
        k.ld(Sall[:, s], din["s_gdn_in"][s].rearrange("h k v -> k h v"), eng="sp" if s % 2 == 0 else "act")
    k.ld(dout["s_gdn_conv"][:, 0:2, :], din["gconv_nat"][:, 1:3, :])
    k.ld(dout["s_lru_conv"][:, 0:2, :], din["lconv_nat"][:, 1:3, :])
    colbase = [c * 128 for c in range(12)] + [2056 + c * 128 for c in range(4)] + \
              [1536 + c * 128 for c in range(4)] + [2568 + c * 128 for c in range(4)]
    for c in range(24):
        for kc in range(8):
            k.mm(k.psb(0)[:, c * NSS:(c + 1) * NSS], w_in[:, kc, colbase[c]:colbase[c] + 128], xT[:, kc, :],
                 start=(kc == 0), stop=(kc == 7))
    pf = bp.alloc(24 * NSS).rearrange("p (c s) -> p c s", c=24)
    k.cpa(pf.rearrange("p c s -> p (c s)"), k.psb(0)[:, 0:24 * NSS])
    for bi, c0 in enumerate((0, 512, 1024, 2056)):
        for kc in range(8):
            k.mm(k.psb(1 + bi)[0:NSS, :], xT[:, kc, :], w_in[:, kc, c0:c0 + 512], start=(kc == 0), stop=(kc == 7))
    bp.mark()
    pre = bp.alloc(2048)
    k.cpv(pre[0:NSS, 0:1024], k.ps[0:NSS, 512:1536])
    k.cpa(pre[0:NSS, 1024:2048], k.ps[0:NSS, 1536:2560])
    k.ld(dout["s_gdn_conv"][:, 2, :], pre[0:NSS, 0:1536])
    k.ld(dout["s_lru_conv"][:, 2, :], pre[0:NSS, 1536:2048])
    bp.release()
    for kc in range(8):
        k.mm(k.psb(5)[0:NSS, 0:8], xT[:, kc, :], w_in[:, kc, 2048:2056], start=(kc == 0), stop=(kc == 7))
    ab = bp.alloc(8)
    k.cpv(ab[0:NSS], k.psb(5)[0:NSS, 0:8])
    cv = bp.alloc(16 * NSS).rearrange("p (c s) -> p c s", c=16)
    tmp = bp.alloc(16 * 3 * NSS).rearrange("p (c i s) -> p c i s", c=16, i=3)
    k.tt(tmp, hist, cw[:, :, 0:3].unsqueeze(3).to_broadcast([128, 16, 3, NSS]), ALU.mult)
    k.red(cv, tmp.rearrange("p c i s -> p c s i"))
    t2 = bp.alloc(16 * NSS).rearrange("p (c s) -> p c s", c=16)
    k.tt(t2, pf[:, 0:16, :], cw[:, :, 3:4].to_broadcast([128, 16, NSS]), ALU.mult)
    k.tt(cv, cv, t2, ALU.add)
    k.tt(cv[:, 12:16, :], cv[:, 12:16, :], cb.unsqueeze(2).to_broadcast([128, 4, NSS]), ALU.add)
    k.act(cv[:, 0:12, :], cv[:, 0:12, :], AF.Silu)
    sq = bp.alloc(8 * NSS)
    k.act(sq.rearrange("p (c s) -> p c s", c=8), cv[:, 0:8, :], AF.Square)
    k.mm(k.psb(5)[:, 128:128 + 8 * NSS], k.ones, sq)
    rs = bp.alloc(8 * NSS)
    k.act(rs[:, 0:4 * NSS], k.psb(5)[:, 128:128 + 4 * NSS], AF.Sqrt, bias=k.c128e6, scale=128.0)
    k.act(rs[:, 4 * NSS:8 * NSS], k.psb(5)[:, 128 + 4 * NSS:128 + 8 * NSS], AF.Sqrt, bias=k.eps6, scale=1.0)
    k.recip(rs, rs)
    qkn = bp.alloc(8 * NSS).rearrange("p (c s) -> p c s", c=8)
    k.tt(qkn, cv[:, 0:8, :], rs.rearrange("p (c s) -> p c s", c=8), ALU.mult)
    P = NSS
    beta = bp.alloc(12)
    tg = bp.alloc(4); tt2 = bp.alloc(4)
    k.act(beta[0:P, 0:4], ab[0:P, 4:8], AF.Sigmoid)
    k.tt(tg[0:P], ab[0:P, 0:4], dtb[0:P], ALU.add)
    k.ts(tt2[0:P], tg[0:P], -1.0, ALU.mult)
    k.tt(tt2[0:P], tt2[0:P], tg[0:P], ALU.max)
    k.act(tt2[0:P], tt2[0:P], AF.Exp, scale=-1.0)
    k.act(tt2[0:P], tt2[0:P], AF.Ln, bias=k.one_c[0:P], scale=1.0)
    k.ts(tg[0:P], tg[0:P], 0.0, ALU.max)
    k.tt(tg[0:P], tg[0:P], tt2[0:P], ALU.add)
    k.tt(tg[0:P], tg[0:P], nA[0:P], ALU.mult)
    k.act(beta[0:P, 4:8], tg[0:P], AF.Exp)
    bd = bp.alloc(2 * 4 * NSS).rearrange("p (a h s) -> p a h s", a=2, h=4)
    k.tt(bd[0:P], beta[0:P, 0:8].rearrange("p (a h) -> p a h", a=2).unsqueeze(3).to_broadcast([P, 2, 4, NSS]),
         k.ident[0:P, 0:NSS].unsqueeze(1).unsqueeze(1).to_broadcast([P, 2, 4, NSS]), ALU.mult)
    k.mm(k.psb(5)[:, 256:256 + 8 * NSS], k.ones[0:P, :], bd[0:P].rearrange("p a h s -> p (a h s)"))
    BC = bp.alloc(3 * 4 * NSS).rearrange("p (a h s) -> p a h s", a=3, h=4)
    k.cpv(BC[:, 0:2].rearrange("p a h s -> p (a h s)"), k.psb(5)[:, 256:256 + 8 * NSS])
    prod = bp.alloc(4 * NSS)
    k.tt(prod.rearrange("p (h s) -> p h s", h=4), qkn[:, 0:4, :], qkn[:, 4:8, :], ALU.mult)
    k.mm(k.psb(5)[:, 384:384 + 4 * NSS], k.ones, prod)
    k.cpv(BC[:, 2].rearrange("p h s -> p (h s)"), k.psb(5)[:, 384:384 + 4 * NSS])
    kq = bp.alloc(4 * NSS * 2).rearrange("p (h s a) -> p h s a", h=4, s=NSS)
    k.cpv(kq[:, :, :, 0], qkn[:, 4:8, :])
    k.cpv(kq[:, :, :, 1], qkn[:, 0:4, :])
    for h in range(4):
        for s in range(NSS):
            c0 = (h * NSS + s) * 2
            k.mm(k.psb(6)[:, c0:c0 + 2], Sall[:, s, h, :], kq[:, h, s, :])
    ksqs = bp.alloc(4 * NSS * 2).rearrange("p (h s a) -> p h s a", h=4, s=NSS)
    k.cpv(ksqs.rearrange("p h s a -> p (h s a)"), k.psb(6)[:, 0:4 * NSS * 2])
    vnT = bp.alloc(4 * NSS).rearrange("p (h s) -> p h s", h=4)
    oT = bp.alloc(4 * NSS).rearrange("p (h s) -> p h s", h=4)
    k.tt(vnT, ksqs[:, :, :, 0], BC[:, 1], ALU.mult)
    k.tt(vnT, cv[:, 8:12, :], vnT, ALU.subtract)
    k.tt(vnT, vnT, BC[:, 0], ALU.mult)
    k.tt(oT, ksqs[:, :, :, 1], BC[:, 1], ALU.mult)
    t3 = bp.alloc(4 * NSS).rearrange("p (h s) -> p h s", h=4)
    k.tt(t3, vnT, BC[:, 2], ALU.mult)
    k.tt(oT, oT, t3, ALU.add)
    for h in range(4):
        k.tr(k.psb(7)[0:NSS, h * 128:(h + 1) * 128], vnT[:, h, :])
    vn_tok = bp.alloc(512)
    k.cpv(vn_tok[0:P], k.psb(7)[0:NSS, :])
    for h in range(4):
        k.tr(k.psb(7)[0:NSS, h * 128:(h + 1) * 128], qkn[:, 4 + h, :])
    k_tok = bp.alloc(512)
    k.cpv(k_tok[0:P], k.psb(7)[0:NSS, :])
    Am = [bp.alloc(512) for _ in range(2)]
    for s in range(NSS):
        am = Am[s % 2]
        k.ts(am[0:P], k_tok[0:P], k.ident[0:P, s:s + 1], ALU.mult)
        bank = 1 + s % 2
        for h in range(4):
            k.mm(k.psb(bank)[:, h * 128:(h + 1) * 128], am[0:P, h * 128:(h + 1) * 128], vn_tok[0:P, h * 128:(h + 1) * 128])
        for h in range(4):
            k.stt(Sall[:, s, h, :], Sall[:, s, h, :], BC[:, 1, h, s:s + 1], k.psb(bank)[:, h * 128:(h + 1) * 128], ALU.mult, ALU.add)
        k.ld(dout["s_gdn"][s].rearrange("h k v -> k h v"), Sall[:, s], eng="sp" if s % 2 == 0 else "act")
    mixT = bp.alloc(8 * NSS).rearrange("p (c s) -> p c s", c=8)
    osq = bp.alloc(4 * NSS)
    k.tt(osq.rearrange("p (h s) -> p h s", h=4), oT, oT, ALU.mult)
    k.mm(k.psb(5)[:, 0:4 * NSS], k.ones, osq)
    rso = bp.alloc(4 * NSS)
    k.act(rso, k.psb(5)[:, 0:4 * NSS], AF.Sqrt, bias=k.eps6, scale=1.0 / 128.0)
    k.recip(rso, rso)
    k.tt(oT, oT, rso.rearrange("p (h s) -> p h s", h=4), ALU.mult)
    k.ts(oT.rearrange("p h s -> p (h s)"), oT.rearrange("p h s -> p (h s)"), nwc, ALU.mult)
    sz = bp.alloc(4 * NSS).rearrange("p (h s) -> p h s", h=4)
    k.act(sz, pf[:, 16:20, :], AF.Silu)
    k.tt(mixT[:, 0:4, :], oT, sz, ALU.mult)
    xc = cv[:, 12:16, :]
    for c in range(4):
        k.mm(k.psb(5)[:, 128 + c * NSS:128 + (c + 1) * NSS], wbd[:, c, :], xc[:, c, :])
        k.mm(k.psb(5)[:, 256 + c * NSS:256 + (c + 1) * NSS], wbd[:, 4 + c, :], xc[:, c, :])
    r = bp.alloc(4 * NSS).rearrange("p (c s) -> p c s", c=4)
    ig = bp.alloc(4 * NSS).rearrange("p (c s) -> p c s", c=4)
    a_ = bp.alloc(4 * NSS).rearrange("p (c s) -> p c s", c=4)
    th = bp.alloc(4 * NSS).rearrange("p (c s) -> p c s", c=4)
    for c in range(4):
        k.act(r[:, c, :], k.psb(5)[:, 128 + c * NSS:128 + (c + 1) * NSS], AF.Sigmoid, bias=lb[:, c:c + 1])
        k.act(ig[:, c, :], k.psb(5)[:, 256 + c * NSS:256 + (c + 1) * NSS], AF.Sigmoid, bias=lb[:, 4 + c:5 + c])
    for c in range(4):
        k.act(a_[:, c, :], r[:, c, :], AF.Exp, scale=m8sp[:, c:c + 1])
        k.act(th[:, c, :], r[:, c, :], AF.Tanh, scale=m8sp[:, c:c + 1])
    bb = bp.alloc(4 * NSS).rearrange("p (c s) -> p c s", c=4)
    k.tt(bb, a_, a_, ALU.mult)
    k.stt(bb, bb, 1.0, th, ALU.add, ALU.mult)
    k.act(bb, bb, AF.Sqrt, scale=-1.0)
    k.tt(bb, bb, ig, ALU.mult)
    k.tt(bb, bb, xc, ALU.mult)
    hn = bp.alloc(4 * NSS).rearrange("p (c s) -> p c s", c=4)
    k.tt(hn, a_, h0, ALU.mult)
    k.tt(hn, hn, bb, ALU.add)
    for c in range(4):
        k.tr(k.psb(7)[0:NSS, c * 128:(c + 1) * 128], hn[:, c, :])
    hl = bp.alloc(512)
    k.cpv(hl[0:P], k.psb(7)[0:NSS, :])
    k.ld(dout["s_lru"], hl[0:P])
    gt = bp.alloc(4 * NSS).rearrange("p (c s) -> p c s", c=4)
    k.gelu_tanh(gt, pf[:, 20:24, :], 4 * NSS)
    k.tt(mixT[:, 4:8, :], gt, hn, ALU.mult)
    for hf in range(2):
        for kc in range(8):
            k.ld(w_oh[:, kc, :], w_out_d[:, kc, hf * 512:(hf + 1) * 512], eng="sp" if kc % 2 == 0 else "act")
        for fc in range(8):
            k.mm(k.psb(2 + hf)[0:NSS, :], mixT[:, fc, :], w_oh[:, fc, :], start=(fc == 0), stop=(fc == 7))
    x1 = bp.alloc(1024)
    k.memset(x1, 0.0)
    k.resid_ln(x1[0:P], xtok[0:P], k.ps[0:NSS, 2 * 512:4 * 512], lng, lnb, P=P)
    k.ld(k.X1[NMAIN + 1], x1)
    bp.release()


def phase_peer(k, layer, Xin, Xout_fn, ntiles):
    S, bp, nc = k.S, k.bp, k.nc
    din = k.din
    NB = 16
    bp.mark()
    wq = bp.alloc(8 * 2048).rearrange("p (kc c) -> p kc c", kc=8)
    wq_d = din["peer_w_q"][layer].rearrange("(kc p) c -> p kc c", p=128)
    for kc in range(8):
        k.ld(wq[:, kc, :], wq_d[:, kc, :], eng="sp" if kc % 2 == 0 else "act")
    keysT = bp.alloc(16 * 128).rearrange("p (g n) -> p g n", g=16)
    k.ld(keysT, din["peer_keysT"][layer])
    lng = bp.alloc(1024); k.ld(lng, din["ln_ffn_g"][layer].partition_broadcast(128))
    lnb = bp.alloc(1024); k.ld(lnb, din["ln_ffn_b"][layer].partition_broadcast(128))
    iota16 = bp.alloc(16); k.ld(iota16, din["iota16"].partition_broadcast(128))
    u_tab = din["peer_u%d" % layer]
    v_tab = din["peer_v%d" % layer]
    gbuf = [bp.alloc(1024) for _ in range(NB)]
    st = [dict(x1=bp.alloc(1024), eidx=bp.alloc(128, I32), gate=bp.alloc(128).rearrange("p (h k) -> p h k", h=8))
          for _ in range(2)]
    xT = bp.alloc(1024).rearrange("p (kc t) -> p kc t", kc=8)
    qT = bp.alloc(2048).rearrange("p (g t) -> p g t", g=16)
    sc = bp.alloc(2048).rearrange("p (g n) -> p g n", g=16)
    m16 = bp.alloc(256).rearrange("p (g k) -> p g k", g=16)
    i16 = bp.alloc(256, U32).rearrange("p (g k) -> p g k", g=16)
    tmp = bp.alloc(256)
    cand = qT.rearrange("p g t -> p (g t)").rearrange("p (h i j) -> p h i j", h=8, i=16)
    b16 = bp.alloc(128).rearrange("p (h k) -> p h k", h=8)
    pos = bp.alloc(128, U32).rearrange("p (h k) -> p h k", h=8)
    hif = bp.alloc(128).rearrange("p (h k) -> p h k", h=8)
    lof = bp.alloc(128).rearrange("p (h k) -> p h k", h=8)
    i16f = bp.alloc(256).rearrange("p (g k) -> p g k", g=16)
    pu = bp.alloc(256, U32)
    sel = bp.alloc(256).rearrange("p (a h k) -> p a h k", a=2, h=8)
    ef = bp.alloc(128)
    gs = bp.alloc(8)
    actp = bp.alloc(128)
    junk = [bp.alloc(1024) for _ in range(2)]
    wgt = bp.alloc(128)
    eidx2 = bp.alloc(16, I32)
    gate2 = bp.alloc(16)
    x1rep = sc.rearrange("p g n -> p (g n)")[:, 0:1024]
    ssel = bp.alloc(128); k.ld(ssel, din["ssel"])
    scrE = k.scratch("peer_scr_e%d" % layer, [16, 128], I32)
    scrG = k.scratch("peer_scr_g%d" % layer, [16, 128])
    full_run = ntiles >= NMAIN + 1
    dgb = [bp.alloc(128) for _ in range(3)]

    def top16(vals, mo, io, tmpb):
        S.op("dve", lambda e: e.max(out=mo[:, 0:8], in_=vals), ins=[vals], outs=[mo[:, 0:8]])
        S.op("dve", lambda e: e.max_index(out=io[:, 0:8], in_max=mo[:, 0:8], in_values=vals), ins=[vals, mo[:, 0:8]], outs=[io[:, 0:8]])
        S.op("dve", lambda e: e.match_replace(out=tmpb, in_to_replace=mo[:, 0:8], in_values=vals, imm_value=-1e30),
             ins=[vals, mo[:, 0:8]], outs=[tmpb])
        S.op("dve", lambda e: e.max(out=mo[:, 8:16], in_=tmpb), ins=[tmpb], outs=[mo[:, 8:16]])
        S.op("dve", lambda e: e.max_index(out=io[:, 8:16], in_max=mo[:, 8:16], in_values=tmpb), ins=[tmpb, mo[:, 8:16]], outs=[io[:, 8:16]])

    def front(t):
        x1, eidx, gate = st[t % 2]["x1"], st[t % 2]["eidx"], st[t % 2]["gate"]
        k.ld(x1, Xin[t])
        for hf in range(2):
            for c in range(4):
                k.tr(k.psb(hf)[:, c * 128:(c + 1) * 128], x1[:, (hf * 4 + c) * 128:(hf * 4 + c + 1) * 128])
            k.cpa(xT[:, hf * 4:(hf + 1) * 4, :].rearrange("p c t -> p (c t)"), k.psb(hf))
        for g4 in range(4):
            for ci in range(4):
                g = g4 * 4 + ci
                for kc in range(8):
                    k.mm(k.psb(2 + g4 % 2)[:, ci * 128:(ci + 1) * 128], wq[:, kc, g * 128:(g + 1) * 128], xT[:, kc, :],
                         start=(kc == 0), stop=(kc == 7))
            k.cpa(qT[:, g4 * 4:(g4 + 1) * 4, :].rearrange("p c t -> p (c t)"), k.psb(2 + g4 % 2))
        for g4 in range(4):
            for ci in range(4):
                g = g4 * 4 + ci
                k.mm(k.psb(4 + g4 % 2)[:, ci * 128:(ci + 1) * 128], qT[:, g, :], keysT[:, g, :])
            k.cpa(sc[:, g4 * 4:(g4 + 1) * 4, :].rearrange("p c t -> p (c t)"), k.psb(4 + g4 % 2))
        for g in range(16):
            top16(sc[:, g, :], m16[:, g, :], i16[:, g, :], tmp[:, 0:128])
        for h in range(8):
            k.tt(cand[:, h], m16[:, 2 * h, :].unsqueeze(2).to_broadcast([128, 16, 16]),
                 m16[:, 2 * h + 1, :].unsqueeze(1).to_broadcast([128, 16, 16]), ALU.add)
        for h in range(8):
            top16(cand[:, h].rearrange("p i j -> p (i j)"), b16[:, h, :], pos[:, h, :], tmp)
        k.cpv(i16f.rearrange("p g k -> p (g k)"), i16.rearrange("p g k -> p (g k)"))
        posu = pos.rearrange("p h k -> p (h k)")
        S.op("dve", lambda e: e.tensor_single_scalar(out=pu[:, 0:128], in_=posu, scalar=4, op=ALU.logical_shift_right), ins=[posu], outs=[pu[:, 0:128]])
        S.op("dve", lambda e: e.tensor_single_scalar(out=pu[:, 128:256], in_=posu, scalar=15, op=ALU.bitwise_and), ins=[posu], outs=[pu[:, 128:256]])
        k.cpv(hif.rearrange("p h k -> p (h k)"), pu[:, 0:128])
        k.cpv(lof.rearrange("p h k -> p (h k)"), pu[:, 128:256])
        eq = cand
        i16v = i16f.rearrange("p (h c) k -> p h c k", c=2)
        for a, idxf in enumerate((hif, lof)):
            k.tt(eq, iota16.unsqueeze(1).unsqueeze(1).to_broadcast([128, 8, 16, 16]),
                 idxf.unsqueeze(3).to_broadcast([128, 8, 16, 16]), ALU.is_equal)
            k.tt(eq, eq, i16v[:, :, a, :].unsqueeze(2).to_broadcast([128, 8, 16, 16]), ALU.mult)
            k.red(sel[:, a], eq)
        k.stt(ef, sel[:, 0].rearrange("p h k -> p (h k)"), 128.0, sel[:, 1].rearrange("p h k -> p (h k)"), ALU.mult, ALU.add)
        k.cpv(eidx, ef)
        k.tt(gate, b16, b16[:, :, 0:1].to_broadcast([128, 8, 16]), ALU.subtract)
        k.act(gate.rearrange("p h k -> p (h k)"), gate.rearrange("p h k -> p (h k)"), AF.Exp)
        k.red(gs, gate)
        k.recip(gs, gs)
        k.tt(gate, gate, gs.unsqueeze(2).to_broadcast([128, 8, 16]), ALU.mult)

    def record_front(t):
        S.defer = []
        front(t)
        lst = S.defer
        S.defer = None
        return lst

    front(0)
    gi = 0
    for t in range(ntiles):
        x1, eidx, gate = st[t % 2]["x1"], st[t % 2]["eidx"], st[t % 2]["gate"]
        pend = record_front(t + 1) if t + 1 < ntiles else []
        pi = 0
        per_step = -(-len(pend) // 120) if pend else 0
        if full_run and t == ntiles - 1:
            k.ld(scrE, eidx[0:16, :])
            k.ld(scrG, gate.rearrange("p h k -> p (h k)")[0:16, :])
            k.ld(eidx2, scrE.rearrange("t (h k) -> (t h) k", h=8))
            k.ld(gate2, scrG.rearrange("t (h k) -> (t h) k", h=8))
            for tok in range(16):
                k.ld(x1rep[tok * 8:(tok + 1) * 8, :], Xin[t][tok].partition_broadcast(8), eng="sp" if tok % 2 == 0 else "act")
            for kk in range(16):
                ub = gbuf[gi % NB]; gi += 1
                S.dma("pool", lambda e, ub=ub, kk=kk: e.indirect_dma_start(out=ub, out_offset=None, in_=u_tab,
                      in_offset=bass.IndirectOffsetOnAxis(ap=eidx2[:, kk:kk + 1], axis=0)), ins=[eidx2[:, kk:kk + 1], u_tab], outs=[ub])
                k.stt(junk[kk % 2], ub, 1.0, x1rep, ALU.mult, ALU.mult, accum=actp[:, kk:kk + 1])
            k.gelu_tanh(wgt[:, 0:16], actp[:, 0:16], 16)
            k.tt(wgt[:, 0:16], wgt[:, 0:16], gate2, ALU.mult)
            yacc = junk[0]
            k.memset(yacc, 0.0)
            for kk in range(16):
                vb = gbuf[gi % NB]; gi += 1
                S.dma("pool", lambda e, vb=vb, kk=kk: e.indirect_dma_start(out=vb, out_offset=None, in_=v_tab,
                      in_offset=bass.IndirectOffsetOnAxis(ap=eidx2[:, kk:kk + 1], axis=0)), ins=[eidx2[:, kk:kk + 1], v_tab], outs=[vb])
                k.stt(yacc, vb, wgt[:, kk:kk + 1], yacc, ALU.mult, ALU.add)
            k.mm(k.psb(6), ssel, yacc[:, 0:512])
            k.mm(k.psb(7), ssel, yacc[:, 512:1024])
            k.cpv(junk[1], k.ps[:, 6 * 512:8 * 512])
            k.resid_ln(x1, x1, junk[1], lng, lnb)
            k.ld(Xout_fn(t), x1)
            continue
        for s in range(128):
            ub = gbuf[gi % NB]; gi += 1
            S.dma("pool", lambda e, ub=ub, s=s, eidx=eidx: e.indirect_dma_start(out=ub, out_offset=None, in_=u_tab,
                  in_offset=bass.IndirectOffsetOnAxis(ap=eidx[:, s:s + 1], axis=0)), ins=[eidx[:, s:s + 1], u_tab], outs=[ub])
            k.stt(junk[s % 2], ub, 1.0, x1, ALU.mult, ALU.mult, accum=actp[:, s:s + 1])
        k.gelu_tanh(wgt, actp, 128)
        k.tt(wgt, wgt, gate.rearrange("p h k -> p (h k)"), ALU.mult)
        yacc = junk[0]
        k.memset(junk[0], 0.0)
        k.memset(junk[1], 0.0, eng="pool")
        pe_slots = [s for s in range(128) if s % 8 in (1, 4, 6)]
        n_dve = 0
        for s in range(128):
            vb = gbuf[gi % NB]; gi += 1
            S.dma("pool", lambda e, vb=vb, s=s, eidx=eidx: e.indirect_dma_start(out=vb, out_offset=None, in_=v_tab,
                  in_offset=bass.IndirectOffsetOnAxis(ap=eidx[:, s:s + 1], axis=0)), ins=[eidx[:, s:s + 1], v_tab], outs=[vb])
            if s % 8 in (1, 4, 6):
                dg = dgb[s % 3]
                k.act(dg, k.ident, AF.Identity, scale=wgt[:, s:s + 1])
                first, last = (s == pe_slots[0]), (s == pe_slots[-1])
                k.mm(k.psb(6), dg, vb[:, 0:512], start=first, stop=last)
                k.mm(k.psb(7), dg, vb[:, 512:1024], start=first, stop=last)
            else:
                ya = junk[n_dve % 2]; n_dve += 1
                k.stt(ya, vb, wgt[:, s:s + 1], ya, ALU.mult, ALU.add)
            if s >= 4:
                for _ in range(per_step):
                    if pi < len(pend):
                        pend[pi](); pi += 1
        while pi < len(pend):
            pend[pi](); pi += 1
        k.tt(yacc, yacc, junk[1], ALU.add)
        k.tt(yacc, yacc, k.ps[:, 6 * 512:8 * 512], ALU.add)
        k.resid_ln(x1, x1, yacc, lng, lnb)
        k.ld(Xout_fn(t), x1)
    bp.release()


def phase_c(k, Xin, Xout):
    S, bp, nc = k.S, k.bp, k.nc
    din = k.din
    bp.mark()
    w_in = bp.alloc(8 * 1536).rearrange("p (kc c) -> p kc c", kc=8)
    w_in_d = din["w_in_c"].rearrange("(kc p) c -> p kc c", p=128)
    for kc in range(8):
        k.ld(w_in[:, kc, :], w_in_d[:, kc, :], eng="sp" if kc % 2 == 0 else "act")
    w_out = bp.alloc(8 * 1024).rearrange("p (kc c) -> p kc c", kc=8)
    w_out_d = din["w_out_c"].rearrange("(kc p) c -> p kc c", p=128)
    for kc in range(8):
        k.ld(w_out[:, kc, :], w_out_d[:, kc, :], eng="sp" if kc % 2 == 0 else "act")
    bqk = bp.alloc(10); k.ld(bqk, din["b_in_qk"])
    bkv = bp.alloc(512); k.ld(bkv, din["b_in_c"][1024:1536].partition_broadcast(128))
    bout = bp.alloc(1024); k.ld(bout, din["b_out_c"].partition_broadcast(128))
    lng = bp.alloc(1024); k.ld(lng, din["ln_mix_g"][1].partition_broadcast(128))
    lnb = bp.alloc(1024); k.ld(lnb, din["ln_mix_b"][1].partition_broadcast(128))
    sink = bp.alloc(16); k.ld(sink, din["swa_sinks"].partition_broadcast(128))
    hmask = bp.alloc(128); k.ld(hmask, din["hmask"])
    rb = bp.alloc(16)
    k.memset(rb[0:33, :], -30000.0)
    k.ld(rb[0:32, :], din["rel_bias"])
    bp.mark()
    Tb = bp.alloc(16 * 256).rearrange("p (h c) -> p h c", h=16)
    bp.mark()
    ohs_ = [bp.alloc(4096) for _ in range(2)]
    stages_ = [bp.alloc(4096) for _ in range(2)]
    Tscr = k.scratch("Tscr", [16, 128 * 256])
    for blk in range(8):
        oh, stage = ohs_[blk % 2], stages_[blk % 2]
        k.ld(oh[0:33, :], din["t5_onehot"][:, blk * 4096:(blk + 1) * 4096], eng="sp")
        for j in range(8):
            k.mm(k.ps[0:16, j * 512:(j + 1) * 512], rb[0:33, :], oh[0:33, j * 512:(j + 1) * 512])
        k.cpv(stage[0:16, 0:2048], k.ps[0:16, 0:2048])
        k.cpa(stage[0:16, 2048:4096], k.ps[0:16, 2048:4096])
        k.ld(Tscr[:, blk * 4096:(blk + 1) * 4096], stage[0:16, :], eng="act")
    bp.release()
    k.ld(Tb, Tscr.rearrange("h (q c) -> q h c", q=128))
    KT = [bp.alloc(256).rearrange("p (c t) -> p c t", c=2) for _ in range(2)]
    V = [bp.alloc(256) for _ in range(2)]
    ntile = NMAIN + 1
    for t in range(ntile):
        bp.mark()
        cur, prv = t % 2, (t + 1) % 2
        x2 = bp.alloc(1024)
        k.ld(x2, Xin[t])
        xT = bp.alloc(1024).rearrange("p (kc t) -> p kc t", kc=8)
        for hf in range(2):
            for c in range(4):
                k.tr(k.psb(hf)[:, c * 128:(c + 1) * 128], x2[:, (hf * 4 + c) * 128:(hf * 4 + c + 1) * 128])
            k.cpa(xT[:, hf * 4:(hf + 1) * 4, :].rearrange("p c t -> p (c t)"), k.psb(hf))
        for c in range(2):
            for kc in range(8):
                k.mm(k.psb(2)[:, c * 128:(c + 1) * 128], w_in[:, kc, 1024 + c * 128:1024 + (c + 1) * 128], xT[:, kc, :],
                     start=(kc == 0), stop=(kc == 7))
        for c in range(2):
            k.act(KT[cur][:, c, :], k.psb(2)[:, c * 128:(c + 1) * 128], AF.Identity, bias=bqk[:, 8 + c:9 + c])
        for kc in range(8):
            k.mm(k.psb(3), xT[:, kc, :], w_in[:, kc, 1024:1536], start=(kc == 0), stop=(kc == 7))
        kvtok = bp.alloc(512)
        k.tt(kvtok, k.psb(3), bkv, ALU.add)
        k.cpv(V[cur], kvtok[:, 256:512], eng="pool")
        if t == ntile - 1:
            k.ld(k.dout["p_swa_k"], kvtok[:, 0:256])
            k.ld(k.dout["p_swa_v"], kvtok[:, 256:512])
        if t == 0:
            bp.release()
            continue
        QT = bp.alloc(1024).rearrange("p (c t) -> p c t", c=8)
        for g4 in range(2):
            for ci in range(4):
                c = g4 * 4 + ci
                for kc in range(8):
                    k.mm(k.psb(4 + g4)[:, ci * 128:(ci + 1) * 128], w_in[:, kc, c * 128:(c + 1) * 128], xT[:, kc, :],
                         start=(kc == 0), stop=(kc == 7))
            for ci in range(4):
                c = g4 * 4 + ci
                k.act(QT[:, c, :], k.psb(4 + g4)[:, ci * 128:(ci + 1) * 128], AF.Identity, bias=bqk[:, c:c + 1])
        lg = bp.alloc(16 * 256).rearrange("p (h c) -> p h c", h=16)
        mx = bp.alloc(16); nmx = bp.alloc(16); ssum = bp.alloc(16); es = bp.alloc(16)
        for hp in range(8):
            bank = 6 + hp % 2
            for hh in range(2):
                h = hp * 2 + hh
                j = h // 4
                half = j % 2
                m = (h // 8) * 4 + h % 4
                qh = QT[half * 64:(half + 1) * 64, m, :]
                k.mm(k.psb(bank)[:, hh * 256:hh * 256 + 128], qh, KT[prv][half * 64:(half + 1) * 64, j // 2, :])
                k.mm(k.psb(bank)[:, hh * 256 + 128:hh * 256 + 256], qh, KT[cur][half * 64:(half + 1) * 64, j // 2, :])
            k.stt(lg[:, hp * 2:hp * 2 + 2, :], k.psb(bank).rearrange("p (h c) -> p h c", h=2), 0.125,
                  Tb[:, hp * 2:hp * 2 + 2, :], ALU.mult, ALU.add)
        if t == 1:
            k.tt(lg[:, :, 0:128], lg[:, :, 0:128], hmask.unsqueeze(1).to_broadcast([128, 16, 128]), ALU.add)
        k.red(mx, lg, op=ALU.max)
        k.tt(mx, mx, sink, ALU.max)
        k.ts(nmx, mx, -1.0, ALU.mult)
        for h in range(16):
            k.act(lg[:, h, :], lg[:, h, :], AF.Exp, bias=nmx[:, h:h + 1], accum=ssum[:, h:h + 1])
        k.tt(es, sink, mx, ALU.subtract)
        k.act(es, es, AF.Exp)
        k.tt(ssum, ssum, es, ALU.add)
        k.recip(ssum, ssum)
        o = bp.alloc(1024).rearrange("p (h d) -> p h d", h=16)
        pT = [bp.alloc(512).rearrange("p (a b t) -> p a b t", a=2, b=2) for _ in range(2)]
        for hp in range(8):
            bank = 4 + hp % 2
            for hh in range(2):
                for b in range(2):
                    k.tr(k.psb(bank)[:, (hh * 2 + b) * 128:(hh * 2 + b + 1) * 128], lg[:, hp * 2 + hh, b * 128:(b + 1) * 128])
            pt = pT[hp % 2]
            if hp % 2 == 0:
                k.cpa(pt.rearrange("p a b t -> p (a b t)"), k.psb(bank))
            else:
                k.cpv(pt.rearrange("p a b t -> p (a b t)"), k.psb(bank))
            for hh in range(2):
                h = hp * 2 + hh
                j = h // 4
                ob = k.psb(6 + (h // 8))[:, (h % 8) * 64:(h % 8 + 1) * 64]
                k.mm(ob, pt[:, hh, 0, :], V[prv][:, j * 64:(j + 1) * 64], start=True, stop=False)
                k.mm(ob, pt[:, hh, 1, :], V[cur][:, j * 64:(j + 1) * 64], start=False, stop=True)
        for hf in range(2):
            k.tt(o[:, hf * 8:(hf + 1) * 8, :], k.psb(6 + hf).rearrange("p (h d) -> p h d", h=8),
                 ssum[:, hf * 8:(hf + 1) * 8].unsqueeze(2).to_broadcast([128, 8, 64]), ALU.mult)
        oT = bp.alloc(1024).rearrange("p (kc t) -> p kc t", kc=8)
        of = o.rearrange("p h d -> p (h d)")
        for hf in range(2):
            for c in range(4):
                k.tr(k.psb(hf)[:, c * 128:(c + 1) * 128], of[:, (hf * 4 + c) * 128:(hf * 4 + c + 1) * 128])
            k.cpa(oT[:, hf * 4:(hf + 1) * 4, :].rearrange("p c t -> p (c t)"), k.psb(hf))
        for hf in range(2):
            for fc in range(8):
                k.mm(k.psb(2 + hf), oT[:, fc, :], w_out[:, fc, hf * 512:(hf + 1) * 512], start=(fc == 0), stop=(fc == 7))
        y = bp.alloc(1024)
        k.tt(y, k.ps[:, 2 * 512:4 * 512], bout, ALU.add)
        x3 = bp.alloc(1024)
        k.resid_ln(x3, x2, y, lng, lnb)
        k.ld(Xout[t], x3)
        bp.release()
    bp.release()
    phase_c_sample(k, Xin[NMAIN + 1], Xout[NMAIN + 1], w_in, w_out, bout, lng, lnb, rb)
    bp.release()


def phase_c_sample(k, Xin_tile, Xout_tile, w_in, w_out, bout, lng, lnb, rb):
    S, bp, nc = k.S, k.bp, k.nc
    din, dout = k.din, k.dout
    P = NS
    bp.mark()
    x2 = bp.alloc(1024)
    k.ld(x2, Xin_tile)
    O = bp.alloc(NS * 64).rearrange("p (s d) -> p s d", s=NS)
    bp.mark()
    xT = bp.alloc(1024).rearrange("p (kc t) -> p kc t", kc=8)
    for hf in range(2):
        for c in range(4):
            k.tr(k.psb(hf)[:, c * 128:(c + 1) * 128], x2[:, (hf * 4 + c) * 128:(hf * 4 + c + 1) * 128])
        k.cpa(xT[:, hf * 4:(hf + 1) * 4, :].rearrange("p c t -> p (c t)"), k.psb(hf))
    ball = bp.alloc(1536); k.ld(ball[0:P], din["b_in_c"].partition_broadcast(P))
    sinkc = bp.alloc(1); k.ld(sinkc[0:16], din["swa_sinks"].rearrange("(p o) -> p o", o=1))
    mdiag = bp.alloc(4); k.ld(mdiag[0:16], din["mdiag"])
    ohs = bp.alloc(128); k.ld(ohs[0:32], din["t5_onehot_s"])
    for b3 in range(3):
        for kc in range(8):
            k.mm(k.psb(2 + b3)[0:P, :], xT[:, kc, 0:P], w_in[:, kc, b3 * 512:(b3 + 1) * 512], start=(kc == 0), stop=(kc == 7))
    qkv = bp.alloc(1536)
    k.tt(qkv[0:P], k.ps[0:P, 2 * 512:5 * 512], ball[0:P], ALU.add)
    qn = bp.alloc(1024)
    for j2 in range(2):
        src = qkv[0:P, j2 * 512:(j2 + 1) * 512].rearrange("p (g jh d) -> p jh g d", g=4, jh=2)
        dst = qn[0:P, j2 * 512:(j2 + 1) * 512].rearrange("p (jh g d) -> p jh g d", jh=2, g=4)
        k.cpv(dst, src)
    KV = bp.alloc(NS * 512).rearrange("p (s c) -> p s c", s=NS)
    for s in range(NS):
        e = "sp" if s % 2 == 0 else "act"
        k.ld(KV[0:127, s, 0:256], din["cache_k"][s, 1:128, :], eng=e)
        k.ld(KV[0:127, s, 256:512], din["cache_v"][s, 1:128, :], eng=e)
        k.ld(KV[127:128, s, :], qkv[s:s + 1, 1024:1536], eng=e)
        k.ld(dout["s_swa_k"][s], KV[:, s, 0:256], eng=e)
        k.ld(dout["s_swa_v"][s], KV[:, s, 256:512], eng=e)
    Esel = bp.alloc(NS * 128).rearrange("p (s m) -> p s m", s=NS)
    k.tt(Esel[0:P], k.ident[0:P, 0:NS].unsqueeze(2).to_broadcast([P, NS, 128]),
         k.ones[0:P, :].unsqueeze(1).to_broadcast([P, NS, 128]), ALU.mult)
    LGT = bp.alloc(NS * 16).rearrange("p (s h) -> p s h", s=NS)
    prod = [bp.alloc(1024) for _ in range(2)]
    for s in range(NS):
        for hf in range(2):
            k.mm(k.psb(6 + hf), Esel[0:P, s, :], qn[0:P, hf * 512:(hf + 1) * 512])
        pr = prod[s % 2]
        k.tt(pr.rearrange("p (j g d) -> p j g d", j=4, g=4), k.ps[:, 6 * 512:8 * 512].rearrange("p (j g d) -> p j g d", j=4, g=4),
             KV[:, s, 0:256].rearrange("p (j d) -> p j d", j=4).unsqueeze(2).to_broadcast([128, 4, 4, 64]), ALU.mult)
        k.red(LGT[:, s, :], pr.rearrange("p (h d) -> p h d", h=16))
    LG = bp.alloc(NS * 128).rearrange("p (s r) -> p s r", s=NS)
    for g4 in range(4):
        for si in range(4):
            s = g4 * 4 + si
            k.tr(k.psb(2 + g4 % 2)[0:16, si * 128:(si + 1) * 128], LGT[:, s, :])
        k.cpv(LG[0:16, g4 * 4:(g4 + 1) * 4, :].rearrange("p s r -> p (s r)"), k.psb(2 + g4 % 2)[0:16, :])
    k.mm(k.psb(4)[0:16, 0:128], rb[0:32, :], ohs[0:32, :])
    Bs = bp.alloc(128)
    k.cpv(Bs[0:16], k.psb(4)[0:16, 0:128])
    k.stt(LG[0:16], LG[0:16], 0.125, Bs[0:16].unsqueeze(1).to_broadcast([16, NS, 128]), ALU.mult, ALU.add)
    mx = bp.alloc(NS); nmx = bp.alloc(NS); ssum = bp.alloc(NS); es = bp.alloc(NS)
    k.red(mx[0:16], LG[0:16], op=ALU.max)
    k.ts(mx[0:16], mx[0:16], sinkc[0:16], ALU.max)
    k.tt(LG[0:16], LG[0:16], mx[0:16].unsqueeze(2).to_broadcast([16, NS, 128]), ALU.subtract)
    k.act(LG[0:16].rearrange("p s r -> p (s r)"), LG[0:16].rearrange("p s r -> p (s r)"), AF.Exp)
    k.red(ssum[0:16], LG[0:16])
    k.ts(es[0:16], mx[0:16], -1.0, ALU.mult, sinkc[0:16], ALU.add)
    k.act(es[0:16], es[0:16], AF.Exp)
    k.tt(ssum[0:16], ssum[0:16], es[0:16], ALU.add)
    k.recip(ssum[0:16], ssum[0:16])
    k.tt(LG[0:16], LG[0:16], ssum[0:16].unsqueeze(2).to_broadcast([16, NS, 128]), ALU.mult)
    PT = bp.alloc(NS * 16).rearrange("p (s h) -> p s h", s=NS)
    for s in range(NS):
        k.tr(k.psb(5)[:, s * 16:(s + 1) * 16], LG[0:16, s, :])
    k.cpv(PT.rearrange("p s h -> p (s h)"), k.psb(5)[:, 0:NS * 16])
    pm = bp.alloc(8 * 256)
    for half in range(2):
        for si in range(8):
            s = half * 8 + si
            k.mm(k.ps[0:16, si * 256:(si + 1) * 256], PT[:, s, :], KV[:, s, 256:512])
        k.tt(pm[0:16].rearrange("p (s j d) -> p s j d", s=8, j=4), k.ps[0:16, 0:2048].rearrange("p (s j d) -> p s j d", s=8, j=4),
             mdiag[0:16].unsqueeze(1).unsqueeze(3).to_broadcast([16, 8, 4, 64]), ALU.mult)
        k.red(O[0:16, half * 8:(half + 1) * 8, :], pm[0:16].rearrange("p (s j d) -> p s d j", s=8, j=4))
    bp.release()
    Oscr = k.scratch("Oscr", [16, NS, 64])
    k.ld(Oscr, O[0:16])
    otok = bp.alloc(1024)
    k.memset(otok, 0.0)
    k.ld(otok[0:P].rearrange("p (h d) -> p h d", h=16), Oscr.rearrange("h s d -> s h d"))
    oT = bp.alloc(1024).rearrange("p (kc t) -> p kc t", kc=8)
    for hf in range(2):
        for c in range(4):
            k.tr(k.psb(hf)[:, c * 128:(c + 1) * 128], otok[:, (hf * 4 + c) * 128:(hf * 4 + c + 1) * 128])
        k.cpa(oT[:, hf * 4:(hf + 1) * 4, :].rearrange("p c t -> p (c t)"), k.psb(hf))
    for hf in range(2):
        for fc in range(8):
            k.mm(k.psb(2 + hf)[0:P, :], oT[:, fc, 0:P], w_out[:, fc, hf * 512:(hf + 1) * 512], start=(fc == 0), stop=(fc == 7))
    y = bp.alloc(1024)
    k.tt(y[0:P], k.ps[0:P, 2 * 512:4 * 512], bout[0:P], ALU.add)
    x3 = bp.alloc(1024)
    k.memset(x3, 0.0)
    k.resid_ln(x3[0:P], x2[0:P], y[0:P], lng, lnb, P=P)
    k.ld(Xout_tile, x3)
    bp.release()


def build(phases=("a", "b", "c", "d"), debug=()):
    k = K(debug=set(debug))
    import os
    k.npeer = int(os.environ.get("KNP", NMAIN + 2))
    nc = k.nc
    inp, out = k.inp, k.out
    inp("xT", [NT, 128, 8, 128]); inp("xtok", [NMAIN + 1, 128, 1024])
    inp("pflag", [128, 1])
    inp("ident", [128, 128]); inp("ones", [128, 128]); inp("triLE", [128, 128]); inp("maskSL", [128, 128])
    inp("w_in_ab", [1024, ABC]); inp("w_out_ab", [1024, 1024])
    inp("conv_w", [128, 16, 4]); inp("lru_cb", [128, 4]); inp("lru_b", [128, 8]); inp("lru_lam", [128, 4])
    inp("lru_wbd", [128, 8, 128]); inp("gdn_dt_bias", [4]); inp("gdn_a_log", [4]); inp("gdn_norm_w", [128])
    inp("ln_mix_g", [2, 1024]); inp("ln_mix_b", [2, 1024]); inp("ln_ffn_g", [2, 1024]); inp("ln_ffn_b", [2, 1024])
    inp("xsT", [128, 8, NS]); inp("xstok", [128, 1024]); inp("convhT", [128, 16, 3, NS]); inp("lru_h0T", [128, 4, NS])
    inp("s_gdn_in", [NS, 4, 128, 128]); inp("gconv_nat", [NS, 3, 1536]); inp("lconv_nat", [NS, 3, 512])
    out("s_gdn", [NS, 4, 128, 128]); out("s_gdn_conv", [NS, 3, 1536]); out("s_lru", [NS, 512]); out("s_lru_conv", [NS, 3, 512])
    out("p_gdn", [4, 128, 128]); out("p_gdn_conv", [3, 1536]); out("p_lru", [4, 128]); out("p_lru_conv", [3, 512])
    if "b" in phases or "d" in phases:
        inp("peer_w_q", [2, 1024, 2048]); inp("peer_keysT", [2, 128, 16, 128]); inp("iota16", [16]); inp("ssel", [128, 128])
        for l_ in range(2):
            inp("peer_u%d" % l_, [16384, 1024]); inp("peer_v%d" % l_, [16384, 1024])
    inp("w_in_c", [1024, 1536]); inp("w_out_c", [1024, 1024]); inp("b_in_qk", [128, 10]); inp("b_in_c", [1536])
    inp("b_out_c", [1024]); inp("swa_sinks", [16]); inp("hmask", [128, 128]); inp("rel_bias", [32, 16])
    inp("t5_onehot", [33, 128 * 256])
    out("p_swa_k", [128, 256]); out("p_swa_v", [128, 256])
    inp("cache_k", [NS, 128, 256]); inp("cache_v", [NS, 128, 256]); inp("mdiag", [16, 4]); inp("t5_onehot_s", [32, 128])
    out("s_swa_k", [NS, 128, 256]); out("s_swa_v", [NS, 128, 256])
    NX = NMAIN + 2
    mk = lambda n: (out(n, [NX, 128, 1024]) if n.lower() in k.debug else k.scratch(n, [NX, 128, 1024]))
    k.X1 = mk("X1"); k.X2 = mk("X2"); k.X3 = mk("X3"); k.X4 = mk("X4")
    with ExitStack() as st:
        big = st.enter_context(nc.sbuf_tensor("big", [128, SBCOLS], F32))
        k.ps = st.enter_context(nc.psum_tensor("ps", [128, 4096], F32))
        k.S = Sched(nc)
        k.S.alloc_sems(st)
        k.bp = Bump(big, SBCOLS)
        bp = k.bp
        k.ident = bp.alloc(128); k.ld(k.ident, k.din["ident"])
        k.ones = bp.alloc(128); k.ld(k.ones, k.din["ones"])
        k.triLE = bp.alloc(128); k.ld(k.triLE, k.din["triLE"])
        k.maskSL = bp.alloc(128); k.ld(k.maskSL, k.din["maskSL"])
        k.one_c = bp.alloc(1); k.memset(k.one_c, 1.0)
        k.eps6 = bp.alloc(1); k.memset(k.eps6, 1e-6)
        k.c128e6 = bp.alloc(1); k.memset(k.c128e6, 128e-6)
        k.eps_ln = bp.alloc(1); k.memset(k.eps_ln, LN_EPS)
        if "a" in phases:
            phase_a(k)
        if "b" in phases:
            phase_peer(k, 0, k.X1, lambda t: k.X2[t], k.npeer)
        if "c" in phases:
            phase_c(k, k.X1 if "ctest" in k.debug else k.X2, k.X3)
        if "d" in phases:
            y_p = out("y_p", [NMAIN, 128, 1024]); y_s = out("y_s", [128, 1024])
            xin = [k.X3[t + 1] for t in range(NMAIN + 1)]
            phase_peer(k, 1, xin, lambda t: (y_p[t] if t < NMAIN else y_s), min(k.npeer, NMAIN + 1))
        k.S.finish()
        print("instructions", k.S.n_ins, "waits", k.S.n_wait, "sbuf peak", bp.peak, k.S.count)
    return k


def host_prep(inputs):
    f = lambda a: np.ascontiguousarray(a, dtype=np.float32)
    xp = inputs["x_prompt"]
    common = {}
    p = np.arange(128)
    common["ident"] = np.eye(128, dtype=np.float32)
    common["ones"] = np.ones((128, 128), np.float32)
    common["triLE"] = (p[:, None] <= p[None, :]).astype(np.float32)
    common["maskSL"] = (p[None, :] < p[:, None]).astype(np.float32)
    common["w_in_ab"] = f(inputs["w_in_ab"][0]); common["w_out_ab"] = f(inputs["w_out_ab"][0])
    cwg = inputs["gdn_conv_w"][0].T.reshape(12, 128, 4)
    cwl = inputs["lru_conv_w"][0].T.reshape(4, 128, 4)
    common["conv_w"] = f(np.concatenate([cwg, cwl], 0).transpose(1, 0, 2))
    common["lru_cb"] = f(inputs["lru_conv_b"][0].reshape(4, 128).T)
    common["lru_b"] = f(np.concatenate([inputs["lru_b_r"][0].reshape(4, 128), inputs["lru_b_i"][0].reshape(4, 128)], 0).T)
    common["lru_lam"] = f(inputs["lru_lam"][0].reshape(4, 128).T)
    wbd = np.zeros((128, 8, 128), np.float32)
    for gi, w in enumerate([inputs["lru_w_r"][0], inputs["lru_w_i"][0]]):
        for n in range(8):
            c, nl = n // 2, n % 2
            wbd[nl * 64:(nl + 1) * 64, gi * 4 + c, nl * 64:(nl + 1) * 64] = w[n]
    common["lru_wbd"] = wbd
    common["gdn_dt_bias"] = f(inputs["gdn_dt_bias"][0]); common["gdn_a_log"] = f(inputs["gdn_a_log"][0])
    common["gdn_norm_w"] = f(inputs["gdn_norm_w"][0])
    for n in ("ln_mix_g", "ln_mix_b", "ln_ffn_g", "ln_ffn_b", "peer_w_q"):
        common[n] = f(inputs[n])
    for l_ in range(2):
        common["peer_u%d" % l_] = f(inputs["peer_u"][l_]); common["peer_v%d" % l_] = f(inputs["peer_v"][l_])
    common["peer_keysT"] = f(inputs["peer_keys"].transpose(0, 4, 1, 2, 3).reshape(2, 128, 16, 128))
    common["iota16"] = np.arange(16, dtype=np.float32)
    common["ssel"] = f((np.arange(128)[:, None] // 8) == np.arange(128)[None, :])
    wc = inputs["w_in_c"][0]; bc = inputs["b_in_c"][0]
    order = []
    for m in range(8):
        for half in range(2):
            h = (m // 4) * 8 + half * 4 + m % 4
            order.extend(range(h * 64, (h + 1) * 64))
    order = np.array(order)
    wcp = np.concatenate([wc[:, order], wc[:, 1024:]], 1)
    bcp = np.concatenate([bc[order], bc[1024:]])
    common["w_in_c"] = f(wcp); common["b_in_c"] = f(bcp)
    common["b_in_qk"] = f(bcp[:1280].reshape(10, 128).T)
    common["w_out_c"] = f(inputs["w_out_c"][0]); common["b_out_c"] = f(inputs["b_out_c"][0])
    common["swa_sinks"] = f(inputs["swa_sinks"][0]); common["rel_bias"] = f(inputs["rel_bias"])
    q = np.arange(128)[:, None]; kk = np.arange(256)[None, :]
    rel = q + 128 - kk
    valid = (rel >= 0) & (rel < 128)
    n = np.maximum(rel, 0); nf = np.maximum(n, 1).astype(np.float32)
    large = 16 + (np.log(nf / 16) / np.float32(np.log(128 / 16)) * 16).astype(np.int32)
    bucket = np.where(n < 16, n, np.minimum(large, 31))
    bucket = np.where(valid, bucket, 32)
    rels = 127 - np.arange(128)
    ns_ = np.maximum(rels, 0); nfs = np.maximum(ns_, 1).astype(np.float32)
    larges = 16 + (np.log(nfs / 16) / np.float32(np.log(128 / 16)) * 16).astype(np.int32)
    bks = np.where(ns_ < 16, ns_, np.minimum(larges, 31))
    common["t5_onehot_s"] = f(bks[None, :] == np.arange(32)[:, None])
    common["mdiag"] = f(np.arange(16)[:, None] // 4 == np.arange(4)[None, :])
    common["t5_onehot"] = f((bucket[None] == np.arange(33)[:, None, None]).reshape(33, -1))
    maps = []
    for c in range(8):
        b, half = c // 2, c % 2
        m = dict(common)
        xs = np.zeros((NT * 128, 1024), np.float32)
        if half == 1:
            xs[:] = xp[b]
        else:
            xs[NPRE * 128:] = xp[b, :NMAIN * 128]
        m["xT"] = f(xs.reshape(NT, 128, 8, 128).transpose(0, 3, 2, 1))
        m["xtok"] = f(xs[(NPRE - 1) * 128:].reshape(NMAIN + 1, 128, 1024))
        m["pflag"] = np.full((128, 1), float(half), np.float32)
        m["cache_k"] = f(inputs["cache_swa_k"][0, c * NS:(c + 1) * NS].reshape(NS, 128, 256))
        m["cache_v"] = f(inputs["cache_swa_v"][0, c * NS:(c + 1) * NS].reshape(NS, 128, 256))
        sl = slice(c * NS, (c + 1) * NS)
        xs_ = inputs["x_sample"][sl, 0, :]
        m["xsT"] = f(xs_.reshape(NS, 8, 128).transpose(2, 1, 0))
        xst = np.zeros((128, 1024), np.float32); xst[:NS] = xs_
        m["xstok"] = xst
        gcv = inputs["state_gdn_conv"][0, sl]; lcv = inputs["state_lru_conv"][0, sl]
        m["gconv_nat"] = f(gcv); m["lconv_nat"] = f(lcv)
        hT = np.concatenate([gcv.reshape(NS, 3, 12, 128), lcv.reshape(NS, 3, 4, 128)], 2)
        m["convhT"] = f(hT.transpose(3, 2, 1, 0))
        m["lru_h0T"] = f(inputs["state_lru"][0, sl].reshape(NS, 4, 128).transpose(2, 1, 0))
        m["s_gdn_in"] = f(inputs["state_gdn"][0, sl])
        m["hmask"] = np.full((128, 128), 0.0 if half == 1 else -30000.0, np.float32)
        maps.append(m)
    return maps


_CACHE = {}


def run(inputs, phases=("a", "b", "c", "d"), debug=()):
    key = (tuple(phases), tuple(debug))
    if key not in _CACHE:
        _CACHE[key] = build(phases, debug)
    k = _CACHE[key]
    maps = host_prep(inputs)
    maps = [{n: m[n] for n in k.din} for m in maps]
    res = run_bass_kernel_spmd(k.nc, maps, core_ids=list(range(8)))
    return res.results


def kernel(**inputs):
    inputs = {n: np.asarray(v) for n, v in inputs.items()}
    r = run(inputs)
    f32 = np.float32
    y_prompt = np.stack([np.concatenate([r[2 * b]["y_p"].reshape(2048, 1024), r[2 * b + 1]["y_p"].reshape(2048, 1024)], 0)
                         for b in range(4)], 0).astype(f32)
    y_sample = np.concatenate([r[c]["y_s"][:NS] for c in range(8)], 0).reshape(128, 1, 1024).astype(f32)
    odd = [r[2 * b + 1] for b in range(4)]
    p_gdn = np.stack([o["p_gdn"] for o in odd], 0)[None].astype(f32)
    p_gdn_conv = np.stack([o["p_gdn_conv"] for o in odd], 0)[None].astype(f32)
    p_lru = np.stack([o["p_lru"].reshape(512) for o in odd], 0)[None].astype(f32)
    p_lru_conv = np.stack([o["p_lru_conv"] for o in odd], 0)[None].astype(f32)
    p_k = np.stack([o["p_swa_k"].reshape(128, 4, 64) for o in odd], 0)[None].astype(f32)
    p_v = np.stack([o["p_swa_v"].reshape(128, 4, 64) for o in odd], 0)[None].astype(f32)
    cat = lambda n, shp: np.concatenate([r[c][n] for c in range(8)], 0).reshape(shp)[None].astype(f32)
    s_gdn = cat("s_gdn", (128, 4, 128, 128))
    s_gdn_conv = cat("s_gdn_conv", (128, 3, 1536))
    s_lru = cat("s_lru", (128, 512))
    s_lru_conv = cat("s_lru_conv", (128, 3, 512))
    s_k = cat("s_swa_k", (128, 128, 4, 64))
    s_v = cat("s_swa_v", (128, 128, 4, 64))
    return (y_prompt, y_sample, p_gdn, p_gdn_conv, p_lru, p_lru_conv, p_k, p_v,
            s_gdn, s_gdn_conv, s_lru, s_lru_conv, s_k, s_v)
```
